# Optimizing a Trainium2 kernel written in Bass

```python
import math
import jax, jax.numpy as jnp
from jax import lax
import numpy as np

D_MODEL = 1024
BATCH = 8
SEQ = 2048
DEPTH = 1
DEC_BATCH = 128
DEC_SEQ = 8
PAST_LEN = 16384
PAGE_SIZE = 128

MIX_WIDTH = D_MODEL
GLA_WIDTH = MIX_WIDTH // 2
CONV_CH = MIX_WIDTH - GLA_WIDTH
GLA_HEADS = 4
GLA_DV = GLA_WIDTH // GLA_HEADS
GLA_DK = GLA_DV // 2
GLA_KW = GLA_HEADS * GLA_DK
GATE_RANK = 16
GATE_NORM = 16.0
GLA_CHUNK = 16
CONV_K = 3
D_FF = 4 * D_MODEL
N_MOD = 6
EPS = 1e-6
IN_SIZES = (GLA_KW, GLA_KW, GLA_WIDTH, GATE_RANK, GLA_WIDTH, CONV_CH, CONV_CH, CONV_CH)
D_IN_PROJ = 2 * GLA_KW + 2 * GLA_WIDTH + GATE_RANK + 3 * CONV_CH

kernel_name = "hymba_gla_shortconv_adaln_decoder_step"


def rmsnorm(x, g):
    xf = x.astype(jnp.float32)
    y = xf * lax.rsqrt(jnp.mean(xf * xf, axis=-1, keepdims=True) + EPS)
    return (y * g.astype(jnp.float32)).astype(x.dtype)


def gla_recurrence(q, k, v, logd, s0):
    b, L, h, dk = q.shape
    dv = v.shape[-1]
    c = math.gcd(L, GLA_CHUNK)
    n = L // c

    def blocks(t):
        return t.astype(jnp.float32).reshape(b, n, c, h, t.shape[-1]).transpose(1, 0, 3, 2, 4)

    mask = jnp.tril(jnp.ones((c, c), dtype=bool))

    def step(S, inp):
        qc, kc, vc, gc = inp
        cum = jnp.cumsum(gc, axis=-2)
        last = cum[..., -1:, :]
        q_in = qc * jnp.exp(cum)
        k_in = kc * jnp.exp(-cum)
        k_out = kc * jnp.exp(last - cum)
        att = jnp.where(mask, jnp.einsum('bhtd,bhsd->bhts', q_in, k_in), 0.0)
        o = jnp.einsum('bhts,bhsv->bhtv', att, vc) + jnp.einsum('bhtd,bhdv->bhtv', q_in, S)
        S = jnp.exp(last[..., 0, :])[..., None] * S + jnp.einsum('bhsd,bhsv->bhdv', k_out, vc)
        return S, o

    S, o = lax.scan(step, s0.astype(jnp.float32), (blocks(q), blocks(k), blocks(v), blocks(logd)))
    o = o.transpose(1, 0, 3, 2, 4).reshape(b, L, h, dv)
    return o, S


def mixer(h, gla_state, conv_state, w_in, w_gate_up, b_gate, gla_norm_g, w_conv, w_out):
    b, L, _ = h.shape
    proj = h @ w_in
    idx = np.cumsum(IN_SIZES)[:-1].tolist()
    q, k, v, gz, r, bg, cg, hin = jnp.split(proj, idx, axis=-1)
    logd = jax.nn.log_sigmoid((gz @ w_gate_up + b_gate).astype(jnp.float32)) / GATE_NORM
    q = q.reshape(b, L, GLA_HEADS, GLA_DK) * (GLA_DK ** -0.5)
    k = k.reshape(b, L, GLA_HEADS, GLA_DK)
    v = v.reshape(b, L, GLA_HEADS, GLA_DV)
    logd = logd.reshape(b, L, GLA_HEADS, GLA_DK)
    o, gla_new = gla_recurrence(q, k, v, logd, gla_state)
    o = rmsnorm(o.astype(h.dtype), gla_norm_g.reshape(GLA_HEADS, GLA_DV))
    o = o.reshape(b, L, GLA_WIDTH) * jax.nn.silu(r)
    u = cg * hin
    u_full = jnp.concatenate([conv_state.astype(u.dtype), u], axis=1)
    z = sum(w_conv[j] * u_full[:, j:j + L] for j in range(CONV_K))
    yc = bg * z
    conv_new = u_full[:, -(CONV_K - 1):]
    out = jnp.concatenate([o, yc], axis=-1) @ w_out
    return out, gla_new, conv_new


def layer(x, c, gla_state, conv_state, w_ada, b_ada, norm1_g, w_in, w_gate_up, b_gate, gla_norm_g,
          w_conv, w_out, norm2_g, w_up, w_down):
    mod = (jax.nn.silu(c) @ w_ada + b_ada)[:, None, :]
    sh1, sc1, g1, sh2, sc2, g2 = jnp.split(mod, N_MOD, axis=-1)
    h = rmsnorm(x, norm1_g) * (1 + sc1) + sh1
    m, gla_new, conv_new = mixer(h, gla_state, conv_state, w_in, w_gate_up, b_gate, gla_norm_g, w_conv, w_out)
    x = x + g1 * m
    h = rmsnorm(x, norm2_g) * (1 + sc2) + sh2
    f = jnp.square(jax.nn.relu(h @ w_up)) @ w_down
    x = x + g2 * f
    return x, gla_new, conv_new


def setup_inputs(seed: int = 0) -> dict:
    key = jax.random.key(seed)
    ks = jax.random.split(key, 24)
    nrm = lambda k, s, sc: jax.random.normal(k, s, jnp.float32) * sc
    return {
        "x_prompt": nrm(ks[0], (BATCH, SEQ, D_MODEL), 1.0),
        "x_sample": nrm(ks[1], (DEC_BATCH, DEC_SEQ, D_MODEL), 1.0),
        "state_gla": nrm(ks[2], (DEPTH, DEC_BATCH, GLA_HEADS, GLA_DK, GLA_DV), 0.3),
        "state_conv": nrm(ks[3], (DEPTH, DEC_BATCH, CONV_K - 1, CONV_CH), 1.0),
        "c_prompt": nrm(ks[4], (BATCH, D_MODEL), 1.0),
        "c_sample": nrm(ks[5], (DEC_BATCH, D_MODEL), 1.0),
        "w_ada": nrm(ks[6], (DEPTH, D_MODEL, N_MOD * D_MODEL), 0.5 * D_MODEL ** -0.5),
        "b_ada": nrm(ks[7], (DEPTH, N_MOD * D_MODEL), 0.02),
        "norm1_g": 1.0 + nrm(ks[8], (DEPTH, D_MODEL), 0.02),
        "w_in": nrm(ks[9], (DEPTH, D_MODEL, D_IN_PROJ), D_MODEL ** -0.5),
        "w_gate_up": nrm(ks[10], (DEPTH, GATE_RANK, GLA_KW), GATE_RANK ** -0.5),
        "b_gate": nrm(ks[11], (DEPTH, GLA_KW), 0.02),
        "gla_norm_g": 1.0 + nrm(ks[12], (DEPTH, GLA_WIDTH), 0.02),
        "w_conv": nrm(ks[13], (DEPTH, CONV_K, CONV_CH), CONV_K ** -0.5),
        "w_out": nrm(ks[14], (DEPTH, MIX_WIDTH, D_MODEL), MIX_WIDTH ** -0.5),
        "norm2_g": 1.0 + nrm(ks[15], (DEPTH, D_MODEL), 0.02),
        "w_up": nrm(ks[16], (DEPTH, D_MODEL, D_FF), D_MODEL ** -0.5),
        "w_down": nrm(ks[17], (DEPTH, D_FF, D_MODEL), D_FF ** -0.5),
        "final_g": 1.0 + nrm(ks[18], (D_MODEL,), 0.02),
    }


def reference(x_prompt, x_sample, state_gla, state_conv, c_prompt, c_sample, w_ada, b_ada, norm1_g, w_in,
              w_gate_up, b_gate, gla_norm_g, w_conv, w_out, norm2_g, w_up, w_down, final_g):
    xp, xs = x_prompt, x_sample
    bp = x_prompt.shape[0]
    gla_p, conv_p, gla_s, conv_s = [], [], [], []
    for l in range(DEPTH):
        lw = (w_ada[l], b_ada[l], norm1_g[l], w_in[l], w_gate_up[l], b_gate[l], gla_norm_g[l],
              w_conv[l], w_out[l], norm2_g[l], w_up[l], w_down[l])
        zero_gla = jnp.zeros((bp, GLA_HEADS, GLA_DK, GLA_DV), jnp.float32)
        zero_conv = jnp.zeros((bp, CONV_K - 1, CONV_CH), x_prompt.dtype)
        xp, sg, sc = layer(xp, c_prompt, zero_gla, zero_conv, *lw)
        gla_p.append(sg)
        conv_p.append(sc)
        xs, sg, sc = layer(xs, c_sample, state_gla[l], state_conv[l], *lw)
        gla_s.append(sg)
        conv_s.append(sc)
    y_prompt = rmsnorm(xp, final_g)
    y_sample = rmsnorm(xs, final_g)
    new_gla_prompt = jnp.stack(gla_p)
    new_conv_prompt = jnp.stack(conv_p)
    new_gla_sample = jnp.stack(gla_s)
    new_conv_sample = jnp.stack(conv_s)
    return (y_prompt, y_sample, new_gla_prompt, new_conv_prompt, new_gla_sample, new_conv_sample)
```

```python
from contextlib import ExitStack
import numpy as np
import ml_dtypes
import concourse.bass as bass
import concourse.mybir as mybir
from concourse.bass_utils import run_bass_kernel_spmd

F32 = mybir.dt.float32
BF16 = mybir.dt.bfloat16
AF = mybir.ActivationFunctionType
ALU = mybir.AluOpType
EPS = 1e-6
NCORES = 8
D = 1024
DIN = 3088
DFF = 4096
NE = 8
ESZ = DFF // NE


class Res:
    __slots__ = ("name", "last_w", "readers")

    def __init__(self, name):
        self.name = name
        self.last_w = None
        self.readers = {}


class Sched:
    ENGS = ("pe", "act", "dve", "pool", "sp")

    def __init__(self, n_dma_sems=14):
        self.streams = {e: [] for e in self.ENGS}
        self.sems = {}
        self.count = {}
        self.known = {e: {} for e in self.ENGS}
        self.n_dma_sems = n_dma_sems
        self.dma_rr = {e: 0 for e in self.ENGS}
        self.sem_keys = ["pe", "act", "dve", "pool"]
        for q in ("sp", "pool"):
            for i in range(n_dma_sems):
                self.sem_keys.append(f"dma_{q}_{i}")
        for k in self.sem_keys:
            self.count[k] = 0
        self.dead = False
        self.limit = 10 ** 9

    def stage(self, n):
        if n >= self.limit:
            self.dead = True

    def _need(self, eng, waits, ev, war=False):
        if ev is None:
            return
        key, val = ev
        if key == eng and (eng == "pe" or (war and eng != "pool")):
            return
        if self.known[eng].get(key, 0) >= val:
            return
        if waits.get(key, 0) < val:
            waits[key] = val

    def _collect(self, eng, reads, writes):
        waits = {}
        for r in reads:
            self._need(eng, waits, r.last_w)
        for w in writes:
            self._need(eng, waits, w.last_w)
            for k, v in w.readers.items():
                self._need(eng, waits, (k, v), war=True)
        for k, v in waits.items():
            self.known[eng][k] = v
        return waits

    def op(self, eng, fn, reads=(), writes=(), signal=True):
        assert signal or eng == "pe"
        if self.dead:
            return None
        waits = self._collect(eng, reads, writes)
        if signal:
            self.count[eng] += 1
            ev = (eng, self.count[eng])
        else:
            ev = (eng, self.count[eng] + 1)
        for r in reads:
            if r.readers.get(ev[0], 0) < ev[1]:
                r.readers[ev[0]] = ev[1]
        for w in writes:
            w.last_w = ev
            w.readers = {}
        self.streams[eng].append((waits, fn, (eng, 1) if signal else None))
        return ev

    def dma(self, q, fn, reads=(), writes=()):
        if self.dead:
            return None
        i = self.dma_rr[q]
        self.dma_rr[q] = (i + 1) % self.n_dma_sems
        key = f"dma_{q}_{i}"
        waits = self._collect(q, reads, writes)
        prev = self.count[key]
        if prev > 0 and self.known[q].get(key, 0) < prev:
            waits[key] = prev
            self.known[q][key] = prev
        self.count[key] = prev + 16
        ev = (key, prev + 16)
        for r in reads:
            if r.readers.get(key, 0) < ev[1]:
                r.readers[key] = ev[1]
        for w in writes:
            w.last_w = ev
            w.readers = {}
        self.streams[q].append((waits, fn, (key, 16)))
        return ev

    def wait_all(self, eng, events):
        waits = {}
        for ev in events:
            self._need(eng, waits, ev)
        for k, v in waits.items():
            self.known[eng][k] = v
        self.streams[eng].append((waits, None, None))

    def emit(self, block):
        sems = self.sems

        def run(engname):
            def body(e):
                for waits, fn, inc in self.streams[engname]:
                    for k, v in waits.items():
                        e.wait_ge(sems[k], v)
                    if fn is None:
                        continue
                    ins = fn(e)
                    if inc is not None:
                        ins.then_inc(sems[inc[0]], inc[1])
            return body

        block.tensor(run("pe"))
        block.scalar(run("act"))
        block.vector(run("dve"))
        block.gpsimd(run("pool"))
        block.sync(run("sp"))


def V(apobj, dims):
    return bass.AP(apobj.tensor, apobj.offset, [list(apobj.ap[0])] + [list(d) for d in dims])


C_ID, C_MP, C_MS, C_SCAN, C_ONES, C_EP, C_ES, C_SMT = 0, 128, 256, 384, 512, 640, 768, 896
NCF = 912
CB_ID, CB_SM = 0, 128
NCB = 128 + 2048


def make_consts():
    cf = np.zeros((128, NCF), np.float32)
    s = np.arange(128)[:, None]
    t = np.arange(128)[None, :]
    cf[:, C_ID:C_ID + 128] = (s == t)
    cf[:, C_MP:C_MP + 128] = (s <= t)
    cf[:, C_MS:C_MS + 128] = (s <= t) & (s // 8 == t // 8)
    cf[:, C_SCAN:C_SCAN + 128] = (t % 8 != 0)
    cf[:, C_ONES:C_ONES + 128] = 1.0
    cf[0, C_EP:C_EP + 128] = 1.0
    for q in range(16):
        cf[1 + q, C_ES + 8 * q:C_ES + 8 * q + 8] = 1.0
    cf[:, C_SMT:C_SMT + 16] = (np.arange(128)[:, None] // 8 == np.arange(16)[None, :])
    cb = np.zeros((128, NCB), np.float32)
    cb[:, CB_ID:CB_ID + 128] = (s == t)
    sm = (np.arange(128)[None, :] // 8 == np.arange(16)[:, None]).astype(np.float32)
    cb[:, CB_SM:CB_SM + 2048] = sm.reshape(1, 2048)
    return cf, cb.astype(ml_dtypes.bfloat16)


def build_program(limit=10 ** 9):
    nc = bass.Bass("TRN2", target_bir_lowering=False)

    def din(name, shape, dt=F32):
        return nc.dram_tensor(name, list(shape), dt, kind="ExternalInput").ap()

    def dout(name, shape):
        return nc.dram_tensor(name, list(shape), F32, kind="ExternalOutput").ap()

    xp = din("xp", [2048, D])
    xs = din("xs", [128, D])
    sgla = din("sgla", [16, 4, 64, 128])
    sconv = din("sconv", [32, 512])
    cvec = din("cvec", [17, D])
    w_ada = din("w_ada", [D, 6 * D])
    b_ada = din("b_ada", [1, 6 * D])
    n1g = din("norm1_g", [1, D])
    w_in = din("w_in", [D, DIN])
    w_gu = din("w_gate_up", [16, 256])
    b_gate = din("b_gate", [1, 256])
    glag = din("gla_norm_g", [1, 512])
    w_conv = din("w_conv", [3, 512])
    w_out = din("w_out", [D, D])
    n2g = din("norm2_g", [1, D])
    w_up = din("w_up", [D, DFF])
    w_down = din("w_down", [DFF, D])
    fing = din("final_g", [1, D])
    cfd = din("consts_f", [128, NCF])
    cbd = din("consts_b", [128, NCB], BF16)

    yp = dout("yp", [2048, D])
    ys = dout("ys", [128, D])
    glap = dout("glap", [4, 64, 128])
    convp = dout("convp", [2, 512])
    glas = dout("glas", [16, 4, 64, 128])
    convs = dout("convs", [32, 512])

    S = Sched()
    S.limit = limit
    R = {}

    def res(name):
        R[name] = Res(name)
        return R[name]

    with ExitStack() as es:
        E = es.enter_context

        def sb(name, shape, dt=F32):
            res(name)
            return E(nc.sbuf_tensor(name, list(shape), dt))

        NSLOT = 9
        x1_all = E(nc.sbuf_tensor("x1_all", [128, NSLOT, D], F32))
        Rx1 = [res(f"x1_{i}") for i in range(NSLOT)]
        h2T_all = E(nc.sbuf_tensor("h2T_all", [128, 8, NSLOT * 128], BF16))
        Rh2 = [res(f"h2_{i}") for i in range(NSLOT)]
        NWB = 8 * DIN + 8 * D
        wbig = E(nc.sbuf_tensor("wbig", [128, NWB], BF16))
        Rwout = res("wout")
        WBLK = {"qk": 0, "v": 512, "gz": 1024, "r": 1040, "B": 1552, "C": 2064, "hin": 2576}
        Rwin_c = {nm: res(f"win_{nm}") for nm in WBLK}
        Rwin_r = list(Rwin_c.values())
        Rring_up = [res(f"ringu{i}") for i in range(3)]
        Rring_dn = [res(f"ringd{i}") for i in range(3)]
        Rring = Rring_up + Rring_dn
        w_in_v = wbig[:, 0:8 * DIN].rearrange("p (k n) -> p k n", k=8)
        w_out_v = wbig[:, 8 * DIN:NWB].rearrange("p (k n) -> p k n", k=8)
        ring_up = [wbig[:, s * 8192:s * 8192 + 4096].rearrange("p (k n) -> p k n", k=8) for s in range(3)]
        ring_dn = [wbig[:, s * 8192 + 4096:(s + 1) * 8192].rearrange("p (k n) -> p k n", k=4) for s in range(3)]

        cst_f = sb("cst_f", [128, NCF])
        cst_b = sb("cst_b", [128, NCB], BF16)
        wg_sb = sb("wg_sb", [16, 256])
        nbg = sb("nbg", [128, 2])
        wc = sb("wc", [128, 3, 4])
        ngT = sb("ngT", [128, 2, 8])
        modT = sb("modT", [128, 4, 8, 17])
        fg_row = sb("fg_row", [128, D])
        gg_row = sb("gg_row", [128, 512])
        Grow = E(nc.sbuf_tensor("Grow", [128, 4, D], F32))
        RG = [res(f"G{i}") for i in range(4)]
        S_p = sb("S_p", [128, 2, 128])
        S_bf = sb("S_bf", [128, 2, 128], BF16)
        u_p = sb("u_p", [128, 4, 130])
        u_s = sb("u_s", [128, 4, 16, 10])
        sc_tok = sb("sc_tok", [32, 512])
        cn_sb = sc_tok
        cn_T = sb("cn_T", [128, 4, 32])
        scT = sb("scT", [128, 8, 17], BF16)
        bch0 = sb("bch0", [17, 512])
        bch = [bch0, bch0]
        ss_bufs = [sb(f"ss{i}", [128, 4]) for i in range(4)]
        ss4 = sb("ss4", [128, 12])
        xs_bf = sb("xs_bf", [128, D], BF16)
        tmpm = E(nc.sbuf_tensor("tmpm", [128, D], F32))
        Rtm = [res("tmpm0"), res("tmpm1")]
        hTbuf = E(nc.sbuf_tensor("hTbuf", [128, 2, 8, 128], BF16))
        RhT = [res("hT0"), res("hT1")]
        hTs = [hTbuf[:, 0, :, :], hTbuf[:, 1, :, :]]
        arena = E(nc.sbuf_tensor("arena", [128, 4096], F32))
        Rarena = []
        arena_off = [0]

        def carve(name, shape, dt=F32):
            n = 1
            for d_ in shape[1:]:
                n *= d_
            ncol = n if dt == F32 else n // 2
            a0 = arena_off[0]
            arena_off[0] += ncol
            assert arena_off[0] <= 4096
            ap_ = arena[:, a0:a0 + ncol]
            if dt != F32:
                ap_ = ap_.bitcast(dt)
            if len(shape) == 3:
                ap_ = ap_.rearrange("p (a b) -> p a b", a=shape[1])
            Rarena.append(res(name))
            return ap_
        qk_sb = [carve(f"qk_sb{i}", [128, 4, 128]) for i in range(2)]
        gz_sb = [sb(f"gz_sb{i}", [16, 128]) for i in range(2)]
        v_bf = [carve(f"v_bf{i}", [128, 512], BF16) for i in range(2)]
        er = sb("er", [128, 512])
        sr = [sb(f"sr{i}", [128, 512]) for i in range(2)]
        hin_sb = carve("hin_sb", [128, 4, 128])
        zc = carve("zc", [128, 4, 128])
        mixbuf = E(nc.sbuf_tensor("mixbuf", [128, 2, 8, 128], BF16))
        res("mixT0"); res("mixT1")
        mixT = [mixbuf[:, 0, :, :], mixbuf[:, 1, :, :]]
        el = carve("el", [128, 2, 128])
        cum = carve("cum", [128, 2, 128])
        eq = carve("eq", [128, 2, 128])
        ek = carve("ek", [128, 2, 128])
        qinz = [sb(f"qinz{i}", [128, 2, 128], BF16) for i in range(2)]
        kin = sb("kin", [128, 2, 128], BF16)
        kin_tok = sb("kin_tok", [128, 256], BF16)
        attm = carve("attm", [128, 4, 128], BF16)
        og = carve("og", [128, 512], BF16)
        assert arena_off[0] == 4096
        assert len(Rarena) == 12
        Rarena_up = Rarena[0:5] + [res("ring_last_up")]
        Rarena_dn = Rarena[5:12] + [res("ring_last_dn")]
        ring_up_last = arena[:, 0:2048].bitcast(BF16).rearrange("p (k n) -> p k n", k=8)
        ring_dn_last = arena[:, 2048:4096].bitcast(BF16).rearrange("p (k n) -> p k n", k=4)
        aT = [hTbuf[:, :, :, :].rearrange("p a k t -> p (a k t)").rearrange("p (f t) -> p f t", f=4),
              mixbuf[:, :, :, :].rearrange("p a k t -> p (a k t)").rearrange("p (f t) -> p f t", f=4)]
        RaT = [RhT, [R["mixT0"], R["mixT1"]]]
        za = sb("za", [128, 4, 128])
        zb = sb("zb", [128, 4, 128])

        pbs = [E(nc.psum_tensor(f"pb{i}", [128, 512], F32)) for i in range(8)]
        Rpb = [res(f"pb{i}") for i in range(8)]
        pb0b = pbs[0][:, :].bitcast(BF16)
        pb5b = pbs[5][:, :].bitcast(BF16)
        pb6b = pbs[6][:, :].bitcast(BF16)

        for k in S.sem_keys:
            S.sems[k] = E(nc.semaphore(k))
        block = E(nc.Block())

        ident_f = cst_f[:, C_ID:C_ID + 128]
        ident_b = cst_b[:, CB_ID:CB_ID + 128]
        ones_f = cst_f[:, C_ONES:C_ONES + 128]

        def slot_bf(j):
            return x1_all[:, j, :].bitcast(BF16)
        wada_ring = [slot_bf(5).rearrange("p (k n) -> p k n", k=8), slot_bf(6).rearrange("p (k n) -> p k n", k=8)]
        Rwada = [Rx1[5], Rx1[6]]
        Ssbf = [slot_bf(5).rearrange("p (s c v) -> p s c v", s=8, c=2), slot_bf(6).rearrange("p (s c v) -> p s c v", s=8, c=2)]
        RSsbf = [Rx1[5], Rx1[6]]
        stages = [x1_all[:, 7, :].rearrange("p (s c v) -> p s c v", s=4, c=2),
                  x1_all[:, 3, :].rearrange("p (s c v) -> p s c v", s=4, c=2)]
        Rstages = [Rx1[7], Rx1[3]]
        qmz = [slot_bf(8)[:, 0:2048].rearrange("p (s t) -> p s t", s=16),
               slot_bf(4)[:, 0:2048].rearrange("p (s t) -> p s t", s=16)]
        Rqmz = [Rx1[8], Rx1[4]]
        km = tmpm[:, :].bitcast(BF16)[:, 0:1024].rearrange("p (s f) -> p s f", s=4)


        cast_rr = [0]

        def cast(out, in_, reads, writes):
            eng = ("dve", "act", "dve")[cast_rr[0] % 3]
            cast_rr[0] += 1
            if eng == "act":
                S.op("act", lambda e: e.activation(out=out, in_=in_, func=AF.Copy), reads=reads, writes=writes)
            else:
                S.op(eng, lambda e: e.tensor_copy(out=out, in_=in_), reads=reads, writes=writes)

        def stage2(a):
            return x1_all[:, a:a + 2, :].rearrange("p a b -> p (a b)")

        def stage4(a):
            return x1_all[:, a:a + 4, :].rearrange("p a b -> p (a b)")
        wada_stage = [(stage2(1), [Rx1[1], Rx1[2]]), (stage2(3), [Rx1[3], Rx1[4]]), (stage2(7), [Rx1[7], Rx1[8]])]
        w_stage = [(stage4(1), [Rx1[1], Rx1[2], Rx1[3], Rx1[4]]), (stage4(5), [Rx1[5], Rx1[6], Rx1[7], Rx1[8]])]

        def ld(out, in_, writes, q="sp", nonc=False):
            if nonc:
                S.dma(q, lambda e: e.dma_start(out=out, in_=in_, allow_slow_non_contiguous=True), writes=writes)
            else:
                S.dma(q, lambda e: e.dma_start(out=out, in_=in_), writes=writes)

        ld(cst_f[:, :], cfd, [R["cst_f"]])
        ld(cst_b[:, :], cbd, [R["cst_b"]])
        ld(tmpm[0:17, :], cvec, Rtm)
        ld(wg_sb[:, :], w_gu, [R["wg_sb"]])
        ld(nbg[:, :], b_gate.rearrange("o (c p) -> p (o c)", p=128), [R["nbg"]], nonc=True)
        ld(wc[:, :, :], w_conv.rearrange("j (c p) -> p j c", p=128), [R["wc"]], nonc=True)
        ld(ngT[:, 0, :], n1g.rearrange("o (c p) -> p (o c)", p=128), [R["ngT"]], nonc=True)
        ld(ngT[:, 1, :], n2g.rearrange("o (c p) -> p (o c)", p=128), [R["ngT"]], nonc=True)
        ld(fg_row[:, :], bass.AP(fing.tensor, fing.offset, [[0, 128], [1, D]]), [R["fg_row"]])
        ld(gg_row[:, :], bass.AP(glag.tensor, glag.offset, [[0, 128], [1, 512]]), [R["gg_row"]])
        ld(sc_tok[:, :], sconv, [R["sc_tok"]])

        S.stage(1)
        S.op("dve", lambda e: e.tensor_scalar(out=nbg[:, :], in0=nbg[:, :], scalar1=-1.0, scalar2=None, op0=ALU.mult),
             reads=[R["nbg"]], writes=[R["nbg"]])
        S.op("pool", lambda e: e.memset(S_p[:, :, :], 0.0), writes=[R["S_p"]])
        S.op("pool", lambda e: e.memset(S_bf[:, :, :], 0.0), writes=[R["S_bf"]])
        S.op("pool", lambda e: e.memset(u_p[:, :, :], 0.0), writes=[R["u_p"]])
        for i in range(2):
            S.op("pool", lambda e, i=i: e.memset(qinz[i][:, :, :], 0.0), writes=[R[f"qinz{i}"]])

        S.stage(2)
        cv = tmpm[0:17, :]
        ex = Grow[0:17, 3, :]
        Rtmpe = [RG[3]]
        cvb = xs_bf[0:17, :]
        S.op("act", lambda e: e.activation(out=ex, in_=cv, func=AF.Exp, scale=-1.0),
             reads=Rtm, writes=Rtmpe)
        S.op("act", lambda e: e.activation(out=ex, in_=ex, func=AF.Ln, bias=1.0), reads=Rtmpe, writes=Rtmpe)
        S.op("act", lambda e: e.activation(out=ex, in_=ex, func=AF.Exp, scale=-1.0), reads=Rtmpe, writes=Rtmpe)
        S.op("dve", lambda e: e.tensor_tensor(out=cvb, in0=cv, in1=ex, op=ALU.mult),
             reads=Rtm + Rtmpe, writes=[R["xs_bf"]])
        for kc in range(8):
            S.op("pe", lambda e, kc=kc: e.transpose(out=pb0b[:, kc * 32:kc * 32 + 17], in_=cvb[:, kc * 128:(kc + 1) * 128],
                                                    identity=ident_b[0:17, 0:17]),
                 reads=[R["xs_bf"], R["cst_b"]], writes=[Rpb[0]], signal=(kc == 7))
        S.op("act", lambda e: e.activation(out=scT[:, :, :], in_=pb0b[:, 0:256].rearrange("p (k s) -> p k s", k=8)[:, :, 0:17],
                                           func=AF.Copy),
             reads=[Rpb[0]], writes=[R["scT"]])

        vec_slot = {0: 1, 1: 0, 3: 3, 4: 2}
        wada_v = w_ada.rearrange("(k p) n -> p k n", p=128)
        modc = er[0:17, 0:512]
        S.stage(3)
        wada_rb = [wbig[:, r * 4096:(r + 1) * 4096].rearrange("p (k n) -> p k n", k=8) for r in range(4)]
        Rwada_rb = [res(f"wadar{r}") for r in range(4)]
        RmodT = [res(f"modT{i}") for i in range(4)]

        def mod_dma(j, ring3, ringR, direct):
            if direct:
                S.dma("pool", lambda e, j=j: e.dma_start(out=ring3, in_=wada_v[:, :, j * 512:(j + 1) * 512]), writes=ringR)
            else:
                stg_ap, stg_R = w_stage[j % 2]
                stg3 = stg_ap.rearrange("p (k n) -> p k n", k=8)
                S.dma("sp", lambda e, j=j, stg3=stg3: e.dma_start(out=stg3, in_=wada_v[:, :, j * 512:(j + 1) * 512]), writes=stg_R)
                cast(ring3, stg3, stg_R, ringR)

        modc2_t = sb("modc2", [17, 512])

        def mod_compute(j, ring3, ringR, part=None):
            vec, sub = j // 2, j % 2
            if part is None:
                mc, Rmc = modc, R["er"]
            else:
                mc, Rmc = modc2_t[:, :], R["modc2"]
            if part in (None, "a"):
                mod_compute_a(j, ring3, ringR, mc, Rmc)
            if part in (None, "b"):
                mod_compute_b(vec, sub, mc, Rmc)

        def mod_compute_a(j, ring3, ringR, mc, Rmc):
            S.dma("sp", lambda e, j=j: e.dma_start(
                out=bch0[:, :], in_=bass.AP(b_ada.tensor, b_ada.offset + j * 512, [[0, 17], [1, 512]])),
                writes=[R["bch0"]])
            for kc in range(8):
                S.op("pe", lambda e, kc=kc: e.matmul(pbs[7][0:17, 0:512], lhsT=scT[:, kc, :], rhs=ring3[:, kc, :],
                                                    start=(kc == 0), stop=(kc == 7)),
                     reads=[R["scT"]] + ringR, writes=[Rpb[7]], signal=(kc == 7))
            S.op("dve", lambda e: e.tensor_tensor(out=mc, in0=pbs[7][0:17, 0:512], in1=bch0[:, :], op=ALU.add),
                 reads=[Rpb[7], R["bch0"]], writes=[Rmc])

        def mod_compute_b(vec, sub, mc, Rmc):
            if vec in vec_slot:
                slot = vec_slot[vec]
                for h in range(4):
                    S.op("pe", lambda e, h=h: e.transpose(out=pbs[6][:, h * 32:h * 32 + 17], in_=mc[:, h * 128:(h + 1) * 128],
                                                          identity=ident_f[0:17, 0:17]),
                         reads=[Rmc, R["cst_f"]], writes=[Rpb[6]], signal=(h == 3))
                S.op("act", lambda e, slot=slot, sub=sub: e.activation(
                    out=modT[:, slot, 4 * sub:4 * sub + 4, :],
                    in_=pbs[6][:, 0:128].rearrange("p (k s) -> p k s", k=4)[:, :, 0:17], func=AF.Copy),
                    reads=[Rpb[6]], writes=[RmodT[slot]])
            else:
                gi0 = 0 if vec == 2 else 2
                for g in range(2):
                    esel = cst_f[0:17, C_EP:C_EP + 128] if g == 0 else cst_f[0:17, C_ES:C_ES + 128]
                    S.op("pe", lambda e, g=g, esel=esel: e.matmul(pbs[5 - g][:, :], lhsT=esel, rhs=mc, start=True, stop=True),
                         reads=[Rmc, R["cst_f"]], writes=[Rpb[5 - g]], signal=True)
                for g in range(2):
                    S.op("act", lambda e, g=g, gi0=gi0, sub=sub: e.activation(
                        out=Grow[:, gi0 + g, sub * 512:(sub + 1) * 512], in_=pbs[5 - g][:, :], func=AF.Copy),
                        reads=[Rpb[5 - g]], writes=[RG[gi0 + g]])

        def mod_post(which, slot):
            S.op("dve", lambda e: e.tensor_scalar(out=modT[:, slot, :, :], in0=modT[:, slot, :, :], scalar1=1.0,
                                                  scalar2=None, op0=ALU.add),
                 reads=[RmodT[slot]], writes=[RmodT[slot]])
            S.op("dve", lambda e: e.tensor_tensor(
                out=modT[:, slot, :, :], in0=modT[:, slot, :, :], in1=V(ngT[:, which, 0:1], [[1, 8], [0, 17]]), op=ALU.mult),
                reads=[RmodT[slot], R["ngT"]], writes=[RmodT[slot]])

        for j in range(4):
            mod_dma(j, wada_rb[j % 4][:, :, :], [Rwada_rb[j % 4]], False)
            mod_compute(j, wada_rb[j % 4][:, :, :], [Rwada_rb[j % 4]])
        mod_post(0, 0)
        late = [4, 5, 6, 7, 8, 9, 10, 11]
        late_rb = [(h2T_all[:, :, 128:640], [Rh2[i] for i in range(1, 5)]), (h2T_all[:, :, 640:1152], [Rh2[i] for i in range(5, 9)])]

        def late_start():
            for i in range(2):
                mod_dma(late[i], late_rb[i][0], late_rb[i][1], True)

        def late_step(i2):
            i, half = i2 // 2, i2 % 2
            ring3, ringR = late_rb[i % 2]
            if half == 0:
                mod_compute(late[i], ring3, ringR, part="a")
                if i + 2 < len(late):
                    mod_dma(late[i + 2], ring3, ringR, True)
            else:
                mod_compute(late[i], ring3, ringR, part="b")
                if late[i] == 9:
                    mod_post(1, 2)
        late_hooks = {}
        for i2 in range(16):
            late_hooks[(i2 // 4, 1 + i2 % 4)] = [i2]

        def mod_bc(slot, kind):
            if kind == "p":
                return V(modT[:, slot, 0, 0:1], [[17, 8], [0, 128]])
            return V(modT[:, slot, 0, 1:2], [[17, 8], [1, 16], [0, 8]])

        def tokview(ap2d, kind):
            if len(ap2d.shape) == 3:
                return ap2d if kind == "p" else ap2d.rearrange("p k (s t) -> p k s t", s=16)
            if kind == "p":
                return ap2d.rearrange("p (k t) -> p k t", k=8)
            return ap2d.rearrange("p (k s t) -> p k s t", k=8, s=16)

        def rstd_from(src_ap, Rsrc, n_inv, si=0):
            ss, Rss = ss_bufs[si], R[f"ss{si}"]
            S.op("act", lambda e: e.activation(out=xs_bf[:, :], in_=src_ap, func=AF.Square, accum_out=ss[:, 0:1]),
                 reads=[Rsrc], writes=[R["xs_bf"], Rss])
            S.op("act", lambda e: e.activation(out=ss[:, 1:2], in_=ss[:, 0:1], func=AF.Ln, scale=n_inv, bias=eps_ap),
                 reads=[Rss, R["epsb"]], writes=[Rss])
            S.op("act", lambda e: e.activation(out=ss[:, 2:3], in_=ss[:, 1:2], func=AF.Exp, scale=-0.5),
                 reads=[Rss], writes=[Rss])
            return ss[:, 2:3], Rss

        def norm_a(src_ap, Rsrc, si=0):
            rs_ap, Rss = rstd_from(src_ap, Rsrc, 1.0 / D, si)
            S.op("dve", lambda e: e.tensor_scalar(out=xs_bf[:, :], in0=src_ap, scalar1=rs_ap, scalar2=None, op0=ALU.mult),
                 reads=[Rsrc, Rss], writes=[R["xs_bf"]])

        def norm_b(gslot, kind, dst_ap, Rdst):
            for kc in range(8):
                S.op("pe", lambda e, kc=kc: e.transpose(out=pb0b[:, kc * 128:(kc + 1) * 128], in_=xs_bf[:, kc * 128:(kc + 1) * 128],
                                                        identity=ident_b),
                     reads=[R["xs_bf"], R["cst_b"]], writes=[Rpb[0]], signal=(kc == 7))
            S.op("dve", lambda e: e.tensor_tensor(out=tokview(tmpm[:, :], kind), in0=tokview(pb0b[:, :], kind),
                                                  in1=mod_bc(gslot, kind), op=ALU.mult),
                 reads=[Rpb[0], RmodT[gslot]], writes=Rtm)
            S.op("pool", lambda e: e.tensor_tensor(out=tokview(dst_ap, kind), in0=tokview(tmpm[:, :], kind),
                                                   in1=mod_bc(gslot + 1, kind), op=ALU.add),
                 reads=Rtm + [RmodT[gslot + 1]], writes=[Rdst])

        eps_t = sb("epsb", [128, 1])
        eps_ap = eps_t[:, 0:1]
        S.op("pool", lambda e: e.memset(eps_t[:, :], EPS), writes=[R["epsb"]])

        CQ, CK, CV_, CGZ, CR, CB, CC, CH = 0, 256, 512, 1024, 1040, 1552, 2064, 2576

        def mm_group(bank, col, lhsT_fn, rhs_fn, reads, nk=8, last_signal=True):
            for kc in range(nk):
                l_ap, r_ap = lhsT_fn(kc), rhs_fn(kc)
                S.op("pe", lambda e, kc=kc, l_ap=l_ap, r_ap=r_ap, col=col, nk=nk: e.matmul(
                    col, lhsT=l_ap, rhs=r_ap, start=(kc == 0), stop=(kc == nk - 1)),
                     reads=(reads(kc) if callable(reads) else reads), writes=[Rpb[bank]],
                     signal=(last_signal and kc == nk - 1))

        def phase1_steps(slot, kind, ptile, last_prompt, par):
            gi = 0 if kind == "p" else 1
            x1 = x1_all[:, slot, :]
            Rx = Rx1[slot]
            qk_c, gz_c, v_c, sr_c, mix_c = qk_sb[par], gz_sb[par], v_bf[par], sr[par], mixT[par]
            Rqk, Rgz, Rv, Rsr, Rmix = R[f"qk_sb{par}"], R[f"gz_sb{par}"], R[f"v_bf{par}"], R[f"sr{par}"], R[f"mixT{par}"]
            hT = hTs[par]
            def rdb(blk):
                return [RhT[par], Rwin_c[blk]]
            mcol = C_MP if kind == "p" else C_MS
            scan0 = ones_f if kind == "p" else cst_f[:, C_SCAN:C_SCAN + 128]

            def N1a():
                norm_a(x1, Rx)

            def N1b():
                norm_b(0, kind, hT, RhT[par])

            def A1():
                mm_group(7, pbs[7][0:16, 0:128], lambda kc: w_in_v[:, kc, CGZ:CGZ + 16], lambda kc: hT[:, kc, :], rdb("gz"))
                for j in range(4):
                    c0 = (CQ if j < 2 else CK) + (j % 2) * 128
                    mm_group(1, pbs[1][:, j * 128:(j + 1) * 128], lambda kc, c0=c0: w_in_v[:, kc, c0:c0 + 128],
                             lambda kc: hT[:, kc, :], rdb("qk"), last_signal=(j == 3))
                S.op("act", lambda e: e.activation(out=gz_c[:, :], in_=pbs[7][0:16, 0:128], func=AF.Copy),
                     reads=[Rpb[7]], writes=[Rgz])
                S.op("act", lambda e: e.activation(out=qk_c[:, :, :].rearrange("p a b -> p (a b)"), in_=pbs[1][:, :], func=AF.Copy),
                     reads=[Rpb[1]], writes=[Rqk])

            def A2():
                mm_group(2, pbs[2][:, :], lambda kc: hT[:, kc, :], lambda kc: w_in_v[:, kc, CV_:CV_ + 512], rdb("v"))
                mm_group(3, pbs[3][:, :], lambda kc: hT[:, kc, :], lambda kc: w_in_v[:, kc, CR:CR + 512], rdb("r"))
                S.op("act", lambda e: e.activation(out=v_c[:, :], in_=pbs[2][:, :], func=AF.Copy), reads=[Rpb[2]], writes=[Rv])
                S.op("act", lambda e: e.activation(out=er[:, :], in_=pbs[3][:, :], func=AF.Exp, scale=-1.0),
                     reads=[Rpb[3]], writes=[R["er"]])
                S.op("act", lambda e: e.activation(out=er[:, :], in_=er[:, :], func=AF.Ln, bias=1.0), reads=[R["er"]], writes=[R["er"]])
                S.op("act", lambda e: e.activation(out=er[:, :], in_=er[:, :], func=AF.Exp, scale=-1.0), reads=[R["er"]], writes=[R["er"]])
                S.op("dve", lambda e: e.tensor_tensor(out=sr_c[:, :], in0=pbs[3][:, :], in1=er[:, :], op=ALU.mult),
                     reads=[Rpb[3], R["er"]], writes=[Rsr])
                S.op("pool", lambda e: e.tensor_tensor(out=sr_c[:, :], in0=sr_c[:, :], in1=gg_row[:, :], op=ALU.mult),
                     reads=[Rsr, R["gg_row"]], writes=[Rsr])

            if kind == "p":
                Ru = R["u_p"]

                def uview(cc, j):
                    return u_p[:, cc, j:j + 128]

                def zview(cc):
                    return zc[:, cc, :]
            else:
                Ru = R["u_s"]

                def uview(cc, j):
                    return u_s[:, cc, :, j:j + 8]

                def zview(cc):
                    return zc[:, cc, :].rearrange("p (s t) -> p s t", s=16)

            def A3p():
                for bank, cbase, blk in ((6, CH, "hin"), (5, CC, "C")):
                    for j in range(4):
                        mm_group(bank, pbs[bank][:, j * 128:(j + 1) * 128],
                                 lambda kc, c0=cbase + j * 128: w_in_v[:, kc, c0:c0 + 128], lambda kc: hT[:, kc, :], rdb(blk),
                                 last_signal=(j == 3))

            def A3e():
                S.op("act", lambda e: e.activation(out=hin_sb[:, :, :].rearrange("p a b -> p (a b)"), in_=pbs[6][:, :], func=AF.Copy),
                     reads=[Rpb[6]], writes=[R["hin_sb"]])
                if kind == "p":
                    S.op("dve", lambda e: e.tensor_tensor(out=u_p[:, :, 2:130], in0=pbs[5][:, :].rearrange("p (c t) -> p c t", c=4),
                                                          in1=hin_sb[:, :, :], op=ALU.mult),
                         reads=[Rpb[5], R["hin_sb"]], writes=[Ru])
                else:
                    for cc in range(4):
                        S.op("dve", lambda e, cc=cc: e.tensor_tensor(
                            out=u_s[:, cc, :, 2:10], in0=pbs[5][:, cc * 128:(cc + 1) * 128].rearrange("p (s t) -> p s t", s=16),
                            in1=hin_sb[:, cc, :].rearrange("p (s t) -> p s t", s=16), op=ALU.mult),
                            reads=[Rpb[5], R["hin_sb"]], writes=[Ru])
                if kind == "p":
                    def uall(j):
                        return u_p[:, :, j:j + 128]

                    def zall(t):
                        return t[:, :, :]

                    def wbc(j):
                        return V(wc[:, j, 0:1], [[1, 4], [0, 128]])
                    S.op("pool", lambda e: e.tensor_tensor(out=zall(za), in0=uall(0), in1=wbc(0), op=ALU.mult),
                         reads=[Ru, R["wc"]], writes=[R["za"]])
                    S.op("pool", lambda e: e.tensor_tensor(out=zall(zb), in0=uall(1), in1=wbc(1), op=ALU.mult),
                         reads=[Ru, R["wc"]], writes=[R["zb"]])
                    S.op("dve", lambda e: e.tensor_tensor(out=zall(zc), in0=uall(2), in1=wbc(2), op=ALU.mult),
                         reads=[Ru, R["wc"]], writes=[R["zc"]])
                else:
                    for cc in range(4):
                        for j, (zt, eng) in enumerate(((za, "pool"), (zb, "pool"), (zc, "dve"))):
                            S.op(eng, lambda e, cc=cc, j=j, zt=zt: e.tensor_tensor(
                                out=zt[:, cc, :].rearrange("p (s t) -> p s t", s=16), in0=uview(cc, j),
                                in1=V(wc[:, j, cc:cc + 1], [[0, 16], [0, 8]]), op=ALU.mult),
                                reads=[Ru, R["wc"]], writes=[R[("za", "zb", "zc")[j]]])
                S.op("dve", lambda e: e.tensor_tensor(out=zc[:, :, :], in0=zc[:, :, :], in1=za[:, :, :], op=ALU.add),
                     reads=[R["zc"], R["za"]], writes=[R["zc"]])
                S.op("dve", lambda e: e.tensor_tensor(out=zc[:, :, :], in0=zc[:, :, :], in1=zb[:, :, :], op=ALU.add),
                     reads=[R["zc"], R["zb"]], writes=[R["zc"]])

            def A4():
                for j in range(4):
                    mm_group(4, pbs[4][:, j * 128:(j + 1) * 128],
                             lambda kc, c0=CB + j * 128: w_in_v[:, kc, c0:c0 + 128], lambda kc: hT[:, kc, :], rdb("B"),
                             last_signal=(j == 3))
                S.op("dve", lambda e: e.tensor_tensor(out=mix_c[:, 4:8, :], in0=pbs[4][:, :].rearrange("p (c t) -> p c t", c=4),
                                                      in1=zc[:, :, :], op=ALU.mult),
                     reads=[Rpb[4], R["zc"]], writes=[Rmix])
                if kind == "p":
                    if last_prompt:
                        for cc in range(4):
                            S.op("pe", lambda e, cc=cc: e.transpose(out=pbs[6][0:2, cc * 128:(cc + 1) * 128], in_=u_p[:, cc, 128:130],
                                                                    identity=ident_f),
                                 reads=[Ru, R["cst_f"]], writes=[Rpb[6]], signal=(cc == 3))
                        S.op("act", lambda e: e.activation(out=cn_sb[0:2, :], in_=pbs[6][0:2, :], func=AF.Copy),
                             reads=[Rpb[6]], writes=[R["sc_tok"]])
                        S.dma("sp", lambda e: e.dma_start(out=convp, in_=cn_sb[0:2, :]), reads=[R["sc_tok"]])
                    else:
                        S.op("pool", lambda e: e.tensor_copy(out=u_p[:, :, 0:2], in_=u_p[:, :, 128:130]), reads=[Ru], writes=[Ru])
                else:
                    S.op("pool", lambda e: e.tensor_copy(out=cn_T[:, :, :].rearrange("p c (s j) -> p c s j", s=16),
                                                         in_=u_s[:, :, :, 8:10]), reads=[Ru], writes=[R["cn_T"]])
                    for cc in range(4):
                        S.op("pe", lambda e, cc=cc: e.transpose(out=pbs[6][0:32, cc * 128:(cc + 1) * 128], in_=cn_T[:, cc, :],
                                                                identity=ident_f),
                             reads=[R["cn_T"], R["cst_f"]], writes=[Rpb[6]], signal=(cc == 3))
                    S.op("act", lambda e: e.activation(out=cn_sb[:, :], in_=pbs[6][0:32, :], func=AF.Copy),
                         reads=[Rpb[6]], writes=[R["sc_tok"]])
                    S.dma("sp", lambda e: e.dma_start(out=convs, in_=cn_sb[:, :]), reads=[R["sc_tok"]])

            def B0():
                for c in range(2):
                    S.op("pe", lambda e, c=c: e.matmul(pbs[7][:, 128 + c * 128:256 + c * 128], lhsT=wg_sb[0:16, c * 128:(c + 1) * 128],
                                                       rhs=gz_c[0:16, :], start=True, stop=True),
                         reads=[R["wg_sb"], Rgz], writes=[Rpb[7]], signal=(c == 1))
                for c in range(2):
                    S.op("act", lambda e, c=c: e.activation(out=el[:, c, :], in_=pbs[7][:, 128 + c * 128:256 + c * 128], func=AF.Exp,
                                                            scale=-1.0, bias=nbg[:, c:c + 1]),
                         reads=[Rpb[7], R["nbg"]], writes=[R["el"]])
                el2 = el[:, :, :].rearrange("p a b -> p (a b)")
                S.op("act", lambda e: e.activation(out=el2, in_=el2, func=AF.Ln, bias=1.0), reads=[R["el"]], writes=[R["el"]])
                for c in range(2):
                    S.op("dve", lambda e, c=c: e.tensor_tensor_scan(out=cum[:, c, :], data0=scan0, data1=el[:, c, :], initial=0.0,
                                                                    op0=ALU.mult, op1=ALU.add),
                         reads=[R["el"], R["cst_f"]], writes=[R["cum"]])
                cum2 = cum[:, :, :].rearrange("p a b -> p (a b)")
                S.op("act", lambda e: e.activation(out=eq[:, :, :].rearrange("p a b -> p (a b)"), in_=cum2, func=AF.Exp, scale=-1.0 / 16),
                     reads=[R["cum"]], writes=[R["eq"]])
                S.op("act", lambda e: e.activation(out=ek[:, :, :].rearrange("p a b -> p (a b)"), in_=cum2, func=AF.Exp, scale=1.0 / 16),
                     reads=[R["cum"]], writes=[R["ek"]])
                for hh in range(2):
                    ps_ = slice(hh * 64, (hh + 1) * 64)
                    S.op("dve", lambda e, hh=hh, ps_=ps_: e.scalar_tensor_tensor(
                        out=qinz[hh][ps_, :, :].rearrange("p a b -> p (a b)"),
                        in0=qk_c[ps_, 0:2, :].rearrange("p a b -> p (a b)"), scalar=0.125,
                        in1=eq[ps_, :, :].rearrange("p a b -> p (a b)"), op0=ALU.mult, op1=ALU.mult),
                        reads=[Rqk, R["eq"]], writes=[R[f"qinz{hh}"]])
                S.op("dve", lambda e: e.tensor_tensor(out=kin[:, :, :].rearrange("p a b -> p (a b)"),
                                                      in0=qk_c[:, 2:4, :].rearrange("p a b -> p (a b)"),
                                                      in1=ek[:, :, :].rearrange("p a b -> p (a b)"), op=ALU.mult),
                     reads=[Rqk, R["ek"]], writes=[R["kin"]])

            def B1():
                for c in range(2):
                    S.op("pe", lambda e, c=c: e.transpose(out=pb5b[:, c * 128:(c + 1) * 128], in_=kin[:, c, :], identity=ident_b),
                         reads=[R["kin"], R["cst_b"]], writes=[Rpb[5]], signal=(c == 1))
                S.op("act", lambda e: e.activation(out=kin_tok[:, :], in_=pb5b[:, 0:256], func=AF.Copy),
                     reads=[Rpb[5]], writes=[R["kin_tok"]])
                for h in range(4):
                    c, hh = h // 2, h % 2
                    S.op("pe", lambda e, h=h, c=c, hh=hh: e.matmul(pbs[1][:, h * 128:(h + 1) * 128],
                                                                   lhsT=kin[:, c, :], rhs=qinz[hh][:, c, :], start=True, stop=True),
                         reads=[R["kin"], R[f"qinz{hh}"]], writes=[Rpb[1]], signal=(h == 3))
                S.op("dve", lambda e: e.tensor_tensor(out=attm[:, :, :], in0=pbs[1][:, :].rearrange("p (h t) -> p h t", h=4),
                                                      in1=V(cst_f[:, mcol:mcol + 1], [[0, 4], [1, 128]]), op=ALU.mult),
                     reads=[Rpb[1], R["cst_f"]], writes=[R["attm"]])
                if kind == "s":
                    for g in range(4):
                        stg, Rstg = stages[g % 2], Rstages[g % 2]
                        S.dma("sp", lambda e, g=g, stg=stg: e.dma_start(
                            out=stg, in_=sgla[g * 4:(g + 1) * 4].rearrange("s (c hh) d v -> (hh d) s c v", hh=2)),
                            writes=[Rstg])
                        S.op("act", lambda e, g=g, stg=stg: e.activation(out=Ssbf[g // 2][:, (g % 2) * 4:(g % 2) * 4 + 4, :, :], in_=stg,
                                                                         func=AF.Copy),
                             reads=[Rstg], writes=[RSsbf[g // 2]])
                    for i in range(2):
                        S.op("pool", lambda e, i=i: e.memset(qmz[i], 0.0), writes=[Rqmz[i]])

            def B2():
                for c in range(2):
                    for hh in range(2):
                        h = 2 * c + hh
                        ps_ = slice(hh * 64, (hh + 1) * 64)
                        if kind == "s":
                            S.op("dve", lambda e, c=c, hh=hh, ps_=ps_: e.tensor_tensor(
                                out=qmz[hh][ps_, :, :], in0=V(qinz[hh][ps_, c, 0:1], [[0, 16], [1, 128]]),
                                in1=cst_b[ps_, CB_SM:CB_SM + 2048].rearrange("p (s t) -> p s t", s=16), op=ALU.mult),
                                reads=[R[f"qinz{hh}"], R["cst_b"]], writes=[Rqmz[hh]])
                        ocol = pbs[2][:, h * 128:(h + 1) * 128]
                        S.op("pe", lambda e, h=h, ocol=ocol: e.matmul(ocol, lhsT=attm[:, h, :], rhs=v_c[:, h * 128:(h + 1) * 128],
                                                                      start=True, stop=False),
                             reads=[R["attm"], Rv], writes=[Rpb[2]], signal=False)
                        if kind == "p":
                            S.op("pe", lambda e, c=c, hh=hh, ocol=ocol: e.matmul(ocol, lhsT=qinz[hh][:, c, :], rhs=S_bf[:, c, :],
                                                                                start=False, stop=True),
                                 reads=[R[f"qinz{hh}"], R["S_bf"]], writes=[Rpb[2]], signal=True)
                        else:
                            for q in range(16):
                                S.op("pe", lambda e, c=c, hh=hh, q=q, ocol=ocol: e.matmul(
                                    ocol, lhsT=qmz[hh][:, q, :], rhs=Ssbf[q // 8][:, q % 8, c, :], start=False, stop=(q == 15)),
                                    reads=[Rqmz[hh], RSsbf[q // 8]], writes=[Rpb[2]], signal=(q == 15))
                for h in range(4):
                    S.op("act", lambda e, h=h: e.activation(out=og[:, h * 128:(h + 1) * 128], in_=pbs[2][:, h * 128:(h + 1) * 128],
                                                            func=AF.Square, accum_out=ss4[:, h:h + 1]),
                         reads=[Rpb[2]], writes=[R["og"], R["ss4"]])
                S.op("act", lambda e: e.activation(out=ss4[:, 4:8], in_=ss4[:, 0:4], func=AF.Ln, scale=1.0 / 128, bias=eps_ap),
                     reads=[R["ss4"], R["epsb"]], writes=[R["ss4"]])
                S.op("act", lambda e: e.activation(out=ss4[:, 8:12], in_=ss4[:, 4:8], func=AF.Exp, scale=-0.5),
                     reads=[R["ss4"]], writes=[R["ss4"]])
                S.op("dve", lambda e: e.tensor_tensor(out=zc[:, :, :], in0=pbs[2][:, :].rearrange("p (h v) -> p h v", h=4),
                                                      in1=V(ss4[:, 8:9], [[1, 4], [0, 128]]), op=ALU.mult),
                     reads=[Rpb[2], R["ss4"]], writes=[R["zc"]])
                S.op("dve", lambda e: e.tensor_tensor(out=og[:, :], in0=zc[:, :, :].rearrange("p a b -> p (a b)"), in1=sr_c[:, :],
                                                      op=ALU.mult),
                     reads=[R["zc"], Rsr], writes=[R["og"]])
            def B2t():
                for h in range(4):
                    S.op("pe", lambda e, h=h: e.transpose(out=pb6b[:, h * 128:(h + 1) * 128], in_=og[:, h * 128:(h + 1) * 128],
                                                          identity=ident_b),
                         reads=[R["og"], R["cst_b"]], writes=[Rpb[6]], signal=(h == 3))
                S.op("act", lambda e: e.activation(out=mix_c[:, 0:4, :].rearrange("p a b -> p (a b)"), in_=pb6b[:, 0:512], func=AF.Copy),
                     reads=[Rpb[6]], writes=[Rmix])

            def B3():
                if kind == "p":
                    for c in range(2):
                        S.op("pe", lambda e, c=c: e.matmul(pbs[3][:, c * 256:(c + 1) * 256], lhsT=kin_tok[:, c * 128:(c + 1) * 128],
                                                           rhs=v_c[:, c * 256:(c + 1) * 256], start=True, stop=True),
                             reads=[R["kin_tok"], Rv], writes=[Rpb[3]], signal=(c == 1))
                    for hh in range(2):
                        ps = slice(hh * 64, (hh + 1) * 64)
                        S.op("dve", lambda e, hh=hh, ps=ps: e.tensor_tensor(
                            out=S_p[ps, :, :], in0=V(pbs[3][ps, hh * 128:hh * 128 + 1], [[256, 2], [1, 128]]),
                            in1=S_p[ps, :, :], op=ALU.add),
                            reads=[Rpb[3], R["S_p"]], writes=[R["S_p"]])
                    S.op("dve", lambda e: e.tensor_tensor(out=S_p[:, :, :], in0=S_p[:, :, :],
                                                          in1=V(eq[:, 0, 127:128], [[128, 2], [0, 128]]), op=ALU.mult),
                         reads=[R["S_p"], R["eq"]], writes=[R["S_p"]])
                    S.op("pool", lambda e: e.tensor_copy(out=S_bf[:, :, :], in_=S_p[:, :, :]), reads=[R["S_p"]], writes=[R["S_bf"]])
                    if last_prompt:
                        for hh in range(2):
                            S.dma("sp", lambda e, hh=hh: e.dma_start(
                                out=glap.rearrange("(c hh) d v -> hh d c v", hh=2)[hh], in_=S_p[hh * 64:(hh + 1) * 64, :, :]),
                                reads=[R["S_p"]])
                else:
                    def stage_in(g):
                        stg, Rstg = stages[g % 2], Rstages[g % 2]
                        S.dma("sp", lambda e, g=g, stg=stg: e.dma_start(
                            out=stg, in_=sgla[g * 4:(g + 1) * 4].rearrange("s (c hh) d v -> (hh d) s c v", hh=2)),
                            writes=[Rstg])
                    stage_in(0)
                    for g in range(4):
                        stg, Rstg = stages[g % 2], Rstages[g % 2]
                        if g + 1 < 4:
                            stage_in(g + 1)
                        S.op("dve", lambda e, g=g: e.tensor_tensor(
                            out=km, in0=V(kin_tok[:, 0:1], [[0, 4], [1, 256]]),
                            in1=V(cst_f[:, C_SMT + 4 * g:C_SMT + 4 * g + 1], [[1, 4], [0, 256]]), op=ALU.mult),
                            reads=[R["kin_tok"], R["cst_f"]], writes=Rtm)
                        for j in range(4):
                            bank = 3 + (j % 2) * 3
                            for c in range(2):
                                S.op("pe", lambda e, j=j, c=c, bank=bank: e.matmul(
                                    pbs[bank][:, c * 256:(c + 1) * 256], lhsT=km[:, j, c * 128:(c + 1) * 128],
                                    rhs=v_c[:, c * 256:(c + 1) * 256], start=True, stop=True),
                                    reads=Rtm + [Rv], writes=[Rpb[bank]], signal=(c == 1))
                            for hh in range(2):
                                ps = slice(hh * 64, (hh + 1) * 64)
                                S.op("dve", lambda e, j=j, hh=hh, ps=ps, bank=bank, stg=stg: e.tensor_tensor(
                                    out=stg[ps, j, :, :], in0=V(pbs[bank][ps, hh * 128:hh * 128 + 1], [[256, 2], [1, 128]]),
                                    in1=stg[ps, j, :, :], op=ALU.add),
                                    reads=[Rpb[bank], Rstg], writes=[Rstg])
                        S.op("dve", lambda e, g=g, stg=stg: e.tensor_tensor(
                            out=stg, in0=stg, in1=V(eq[:, 0, 32 * g + 7:32 * g + 8], [[8, 4], [128, 2], [0, 128]]), op=ALU.mult),
                            reads=[Rstg, R["eq"]], writes=[Rstg])
                        S.dma("sp", lambda e, g=g, stg=stg: e.dma_start(
                            out=glas[g * 4:(g + 1) * 4].rearrange("s (c hh) d v -> (hh d) s c v", hh=2), in_=stg),
                            reads=[Rstg])

            def B4():
                obank = (1, 7)
                for half in range(2):
                    mm_group(obank[half], pbs[obank[half]][:, :], lambda kc: mix_c[:, kc, :],
                             lambda kc, half=half: w_out_v[:, kc, half * 512:(half + 1) * 512], [Rmix, Rwout])
                for half in range(2):
                    S.op("dve", lambda e, half=half: e.tensor_tensor(out=tmpm[:, half * 512:(half + 1) * 512],
                                                                     in0=pbs[obank[half]][:, :],
                                                                     in1=Grow[:, gi, half * 512:(half + 1) * 512], op=ALU.mult),
                         reads=[Rpb[obank[half]], RG[gi]], writes=[Rtm[half]])
                    S.op("pool", lambda e, half=half: e.tensor_tensor(out=x1[:, half * 512:(half + 1) * 512],
                                                                      in0=x1[:, half * 512:(half + 1) * 512],
                                                                      in1=tmpm[:, half * 512:(half + 1) * 512], op=ALU.add),
                         reads=[Rx, Rtm[half]], writes=[Rx])

            def N2a():
                norm_a(x1, Rx, 1)

            def N2b():
                norm_b(2, kind, h2T_all[:, :, slot * 128:(slot + 1) * 128], Rh2[slot])

            return dict(N1a=N1a, N1b=N1b, A1=A1, A2=A2, A3p=A3p, A3e=A3e, A4=A4, B0=B0, B1=B1, B2=B2, B2t=B2t, B3=B3, B4=B4, N2a=N2a, N2b=N2b)

        def load_ring(e_idx, rs, first):
            if e_idx == NE - 1:
                w_u, w_d, r_up, r_dn = Rarena_up, Rarena_dn, ring_up_last, ring_dn_last
            else:
                w_u = [Rring_up[rs]] + (Rwin_r if first else [])
                w_d, r_up, r_dn = [Rring_dn[rs]], ring_up[rs], ring_dn[rs]
            S.dma("pool", lambda e: e.dma_start(out=r_up[:, :, :],
                                               in_=w_up.rearrange("(k p) n -> p k n", p=128)[:, :, e_idx * ESZ:(e_idx + 1) * ESZ]),
                  writes=w_u)
            S.dma("pool", lambda e: e.dma_start(out=r_dn[:, :, :],
                                               in_=w_down[e_idx * ESZ:(e_idx + 1) * ESZ, :].rearrange("(k p) n -> p k n", p=128)),
                  writes=w_d)

        ring_ctr = [0]
        ab_ctr = [0]
        pre_loaded = [False]

        def pre_phase2():
            base = ring_ctr[0]
            load_ring(0, base % 3, True)
            load_ring(1, (base + 1) % 3, False)
            pre_loaded[0] = True

        def phase2(slots_info, early_reload=False, after_first_up=None, next_x=None):
            nsl = len(slots_info)
            gsz = 3 if nsl % 4 == 1 else 4
            sts = [list(range(i, min(i + gsz, nsl))) for i in range(0, nsl, gsz)]
            base = ring_ctr[0]
            if not pre_loaded[0]:
                load_ring(0, base % 3, True)
                load_ring(1, (base + 1) % 3, False)
            pre_loaded[0] = False
            units = [(e_idx, st) for e_idx in range(NE) for st in sts]

            def up_part(e_idx, st, ab):
                rs = (base + e_idx) % 3
                T = len(st) * 128
                tok0 = st[0] * 128
                for fc in range(4):
                    r_up, Rr = (ring_up_last, Rarena_up) if e_idx == NE - 1 else (ring_up[rs], [Rring_up[rs]])
                    mm_group(fc, pbs[fc][:, 0:T], lambda kc, fc=fc, r_up=r_up: r_up[:, kc, fc * 128:(fc + 1) * 128],
                             lambda kc: h2T_all[:, kc, tok0:tok0 + T], Rr + [Rh2[s_] for s_ in st])
                for fc in range(4):
                    rbuf, Rrb = (er, R["er"]) if fc % 2 == 0 else (sr[0], R["sr0"])
                    S.op("act", lambda e, fc=fc, rbuf=rbuf, T=T: e.activation(out=rbuf[:, 0:T], in_=pbs[fc][:, 0:T], func=AF.Relu),
                         reads=[Rpb[fc]], writes=[Rrb])
                    S.op("dve", lambda e, fc=fc, rbuf=rbuf, ab=ab, T=T: e.scalar_tensor_tensor(
                        out=aT[ab][:, fc, 0:T], in0=pbs[fc][:, 0:T], scalar=0.0, in1=rbuf[:, 0:T], op0=ALU.max, op1=ALU.mult),
                        reads=[Rpb[fc], Rrb], writes=RaT[ab])

            def down_part(e_idx, st, ab):
                rs = (base + e_idx) % 3
                finals = []
                for si, sidx in enumerate(st):
                    slot, kind, out_ap = slots_info[sidx]
                    gi = 2 if kind == "p" else 3
                    x1 = x1_all[:, slot, :]
                    for half in range(2):
                        bank = 4 + half + 2 * (si % 2)
                        mm_group(bank, pbs[bank][:, :], lambda kc, si=si, ab=ab: aT[ab][:, kc, si * 128:(si + 1) * 128],
                                 lambda kc, half=half: (ring_dn_last if e_idx == NE - 1 else ring_dn[rs])[:, kc, half * 512:(half + 1) * 512],
                                 RaT[ab] + (Rarena_dn if e_idx == NE - 1 else [Rring_dn[rs]]), nk=4)
                    for half in range(2):
                        bank = 4 + half + 2 * (si % 2)
                        if si % 2 == 0:
                            tbuf, Rtb = tmpm[:, half * 512:(half + 1) * 512], Rtm[half]
                        else:
                            tz = za if half == 0 else zb
                            tbuf, Rtb = tz[:, :, :].rearrange("p a b -> p (a b)"), R["za" if half == 0 else "zb"]
                        S.op("dve", lambda e, half=half, bank=bank, gi=gi, tbuf=tbuf: e.tensor_tensor(
                            out=tbuf, in0=pbs[bank][:, :],
                            in1=Grow[:, gi, half * 512:(half + 1) * 512], op=ALU.mult),
                            reads=[Rpb[bank], RG[gi]], writes=[Rtb])
                        S.op("pool" if half == 0 else "dve", lambda e, half=half, x1=x1, tbuf=tbuf: e.tensor_tensor(
                            out=x1[:, half * 512:(half + 1) * 512], in0=x1[:, half * 512:(half + 1) * 512],
                            in1=tbuf, op=ALU.add),
                            reads=[Rx1[slot], Rtb], writes=[Rx1[slot]])
                    if e_idx == NE - 1:
                        def fin(slot=slot, x1=x1, out_ap=out_ap, si=si):
                            rs_ap, Rss = rstd_from(x1, Rx1[slot], 1.0 / D, si % 4)
                            S.op("dve", lambda e: e.scalar_tensor_tensor(out=x1, in0=x1, scalar=rs_ap, in1=fg_row[:, :],
                                                                         op0=ALU.mult, op1=ALU.mult),
                                 reads=[Rx1[slot], Rss, R["fg_row"]], writes=[Rx1[slot]])
                            S.dma("sp", lambda e: e.dma_start(out=out_ap, in_=x1), reads=[Rx1[slot]])
                            if next_x is not None and slot in next_x:
                                nsrc = next_x[slot]
                                S.dma("sp", lambda e: e.dma_start(out=x1_all[:, slot, :], in_=nsrc), writes=[Rx1[slot]])
                        finals.append(fin)
                for f_ in finals:
                    f_()

            abs_ = []
            for u, (e_idx, st) in enumerate(units):
                ab = ab_ctr[0] % 2
                ab_ctr[0] += 1
                abs_.append(ab)
                up_part(e_idx, st, ab)
                if u == 0 and after_first_up is not None:
                    after_first_up()
                if u > 0:
                    pe_, pst = units[u - 1]
                    down_part(pe_, pst, abs_[u - 1])
                if st is sts[0] and e_idx + 2 < NE:
                    load_ring(e_idx + 2, (base + e_idx + 2) % 3, False)
                if early_reload and st is sts[0] and e_idx == NE - 1:
                    for i_, blk in enumerate(WORDER):
                        c0 = WBLK[blk]
                        wd = 16 if blk == "gz" else 512
                        S.dma("pool", lambda e, c0=c0, wd=wd: e.dma_start(out=w_in_v[:, :, c0:c0 + wd], in_=w_in_kv[:, :, c0:c0 + wd],
                                                                       allow_slow_non_contiguous=True),
                              writes=[Rwin_c[blk]] + (Rring if i_ == 0 else []))
            le, lst = units[-1]
            down_part(le, lst, abs_[-1])
            ring_ctr[0] += NE

        res("out")
        S.stage(5)
        for cc in range(4):
            S.op("pe", lambda e, cc=cc: e.transpose(out=pbs[6][:, cc * 32:(cc + 1) * 32], in_=sc_tok[:, cc * 128:(cc + 1) * 128],
                                                    identity=ident_f[0:32, 0:32]),
                 reads=[R["sc_tok"], R["cst_f"]], writes=[Rpb[6]], signal=(cc == 3))
        S.op("act", lambda e: e.activation(out=u_s[:, :, :, 0:2], in_=pbs[6][:, 0:128].rearrange("p (c s j) -> p c s j", c=4, s=16),
                                           func=AF.Copy),
             reads=[Rpb[6]], writes=[R["u_s"]])

        w_in_kv = w_in.rearrange("(k p) n -> p k n", p=128)
        WORDER = ("gz", "qk", "v", "r", "hin", "C", "B")

        w_stage_rr = [0]

        def load_w_block(blk):
            c0 = WBLK[blk]
            if blk == "gz":
                S.dma("pool", lambda e, c0=c0: e.dma_start(out=w_in_v[:, :, c0:c0 + 16], in_=w_in_kv[:, :, c0:c0 + 16],
                                                           allow_slow_non_contiguous=True),
                      writes=[Rwin_c[blk]] + Rring + Rwada_rb)
                return
            stg_ap, stg_R = w_stage[w_stage_rr[0] % 2]
            w_stage_rr[0] += 1
            stg3 = stg_ap.rearrange("p (k n) -> p k n", k=8)
            S.dma("sp", lambda e, c0=c0, stg3=stg3: e.dma_start(out=stg3, in_=w_in_kv[:, :, c0:c0 + 512]), writes=stg_R)
            cast(w_in_v[:, :, c0:c0 + 512], stg3, stg_R, [Rwin_c[blk]] + Rring + Rwada_rb)

        def load_w_out():
            wo_v = w_out.rearrange("(k p) n -> p k n", p=128)
            S.dma("pool", lambda e: e.dma_start(out=w_out_v[:, :, :], in_=wo_v), writes=[Rwout])

        passes = [
            [(0, "s", None)] + [(1 + i, "p", i) for i in range(8)],
            [(i, "p", 8 + i) for i in range(8)],
        ]
        for pi, tiles in enumerate(passes):
            S.stage(6 + 100 * pi)
            def xload(slot, kind, pt):
                src = xs if kind == "s" else xp[pt * 128:(pt + 1) * 128, :]
                S.dma("sp", lambda e: e.dma_start(out=x1_all[:, slot, :], in_=src), writes=[Rx1[slot]])
            if pi == 0:
                xload(*tiles[0])
            NPF = 3
            steps = [phase1_steps(slot, kind, pt, (kind == "p" and pt == 15), k % 2) for k, (slot, kind, pt) in enumerate(tiles)]
            nt = len(tiles)
            ND = 3 if pi == 0 else 0

            def call(k, name):
                if 0 <= k < nt:
                    steps[k][name]()
            if pi == 0:
                call(0, "N1a"); call(0, "N1b")
                load_w_block("gz"); load_w_block("qk")
                call(0, "A1")
                load_w_block("v"); load_w_block("r")
                call(0, "A2")
                load_w_block("hin"); load_w_block("C")
                call(0, "A3p"); call(0, "A3e")
                load_w_block("B")
                call(0, "A4")
                load_w_out()
                for k_ in range(1, min(NPF, len(tiles))):
                    xload(*tiles[k_])
            else:
                for nm in ("N1a", "N1b", "A1", "A2", "A3p", "A3e", "A4"):
                    call(0, nm)
            if pi == 0:
                late_start()
            call(1, "N1a")
            call(1, "N1b")
            for k in range(nt):
                if pi == 0 and k == 0:
                    pass
                elif pi == 0 and k == 1:
                    xload(*tiles[3])
                    xload(*tiles[4])
                elif k + NPF < nt:
                    xload(*tiles[k + NPF])
                S.stage(10 + 100 * pi + k)

                def hook(sub):
                    if pi == 0:
                        for i_ in late_hooks.get((k, sub), []):
                            late_step(i_)
                call(k + 1, "A1"); call(k, "B0"); call(k - 1 - ND, "N2b")
                hook(1)
                call(k + 1, "A2"); call(k, "B1"); call(k + 2, "N1a")
                hook(2)
                call(k + 1, "A3p"); call(k + 2, "N1b"); call(k, "B2"); call(k + 1, "A3e")
                hook(3)
                call(k + 1, "A4")
                if k == nt - 2:
                    pre_phase2()
                call(k, "B2t"); call(k, "B3")
                hook(4)
                call(k, "B4")
                call(k - ND, "N2a")
                hook(5)
            def flush_norm2(nt=nt, ND=ND, call=call):
                call(nt - 1 - ND, "N2b")
                for kk in range(nt - ND, nt):
                    call(kk, "N2a")
                    call(kk, "N2b")
            info = []
            for (slot, kind, pt) in tiles:
                out_ap = ys if kind == "s" else yp[pt * 128:(pt + 1) * 128, :]
                info.append((slot, kind, out_ap))
            S.stage(50 + 100 * pi)
            nx = None
            if pi == 0:
                nx = {sl_: xp[pt_ * 128:(pt_ + 1) * 128, :] for (sl_, kd_, pt_) in passes[1][:NPF]}
            phase2(info, early_reload=(pi == 0), after_first_up=flush_norm2, next_x=nx)

        S.wait_all("sp", [(k, v) for k, v in S.count.items() if k.startswith("dma_sp") and v > 0])
        S.emit(block)
    return nc


_CACHE = {}


def kernel(x_prompt, x_sample, state_gla, state_conv, c_prompt, c_sample, w_ada, b_ada, norm1_g, w_in,
           w_gate_up, b_gate, gla_norm_g, w_conv, w_out, norm2_g, w_up, w_down, final_g):
    f = lambda a: np.ascontiguousarray(np.asarray(a, dtype=np.float32))
    x_prompt, x_sample, state_gla, state_conv = f(x_prompt), f(x_sample), f(state_gla), f(state_conv)
    c_prompt, c_sample = f(c_prompt), f(c_sample)
    if "nc" not in _CACHE:
        _CACHE["nc"] = build_program()
        _CACHE["consts"] = make_consts()
    nc = _CACHE["nc"]
    cf, cb = _CACHE["consts"]
    shared = {
        "w_ada": f(w_ada)[0], "b_ada": f(b_ada).reshape(1, -1), "norm1_g": f(norm1_g).reshape(1, -1),
        "w_in": f(w_in)[0], "w_gate_up": f(w_gate_up)[0], "b_gate": f(b_gate).reshape(1, -1),
        "gla_norm_g": f(gla_norm_g).reshape(1, -1), "w_conv": f(w_conv)[0], "w_out": f(w_out)[0],
        "norm2_g": f(norm2_g).reshape(1, -1), "w_up": f(w_up)[0], "w_down": f(w_down)[0],
        "final_g": f(final_g).reshape(1, -1), "consts_f": cf, "consts_b": cb,
    }
    in_maps = []
    for c in range(NCORES):
        m = dict(shared)
        m["xp"] = x_prompt[c]
        m["xs"] = np.ascontiguousarray(x_sample[16 * c:16 * c + 16].reshape(128, D))
        m["sgla"] = np.ascontiguousarray(state_gla[0, 16 * c:16 * c + 16])
        m["sconv"] = np.ascontiguousarray(state_conv[0, 16 * c:16 * c + 16].reshape(32, 512))
        m["cvec"] = np.ascontiguousarray(np.concatenate([c_prompt[c:c + 1], c_sample[16 * c:16 * c + 16]], axis=0))
        in_maps.append(m)
    res = run_bass_kernel_spmd(nc, in_maps, core_ids=list(range(NCORES)))
    rs = res.results
    y_prompt = np.stack([np.asarray(r["yp"]) for r in rs], axis=0).astype(np.float32)
    y_sample = np.concatenate([np.asarray(r["ys"]).reshape(16, 8, D) for r in rs], axis=0).astype(np.float32)
    gla_p = np.stack([np.asarray(r["glap"]) for r in rs], axis=0)[None].astype(np.float32)
    conv_p = np.stack([np.asarray(r["convp"]) for r in rs], axis=0)[None].astype(np.float32)
    gla_s = np.concatenate([np.asarray(r["glas"]) for r in rs], axis=0)[None].astype(np.float32)
    conv_s = np.concatenate([np.asarray(r["convs"]).reshape(16, 2, 512) for r in rs], axis=0)[None].astype(np.float32)
    return (y_prompt, y_sample, gla_p, conv_p, gla_s, conv_s)
```

```python
from contextlib import ExitStack
import numpy as np
import ml_dtypes
import concourse.bass as bass
import concourse.mybir as mybir
from concourse.bass_utils import run_bass_kernel_spmd

F32 = mybir.dt.float32
BF16 = mybir.dt.bfloat16
AF = mybir.ActivationFunctionType
ALU = mybir.AluOpType
EPS = 1e-6
NCORES = 8
D = 1024
DIN = 3088
DFF = 4096
NE = 8
ESZ = DFF // NE


class Res:
    __slots__ = ("name", "last_w", "readers")

    def __init__(self, name):
        self.name = name
        self.last_w = None
        self.readers = {}


class Sched:
    ENGS = ("pe", "act", "dve", "pool", "sp")

    def __init__(self, n_dma_sems=8):
        self.streams = {e: [] for e in self.ENGS}
        self.sems = {}
        self.count = {}
        self.known = {e: {} for e in self.ENGS}
        self.n_dma_sems = n_dma_sems
        self.dma_rr = {e: 0 for e in self.ENGS}
        self.sem_keys = ["pe", "act", "dve", "pool"]
        for q in ("sp", "pool"):
            for i in range(n_dma_sems):
                self.sem_keys.append(f"dma_{q}_{i}")
        for k in self.sem_keys:
            self.count[k] = 0
        self.dead = False
        self.limit = 10 ** 9

    def stage(self, n):
        if n >= self.limit:
            self.dead = True

    def _need(self, eng, waits, ev, war=False):
        if ev is None:
            return
        key, val = ev
        if key == eng and (eng == "pe" or (war and eng != "pool")):
            return
        if self.known[eng].get(key, 0) >= val:
            return
        if waits.get(key, 0) < val:
            waits[key] = val

    def _collect(self, eng, reads, writes):
        waits = {}
        for r in reads:
            self._need(eng, waits, r.last_w)
        for w in writes:
            self._need(eng, waits, w.last_w)
            for k, v in w.readers.items():
                self._need(eng, waits, (k, v), war=True)
        for k, v in waits.items():
            self.known[eng][k] = v
        return waits

    def op(self, eng, fn, reads=(), writes=(), signal=True):
        assert signal or eng == "pe"
        if self.dead:
            return None
        waits = self._collect(eng, reads, writes)
        if signal:
            self.count[eng] += 1
            ev = (eng, self.count[eng])
        else:
            ev = (eng, self.count[eng] + 1)
        for r in reads:
            if r.readers.get(ev[0], 0) < ev[1]:
                r.readers[ev[0]] = ev[1]
        for w in writes:
            w.last_w = ev
            w.readers = {}
        self.streams[eng].append((waits, fn, (eng, 1) if signal else None))
        return ev

    def dma(self, q, fn, reads=(), writes=()):
        if self.dead:
            return None
        i = self.dma_rr[q]
        self.dma_rr[q] = (i + 1) % self.n_dma_sems
        key = f"dma_{q}_{i}"
        waits = self._collect(q, reads, writes)
        prev = self.count[key]
        if prev > 0 and self.known[q].get(key, 0) < prev:
            waits[key] = prev
            self.known[q][key] = prev
        self.count[key] = prev + 16
        ev = (key, prev + 16)
        for r in reads:
            if r.readers.get(key, 0) < ev[1]:
                r.readers[key] = ev[1]
        for w in writes:
            w.last_w = ev
            w.readers = {}
        self.streams[q].append((waits, fn, (key, 16)))
        return ev

    def wait_all(self, eng, events):
        waits = {}
        for ev in events:
            self._need(eng, waits, ev)
        for k, v in waits.items():
            self.known[eng][k] = v
        self.streams[eng].append((waits, None, None))

    def emit(self, block):
        sems = self.sems

        def run(engname):
            def body(e):
                for waits, fn, inc in self.streams[engname]:
                    for k, v in waits.items():
                        e.wait_ge(sems[k], v)
                    if fn is None:
                        continue
                    ins = fn(e)
                    if inc is not None:
                        ins.then_inc(sems[inc[0]], inc[1])
            return body

        block.tensor(run("pe"))
        block.scalar(run("act"))
        block.vector(run("dve"))
        block.gpsimd(run("pool"))
        block.sync(run("sp"))


def V(apobj, dims):
    return bass.AP(apobj.tensor, apobj.offset, [list(apobj.ap[0])] + [list(d) for d in dims])


C_ID, C_MP, C_MS, C_SCAN, C_ONES, C_EP, C_ES, C_SMT = 0, 128, 256, 384, 512, 640, 768, 896
NCF = 912
CB_ID, CB_SM = 0, 128
NCB = 128 + 2048


def make_consts():
    cf = np.zeros((128, NCF), np.float32)
    s = np.arange(128)[:, None]
    t = np.arange(128)[None, :]
    cf[:, C_ID:C_ID + 128] = (s == t)
    cf[:, C_MP:C_MP + 128] = (s <= t)
    cf[:, C_MS:C_MS + 128] = (s <= t) & (s // 8 == t // 8)
    cf[:, C_SCAN:C_SCAN + 128] = (t % 8 != 0)
    cf[:, C_ONES:C_ONES + 128] = 1.0
    cf[0, C_EP:C_EP + 128] = 1.0
    for q in range(16):
        cf[1 + q, C_ES + 8 * q:C_ES + 8 * q + 8] = 1.0
    cf[:, C_SMT:C_SMT + 16] = (np.arange(128)[:, None] // 8 == np.arange(16)[None, :])
    cb = np.zeros((128, NCB), np.float32)
    cb[:, CB_ID:CB_ID + 128] = (s == t)
    sm = (np.arange(128)[None, :] // 8 == np.arange(16)[:, None]).astype(np.float32)
    cb[:, CB_SM:CB_SM + 2048] = sm.reshape(1, 2048)
    return cf, cb.astype(ml_dtypes.bfloat16)


def build_program(limit=10 ** 9):
    nc = bass.Bass("TRN2", target_bir_lowering=False)

    def din(name, shape, dt=F32):
        return nc.dram_tensor(name, list(shape), dt, kind="ExternalInput").ap()

    def dout(name, shape):
        return nc.dram_tensor(name, list(shape), F32, kind="ExternalOutput").ap()

    xp = din("xp", [2048, D])
    xs = din("xs", [128, D])
    sgla = din("sgla", [16, 4, 64, 128])
    sconv = din("sconv", [32, 512])
    cvec = din("cvec", [17, D])
    w_ada = din("w_ada", [D, 6 * D])
    b_ada = din("b_ada", [1, 6 * D])
    n1g = din("norm1_g", [1, D])
    w_in = din("w_in", [D, DIN])
    w_gu = din("w_gate_up", [16, 256])
    b_gate = din("b_gate", [1, 256])
    glag = din("gla_norm_g", [1, 512])
    w_conv = din("w_conv", [3, 512])
    w_out = din("w_out", [D, D])
    n2g = din("norm2_g", [1, D])
    w_up = din("w_up", [D, DFF])
    w_down = din("w_down", [DFF, D])
    fing = din("final_g", [1, D])
    cfd = din("consts_f", [128, NCF])
    cbd = din("consts_b", [128, NCB], BF16)

    yp = dout("yp", [2048, D])
    ys = dout("ys", [128, D])
    glap = dout("glap", [4, 64, 128])
    convp = dout("convp", [2, 512])
    glas = dout("glas", [16, 4, 64, 128])
    convs = dout("convs", [32, 512])

    S = Sched()
    S.limit = limit
    R = {}

    def res(name):
        R[name] = Res(name)
        return R[name]

    with ExitStack() as es:
        E = es.enter_context

        def sb(name, shape, dt=F32):
            res(name)
            return E(nc.sbuf_tensor(name, list(shape), dt))

        NSLOT = 9
        x1_all = E(nc.sbuf_tensor("x1_all", [128, NSLOT, D], F32))
        Rx1 = [res(f"x1_{i}") for i in range(NSLOT)]
        h2T_all = E(nc.sbuf_tensor("h2T_all", [128, 8, NSLOT * 128], BF16))
        Rh2 = [res(f"h2_{i}") for i in range(NSLOT)]
        NWB = 8 * DIN + 8 * D
        wbig = E(nc.sbuf_tensor("wbig", [128, NWB], BF16))
        Rwout = res("wout")
        WBLK = {"qk": 0, "v": 512, "gz": 1024, "r": 1040, "B": 1552, "C": 2064, "hin": 2576}
        Rwin_c = {nm: res(f"win_{nm}") for nm in WBLK}
        Rwin_r = list(Rwin_c.values())
        Rring_up = [res(f"ringu{i}") for i in range(3)]
        Rring_dn = [res(f"ringd{i}") for i in range(3)]
        Rring = Rring_up + Rring_dn
        w_in_v = wbig[:, 0:8 * DIN].rearrange("p (k n) -> p k n", k=8)
        w_out_v = wbig[:, 8 * DIN:NWB].rearrange("p (k n) -> p k n", k=8)
        ring_up = [wbig[:, s * 8192:s * 8192 + 4096].rearrange("p (k n) -> p k n", k=8) for s in range(3)]
        ring_dn = [wbig[:, s * 8192 + 4096:(s + 1) * 8192].rearrange("p (k n) -> p k n", k=4) for s in range(3)]

        cst_f = sb("cst_f", [128, NCF])
        cst_b = sb("cst_b", [128, NCB], BF16)
        wg_sb = sb("wg_sb", [16, 256])
        nbg = sb("nbg", [128, 2])
        wc = sb("wc", [128, 3, 4])
        ngT = sb("ngT", [128, 2, 8])
        modT = sb("modT", [128, 4, 8, 17])
        fg_row = sb("fg_row", [128, D])
        gg_row = sb("gg_row", [128, 512])
        Grow = E(nc.sbuf_tensor("Grow", [128, 4, D], F32))
        RG = [res(f"G{i}") for i in range(4)]
        S_p = sb("S_p", [128, 2, 128])
        S_bf = sb("S_bf", [128, 2, 128], BF16)
        u_p = sb("u_p", [128, 4, 130])
        u_s = sb("u_s", [128, 4, 16, 10])
        sc_tok = sb("sc_tok", [32, 512])
        cn_sb = sc_tok
        cn_T = sb("cn_T", [128, 4, 32])
        scT = sb("scT", [128, 8, 17], BF16)
        bch0 = sb("bch0", [17, 512])
        bch = [bch0, bch0]
        ss_bufs = [sb(f"ss{i}", [128, 4]) for i in range(4)]
        ss4 = sb("ss4", [128, 12])
        xs_bf = sb("xs_bf", [128, D], BF16)
        tmpm = E(nc.sbuf_tensor("tmpm", [128, D], F32))
        Rtm = [res("tmpm0"), res("tmpm1")]
        hTbuf = E(nc.sbuf_tensor("hTbuf", [128, 2, 8, 128], BF16))
        RhT = [res("hT0"), res("hT1")]
        hTs = [hTbuf[:, 0, :, :], hTbuf[:, 1, :, :]]
        arena = E(nc.sbuf_tensor("arena", [128, 4096], F32))
        Rarena = []
        arena_off = [0]

        def carve(name, shape, dt=F32):
            n = 1
            for d_ in shape[1:]:
                n *= d_
            ncol = n if dt == F32 else n // 2
            a0 = arena_off[0]
            arena_off[0] += ncol
            assert arena_off[0] <= 4096
            ap_ = arena[:, a0:a0 + ncol]
            if dt != F32:
                ap_ = ap_.bitcast(dt)
            if len(shape) == 3:
                ap_ = ap_.rearrange("p (a b) -> p a b", a=shape[1])
            Rarena.append(res(name))
            return ap_
        qk_sb = [carve(f"qk_sb{i}", [128, 4, 128]) for i in range(2)]
        gz_sb = [sb(f"gz_sb{i}", [16, 128]) for i in range(2)]
        v_bf = [carve(f"v_bf{i}", [128, 512], BF16) for i in range(2)]
        er = sb("er", [128, 512])
        sr = [sb(f"sr{i}", [128, 512]) for i in range(2)]
        hin_sb = carve("hin_sb", [128, 4, 128])
        zc = carve("zc", [128, 4, 128])
        mixbuf = E(nc.sbuf_tensor("mixbuf", [128, 2, 8, 128], BF16))
        res("mixT0"); res("mixT1")
        mixT = [mixbuf[:, 0, :, :], mixbuf[:, 1, :, :]]
        el = carve("el", [128, 2, 128])
        cum = carve("cum", [128, 2, 128])
        eq = carve("eq", [128, 2, 128])
        ek = carve("ek", [128, 2, 128])
        qinz = [sb(f"qinz{i}", [128, 2, 128], BF16) for i in range(2)]
        kin = sb("kin", [128, 2, 128], BF16)
        kin_tok = sb("kin_tok", [128, 256], BF16)
        attm = carve("attm", [128, 4, 128], BF16)
        og = carve("og", [128, 512], BF16)
        assert arena_off[0] == 4096
        assert len(Rarena) == 12
        Rarena_up = Rarena[0:5] + [res("ring_last_up")]
        Rarena_dn = Rarena[5:12] + [res("ring_last_dn")]
        ring_up_last = arena[:, 0:2048].bitcast(BF16).rearrange("p (k n) -> p k n", k=8)
        ring_dn_last = arena[:, 2048:4096].bitcast(BF16).rearrange("p (k n) -> p k n", k=4)
        aT = [hTbuf[:, :, :, :].rearrange("p a k t -> p (a k t)").rearrange("p (f t) -> p f t", f=4),
              mixbuf[:, :, :, :].rearrange("p a k t -> p (a k t)").rearrange("p (f t) -> p f t", f=4)]
        RaT = [RhT, [R["mixT0"], R["mixT1"]]]
        za = sb("za", [128, 4, 128])
        zb = sb("zb", [128, 4, 128])

        pbs = [E(nc.psum_tensor(f"pb{i}", [128, 512], F32)) for i in range(8)]
        Rpb = [res(f"pb{i}") for i in range(8)]
        pb0b = pbs[0][:, :].bitcast(BF16)
        pb5b = pbs[5][:, :].bitcast(BF16)
        pb6b = pbs[6][:, :].bitcast(BF16)

        for k in S.sem_keys:
            S.sems[k] = E(nc.semaphore(k))
        block = E(nc.Block())

        ident_f = cst_f[:, C_ID:C_ID + 128]
        ident_b = cst_b[:, CB_ID:CB_ID + 128]
        ones_f = cst_f[:, C_ONES:C_ONES + 128]

        def slot_bf(j):
            return x1_all[:, j, :].bitcast(BF16)
        wada_ring = [slot_bf(5).rearrange("p (k n) -> p k n", k=8), slot_bf(6).rearrange("p (k n) -> p k n", k=8)]
        Rwada = [Rx1[5], Rx1[6]]
        Ssbf = [slot_bf(5).rearrange("p (s c v) -> p s c v", s=8, c=2), slot_bf(6).rearrange("p (s c v) -> p s c v", s=8, c=2)]
        RSsbf = [Rx1[5], Rx1[6]]
        stages = [x1_all[:, 7, :].rearrange("p (s c v) -> p s c v", s=4, c=2),
                  x1_all[:, 3, :].rearrange("p (s c v) -> p s c v", s=4, c=2)]
        Rstages = [Rx1[7], Rx1[3]]
        qmz = [slot_bf(8)[:, 0:2048].rearrange("p (s t) -> p s t", s=16),
               slot_bf(4)[:, 0:2048].rearrange("p (s t) -> p s t", s=16)]
        Rqmz = [Rx1[8], Rx1[4]]
        km = tmpm[:, :].bitcast(BF16)[:, 0:1024].rearrange("p (s f) -> p s f", s=4)


        cast_rr = [0]

        def cast(out, in_, reads, writes):
            eng = ("dve", "act", "dve")[cast_rr[0] % 3]
            cast_rr[0] += 1
            if eng == "act":
                S.op("act", lambda e: e.activation(out=out, in_=in_, func=AF.Copy), reads=reads, writes=writes)
            else:
                S.op(eng, lambda e: e.tensor_copy(out=out, in_=in_), reads=reads, writes=writes)

        def stage2(a):
            return x1_all[:, a:a + 2, :].rearrange("p a b -> p (a b)")

        def stage4(a):
            return x1_all[:, a:a + 4, :].rearrange("p a b -> p (a b)")
        wada_stage = [(stage2(1), [Rx1[1], Rx1[2]]), (stage2(3), [Rx1[3], Rx1[4]]), (stage2(7), [Rx1[7], Rx1[8]])]
        w_stage = [(stage4(1), [Rx1[1], Rx1[2], Rx1[3], Rx1[4]]), (stage4(5), [Rx1[5], Rx1[6], Rx1[7], Rx1[8]])]

        def ld(out, in_, writes, q="sp", nonc=False):
            if nonc:
                S.dma(q, lambda e: e.dma_start(out=out, in_=in_, allow_slow_non_contiguous=True), writes=writes)
            else:
                S.dma(q, lambda e: e.dma_start(out=out, in_=in_), writes=writes)

        ld(cst_f[:, :], cfd, [R["cst_f"]])
        ld(cst_b[:, :], cbd, [R["cst_b"]])
        ld(tmpm[0:17, :], cvec, Rtm)
        ld(wg_sb[:, :], w_gu, [R["wg_sb"]])
        ld(nbg[:, :], b_gate.rearrange("o (c p) -> p (o c)", p=128), [R["nbg"]], nonc=True)
        ld(wc[:, :, :], w_conv.rearrange("j (c p) -> p j c", p=128), [R["wc"]], nonc=True)
        ld(ngT[:, 0, :], n1g.rearrange("o (c p) -> p (o c)", p=128), [R["ngT"]], nonc=True)
        ld(ngT[:, 1, :], n2g.rearrange("o (c p) -> p (o c)", p=128), [R["ngT"]], nonc=True)
        ld(fg_row[:, :], bass.AP(fing.tensor, fing.offset, [[0, 128], [1, D]]), [R["fg_row"]])
        ld(gg_row[:, :], bass.AP(glag.tensor, glag.offset, [[0, 128], [1, 512]]), [R["gg_row"]])
        ld(sc_tok[:, :], sconv, [R["sc_tok"]])

        S.stage(1)
        S.op("dve", lambda e: e.tensor_scalar(out=nbg[:, :], in0=nbg[:, :], scalar1=-1.0, scalar2=None, op0=ALU.mult),
             reads=[R["nbg"]], writes=[R["nbg"]])
        S.op("pool", lambda e: e.memset(S_p[:, :, :], 0.0), writes=[R["S_p"]])
        S.op("pool", lambda e: e.memset(S_bf[:, :, :], 0.0), writes=[R["S_bf"]])
        S.op("pool", lambda e: e.memset(u_p[:, :, :], 0.0), writes=[R["u_p"]])
        for i in range(2):
            S.op("pool", lambda e, i=i: e.memset(qinz[i][:, :, :], 0.0), writes=[R[f"qinz{i}"]])

        S.stage(2)
        cv = tmpm[0:17, :]
        ex = Grow[0:17, 3, :]
        Rtmpe = [RG[3]]
        cvb = xs_bf[0:17, :]
        S.op("act", lambda e: e.activation(out=ex, in_=cv, func=AF.Exp, scale=-1.0),
             reads=Rtm, writes=Rtmpe)
        S.op("act", lambda e: e.activation(out=ex, in_=ex, func=AF.Ln, bias=1.0), reads=Rtmpe, writes=Rtmpe)
        S.op("act", lambda e: e.activation(out=ex, in_=ex, func=AF.Exp, scale=-1.0), reads=Rtmpe, writes=Rtmpe)
        S.op("dve", lambda e: e.tensor_tensor(out=cvb, in0=cv, in1=ex, op=ALU.mult),
             reads=Rtm + Rtmpe, writes=[R["xs_bf"]])
        for kc in range(8):
            S.op("pe", lambda e, kc=kc: e.transpose(out=pb0b[:, kc * 32:kc * 32 + 17], in_=cvb[:, kc * 128:(kc + 1) * 128],
                                                    identity=ident_b[0:17, 0:17]),
                 reads=[R["xs_bf"], R["cst_b"]], writes=[Rpb[0]], signal=(kc == 7))
        S.op("act", lambda e: e.activation(out=scT[:, :, :], in_=pb0b[:, 0:256].rearrange("p (k s) -> p k s", k=8)[:, :, 0:17],
                                           func=AF.Copy),
             reads=[Rpb[0]], writes=[R["scT"]])

        vec_slot = {0: 1, 1: 0, 3: 3, 4: 2}
        wada_v = w_ada.rearrange("(k p) n -> p k n", p=128)
        modc = er[0:17, 0:512]
        S.stage(3)
        wada_rb = [wbig[:, r * 4096:(r + 1) * 4096].rearrange("p (k n) -> p k n", k=8) for r in range(4)]
        Rwada_rb = [res(f"wadar{r}") for r in range(4)]
        RmodT = [res(f"modT{i}") for i in range(4)]

        def mod_dma(j, ring3, ringR, direct):
            if direct:
                S.dma("pool", lambda e, j=j: e.dma_start(out=ring3, in_=wada_v[:, :, j * 512:(j + 1) * 512]), writes=ringR)
            else:
                stg_ap, stg_R = w_stage[j % 2]
                stg3 = stg_ap.rearrange("p (k n) -> p k n", k=8)
                S.dma("sp", lambda e, j=j, stg3=stg3: e.dma_start(out=stg3, in_=wada_v[:, :, j * 512:(j + 1) * 512]), writes=stg_R)
                cast(ring3, stg3, stg_R, ringR)

        modc2_t = sb("modc2", [17, 512])

        def mod_compute(j, ring3, ringR, part=None):
            vec, sub = j // 2, j % 2
            if part is None:
                mc, Rmc = modc, R["er"]
            else:
                mc, Rmc = modc2_t[:, :], R["modc2"]
            if part in (None, "a"):
                mod_compute_a(j, ring3, ringR, mc, Rmc)
            if part in (None, "b"):
                mod_compute_b(vec, sub, mc, Rmc)

        def mod_compute_a(j, ring3, ringR, mc, Rmc):
            S.dma("sp", lambda e, j=j: e.dma_start(
                out=bch0[:, :], in_=bass.AP(b_ada.tensor, b_ada.offset + j * 512, [[0, 17], [1, 512]])),
                writes=[R["bch0"]])
            for kc in range(8):
                S.op("pe", lambda e, kc=kc: e.matmul(pbs[7][0:17, 0:512], lhsT=scT[:, kc, :], rhs=ring3[:, kc, :],
                                                    start=(kc == 0), stop=(kc == 7)),
                     reads=[R["scT"]] + ringR, writes=[Rpb[7]], signal=(kc == 7))
            S.op("dve", lambda e: e.tensor_tensor(out=mc, in0=pbs[7][0:17, 0:512], in1=bch0[:, :], op=ALU.add),
                 reads=[Rpb[7], R["bch0"]], writes=[Rmc])

        def mod_compute_b(vec, sub, mc, Rmc):
            if vec in vec_slot:
                slot = vec_slot[vec]
                for h in range(4):
                    S.op("pe", lambda e, h=h: e.transpose(out=pbs[6][:, h * 32:h * 32 + 17], in_=mc[:, h * 128:(h + 1) * 128],
                                                          identity=ident_f[0:17, 0:17]),
                         reads=[Rmc, R["cst_f"]], writes=[Rpb[6]], signal=(h == 3))
                S.op("act", lambda e, slot=slot, sub=sub: e.activation(
                    out=modT[:, slot, 4 * sub:4 * sub + 4, :],
                    in_=pbs[6][:, 0:128].rearrange("p (k s) -> p k s", k=4)[:, :, 0:17], func=AF.Copy),
                    reads=[Rpb[6]], writes=[RmodT[slot]])
            else:
                gi0 = 0 if vec == 2 else 2
                for g in range(2):
                    esel = cst_f[0:17, C_EP:C_EP + 128] if g == 0 else cst_f[0:17, C_ES:C_ES + 128]
                    S.op("pe", lambda e, g=g, esel=esel: e.matmul(pbs[5 - g][:, :], lhsT=esel, rhs=mc, start=True, stop=True),
                         reads=[Rmc, R["cst_f"]], writes=[Rpb[5 - g]], signal=True)
                for g in range(2):
                    S.op("act", lambda e, g=g, gi0=gi0, sub=sub: e.activation(
                        out=Grow[:, gi0 + g, sub * 512:(sub + 1) * 512], in_=pbs[5 - g][:, :], func=AF.Copy),
                        reads=[Rpb[5 - g]], writes=[RG[gi0 + g]])

        def mod_post(which, slot):
            S.op("dve", lambda e: e.tensor_scalar(out=modT[:, slot, :, :], in0=modT[:, slot, :, :], scalar1=1.0,
                                                  scalar2=None, op0=ALU.add),
                 reads=[RmodT[slot]], writes=[RmodT[slot]])
            S.op("dve", lambda e: e.tensor_tensor(
                out=modT[:, slot, :, :], in0=modT[:, slot, :, :], in1=V(ngT[:, which, 0:1], [[1, 8], [0, 17]]), op=ALU.mult),
                reads=[RmodT[slot], R["ngT"]], writes=[RmodT[slot]])

        for j in range(4):
            mod_dma(j, wada_rb[j % 4][:, :, :], [Rwada_rb[j % 4]], False)
            mod_compute(j, wada_rb[j % 4][:, :, :], [Rwada_rb[j % 4]])
        mod_post(0, 0)
        late = [4, 5, 6, 7, 8, 9, 10, 11]
        late_rb = [(h2T_all[:, :, 128:640], [Rh2[i] for i in range(1, 5)]), (h2T_all[:, :, 640:1152], [Rh2[i] for i in range(5, 9)])]

        def late_start():
            for i in range(2):
                mod_dma(late[i], late_rb[i][0], late_rb[i][1], True)

        def late_step(i2):
            i, half = i2 // 2, i2 % 2
            ring3, ringR = late_rb[i % 2]
            if half == 0:
                mod_compute(late[i], ring3, ringR, part="a")
                if i + 2 < len(late):
                    mod_dma(late[i + 2], ring3, ringR, True)
            else:
                mod_compute(late[i], ring3, ringR, part="b")
                if late[i] == 9:
                    mod_post(1, 2)
        late_hooks = {}
        for i2 in range(16):
            late_hooks[(i2 // 4, 1 + i2 % 4)] = [i2]

        def mod_bc(slot, kind):
            if kind == "p":
                return V(modT[:, slot, 0, 0:1], [[17, 8], [0, 128]])
            return V(modT[:, slot, 0, 1:2], [[17, 8], [1, 16], [0, 8]])

        def tokview(ap2d, kind):
            if len(ap2d.shape) == 3:
                return ap2d if kind == "p" else ap2d.rearrange("p k (s t) -> p k s t", s=16)
            if kind == "p":
                return ap2d.rearrange("p (k t) -> p k t", k=8)
            return ap2d.rearrange("p (k s t) -> p k s t", k=8, s=16)

        def rstd_from(src_ap, Rsrc, n_inv, si=0):
            ss, Rss = ss_bufs[si], R[f"ss{si}"]
            S.op("act", lambda e: e.activation(out=xs_bf[:, :], in_=src_ap, func=AF.Square, accum_out=ss[:, 0:1]),
                 reads=[Rsrc], writes=[R["xs_bf"], Rss])
            S.op("act", lambda e: e.activation(out=ss[:, 1:2], in_=ss[:, 0:1], func=AF.Ln, scale=n_inv, bias=eps_ap),
                 reads=[Rss, R["epsb"]], writes=[Rss])
            S.op("act", lambda e: e.activation(out=ss[:, 2:3], in_=ss[:, 1:2], func=AF.Exp, scale=-0.5),
                 reads=[Rss], writes=[Rss])
            return ss[:, 2:3], Rss

        def norm_a(src_ap, Rsrc, si=0):
            rs_ap, Rss = rstd_from(src_ap, Rsrc, 1.0 / D, si)
            S.op("dve", lambda e: e.tensor_scalar(out=xs_bf[:, :], in0=src_ap, scalar1=rs_ap, scalar2=None, op0=ALU.mult),
                 reads=[Rsrc, Rss], writes=[R["xs_bf"]])

        def norm_b(gslot, kind, dst_ap, Rdst):
            for kc in range(8):
                S.op("pe", lambda e, kc=kc: e.transpose(out=pb0b[:, kc * 128:(kc + 1) * 128], in_=xs_bf[:, kc * 128:(kc + 1) * 128],
                                                        identity=ident_b),
                     reads=[R["xs_bf"], R["cst_b"]], writes=[Rpb[0]], signal=(kc == 7))
            S.op("dve", lambda e: e.tensor_tensor(out=tokview(tmpm[:, :], kind), in0=tokview(pb0b[:, :], kind),
                                                  in1=mod_bc(gslot, kind), op=ALU.mult),
                 reads=[Rpb[0], RmodT[gslot]], writes=Rtm)
            S.op("pool", lambda e: e.tensor_tensor(out=tokview(dst_ap, kind), in0=tokview(tmpm[:, :], kind),
                                                   in1=mod_bc(gslot + 1, kind), op=ALU.add),
                 reads=Rtm + [RmodT[gslot + 1]], writes=[Rdst])

        eps_t = sb("epsb", [128, 1])
        eps_ap = eps_t[:, 0:1]
        S.op("pool", lambda e: e.memset(eps_t[:, :], EPS), writes=[R["epsb"]])

        CQ, CK, CV_, CGZ, CR, CB, CC, CH = 0, 256, 512, 1024, 1040, 1552, 2064, 2576

        def mm_group(bank, col, lhsT_fn, rhs_fn, reads, nk=8, last_signal=True):
            for kc in range(nk):
                l_ap, r_ap = lhsT_fn(kc), rhs_fn(kc)
                S.op("pe", lambda e, kc=kc, l_ap=l_ap, r_ap=r_ap, col=col, nk=nk: e.matmul(
                    col, lhsT=l_ap, rhs=r_ap, start=(kc == 0), stop=(kc == nk - 1)),
                     reads=(reads(kc) if callable(reads) else reads), writes=[Rpb[bank]],
                     signal=(last_signal and kc == nk - 1))

        def phase1_steps(slot, kind, ptile, last_prompt, par):
            gi = 0 if kind == "p" else 1
            x1 = x1_all[:, slot, :]
            Rx = Rx1[slot]
            qk_c, gz_c, v_c, sr_c, mix_c = qk_sb[par], gz_sb[par], v_bf[par], sr[par], mixT[par]
            Rqk, Rgz, Rv, Rsr, Rmix = R[f"qk_sb{par}"], R[f"gz_sb{par}"], R[f"v_bf{par}"], R[f"sr{par}"], R[f"mixT{par}"]
            hT = hTs[par]
            def rdb(blk):
                return [RhT[par], Rwin_c[blk]]
            mcol = C_MP if kind == "p" else C_MS
            scan0 = ones_f if kind == "p" else cst_f[:, C_SCAN:C_SCAN + 128]

            def N1a():
                norm_a(x1, Rx)

            def N1b():
                norm_b(0, kind, hT, RhT[par])

            def A1():
                mm_group(7, pbs[7][0:16, 0:128], lambda kc: w_in_v[:, kc, CGZ:CGZ + 16], lambda kc: hT[:, kc, :], rdb("gz"))
                for j in range(4):
                    c0 = (CQ if j < 2 else CK) + (j % 2) * 128
                    mm_group(1, pbs[1][:, j * 128:(j + 1) * 128], lambda kc, c0=c0: w_in_v[:, kc, c0:c0 + 128],
                             lambda kc: hT[:, kc, :], rdb("qk"), last_signal=(j == 3))
                S.op("act", lambda e: e.activation(out=gz_c[:, :], in_=pbs[7][0:16, 0:128], func=AF.Copy),
                     reads=[Rpb[7]], writes=[Rgz])
                S.op("act", lambda e: e.activation(out=qk_c[:, :, :].rearrange("p a b -> p (a b)"), in_=pbs[1][:, :], func=AF.Copy),
                     reads=[Rpb[1]], writes=[Rqk])

            def A2():
                mm_group(2, pbs[2][:, :], lambda kc: hT[:, kc, :], lambda kc: w_in_v[:, kc, CV_:CV_ + 512], rdb("v"))
                mm_group(3, pbs[3][:, :], lambda kc: hT[:, kc, :], lambda kc: w_in_v[:, kc, CR:CR + 512], rdb("r"))
                S.op("act", lambda e: e.activation(out=v_c[:, :], in_=pbs[2][:, :], func=AF.Copy), reads=[Rpb[2]], writes=[Rv])
                S.op("act", lambda e: e.activation(out=er[:, :], in_=pbs[3][:, :], func=AF.Exp, scale=-1.0),
                     reads=[Rpb[3]], writes=[R["er"]])
                S.op("act", lambda e: e.activation(out=er[:, :], in_=er[:, :], func=AF.Ln, bias=1.0), reads=[R["er"]], writes=[R["er"]])
                S.op("act", lambda e: e.activation(out=er[:, :], in_=er[:, :], func=AF.Exp, scale=-1.0), reads=[R["er"]], writes=[R["er"]])
                S.op("dve", lambda e: e.tensor_tensor(out=sr_c[:, :], in0=pbs[3][:, :], in1=er[:, :], op=ALU.mult),
                     reads=[Rpb[3], R["er"]], writes=[Rsr])
                S.op("pool", lambda e: e.tensor_tensor(out=sr_c[:, :], in0=sr_c[:, :], in1=gg_row[:, :], op=ALU.mult),
                     reads=[Rsr, R["gg_row"]], writes=[Rsr])

            if kind == "p":
                Ru = R["u_p"]

                def uview(cc, j):
                    return u_p[:, cc, j:j + 128]

                def zview(cc):
                    return zc[:, cc, :]
            else:
                Ru = R["u_s"]

                def uview(cc, j):
                    return u_s[:, cc, :, j:j + 8]

                def zview(cc):
                    return zc[:, cc, :].rearrange("p (s t) -> p s t", s=16)

            def A3p():
                for bank, cbase, blk in ((6, CH, "hin"), (5, CC, "C")):
                    for j in range(4):
                        mm_group(bank, pbs[bank][:, j * 128:(j + 1) * 128],
                                 lambda kc, c0=cbase + j * 128: w_in_v[:, kc, c0:c0 + 128], lambda kc: hT[:, kc, :], rdb(blk),
                                 last_signal=(j == 3))

            def A3e():
                S.op("act", lambda e: e.activation(out=hin_sb[:, :, :].rearrange("p a b -> p (a b)"), in_=pbs[6][:, :], func=AF.Copy),
                     reads=[Rpb[6]], writes=[R["hin_sb"]])
                if kind == "p":
                    S.op("dve", lambda e: e.tensor_tensor(out=u_p[:, :, 2:130], in0=pbs[5][:, :].rearrange("p (c t) -> p c t", c=4),
                                                          in1=hin_sb[:, :, :], op=ALU.mult),
                         reads=[Rpb[5], R["hin_sb"]], writes=[Ru])
                else:
                    for cc in range(4):
                        S.op("dve", lambda e, cc=cc: e.tensor_tensor(
                            out=u_s[:, cc, :, 2:10], in0=pbs[5][:, cc * 128:(cc + 1) * 128].rearrange("p (s t) -> p s t", s=16),
                            in1=hin_sb[:, cc, :].rearrange("p (s t) -> p s t", s=16), op=ALU.mult),
                            reads=[Rpb[5], R["hin_sb"]], writes=[Ru])
                if kind == "p":
                    def uall(j):
                        return u_p[:, :, j:j + 128]

                    def zall(t):
                        return t[:, :, :]

                    def wbc(j):
                        return V(wc[:, j, 0:1], [[1, 4], [0, 128]])
                    S.op("pool", lambda e: e.tensor_tensor(out=zall(za), in0=uall(0), in1=wbc(0), op=ALU.mult),
                         reads=[Ru, R["wc"]], writes=[R["za"]])
                    S.op("pool", lambda e: e.tensor_tensor(out=zall(zb), in0=uall(1), in1=wbc(1), op=ALU.mult),
                         reads=[Ru, R["wc"]], writes=[R["zb"]])
                    S.op("dve", lambda e: e.tensor_tensor(out=zall(zc), in0=uall(2), in1=wbc(2), op=ALU.mult),
                         reads=[Ru, R["wc"]], writes=[R["zc"]])
                else:
                    for cc in range(4):
                        for j, (zt, eng) in enumerate(((za, "pool"), (zb, "pool"), (zc, "dve"))):
                            S.op(eng, lambda e, cc=cc, j=j, zt=zt: e.tensor_tensor(
                                out=zt[:, cc, :].rearrange("p (s t) -> p s t", s=16), in0=uview(cc, j),
                                in1=V(wc[:, j, cc:cc + 1], [[0, 16], [0, 8]]), op=ALU.mult),
                                reads=[Ru, R["wc"]], writes=[R[("za", "zb", "zc")[j]]])
                S.op("dve", lambda e: e.tensor_tensor(out=zc[:, :, :], in0=zc[:, :, :], in1=za[:, :, :], op=ALU.add),
                     reads=[R["zc"], R["za"]], writes=[R["zc"]])
                S.op("dve", lambda e: e.tensor_tensor(out=zc[:, :, :], in0=zc[:, :, :], in1=zb[:, :, :], op=ALU.add),
                     reads=[R["zc"], R["zb"]], writes=[R["zc"]])

            def A4():
                for j in range(4):
                    mm_group(4, pbs[4][:, j * 128:(j + 1) * 128],
                             lambda kc, c0=CB + j * 128: w_in_v[:, kc, c0:c0 + 128], lambda kc: hT[:, kc, :], rdb("B"),
                             last_signal=(j == 3))
                S.op("dve", lambda e: e.tensor_tensor(out=mix_c[:, 4:8, :], in0=pbs[4][:, :].rearrange("p (c t) -> p c t", c=4),
                                                      in1=zc[:, :, :], op=ALU.mult),
                     reads=[Rpb[4], R["zc"]], writes=[Rmix])
                if kind == "p":
                    if last_prompt:
                        for cc in range(4):
                            S.op("pe", lambda e, cc=cc: e.transpose(out=pbs[6][0:2, cc * 128:(cc + 1) * 128], in_=u_p[:, cc, 128:130],
                                                                    identity=ident_f),
                                 reads=[Ru, R["cst_f"]], writes=[Rpb[6]], signal=(cc == 3))
                        S.op("act", lambda e: e.activation(out=cn_sb[0:2, :], in_=pbs[6][0:2, :], func=AF.Copy),
                             reads=[Rpb[6]], writes=[R["sc_tok"]])
                        S.dma("sp", lambda e: e.dma_start(out=convp, in_=cn_sb[0:2, :]), reads=[R["sc_tok"]])
                    else:
                        S.op("pool", lambda e: e.tensor_copy(out=u_p[:, :, 0:2], in_=u_p[:, :, 128:130]), reads=[Ru], writes=[Ru])
                else:
                    S.op("pool", lambda e: e.tensor_copy(out=cn_T[:, :, :].rearrange("p c (s j) -> p c s j", s=16),
                                                         in_=u_s[:, :, :, 8:10]), reads=[Ru], writes=[R["cn_T"]])
                    for cc in range(4):
                        S.op("pe", lambda e, cc=cc: e.transpose(out=pbs[6][0:32, cc * 128:(cc + 1) * 128], in_=cn_T[:, cc, :],
                                                                identity=ident_f),
                             reads=[R["cn_T"], R["cst_f"]], writes=[Rpb[6]], signal=(cc == 3))
                    S.op("act", lambda e: e.activation(out=cn_sb[:, :], in_=pbs[6][0:32, :], func=AF.Copy),
                         reads=[Rpb[6]], writes=[R["sc_tok"]])
                    S.dma("sp", lambda e: e.dma_start(out=convs, in_=cn_sb[:, :]), reads=[R["sc_tok"]])

            def B0():
                for c in range(2):
                    S.op("pe", lambda e, c=c: e.matmul(pbs[7][:, 128 + c * 128:256 + c * 128], lhsT=wg_sb[0:16, c * 128:(c + 1) * 128],
                                                       rhs=gz_c[0:16, :], start=True, stop=True),
                         reads=[R["wg_sb"], Rgz], writes=[Rpb[7]], signal=(c == 1))
                for c in range(2):
                    S.op("act", lambda e, c=c: e.activation(out=el[:, c, :], in_=pbs[7][:, 128 + c * 128:256 + c * 128], func=AF.Exp,
                                                            scale=-1.0, bias=nbg[:, c:c + 1]),
                         reads=[Rpb[7], R["nbg"]], writes=[R["el"]])
                el2 = el[:, :, :].rearrange("p a b -> p (a b)")
                S.op("act", lambda e: e.activation(out=el2, in_=el2, func=AF.Ln, bias=1.0), reads=[R["el"]], writes=[R["el"]])
                for c in range(2):
                    S.op("dve", lambda e, c=c: e.tensor_tensor_scan(out=cum[:, c, :], data0=scan0, data1=el[:, c, :], initial=0.0,
                                                                    op0=ALU.mult, op1=ALU.add),
                         reads=[R["el"], R["cst_f"]], writes=[R["cum"]])
                cum2 = cum[:, :, :].rearrange("p a b -> p (a b)")
                S.op("act", lambda e: e.activation(out=eq[:, :, :].rearrange("p a b -> p (a b)"), in_=cum2, func=AF.Exp, scale=-1.0 / 16),
                     reads=[R["cum"]], writes=[R["eq"]])
                S.op("act", lambda e: e.activation(out=ek[:, :, :].rearrange("p a b -> p (a b)"), in_=cum2, func=AF.Exp, scale=1.0 / 16),
                     reads=[R["cum"]], writes=[R["ek"]])
                for hh in range(2):
                    ps_ = slice(hh * 64, (hh + 1) * 64)
                    S.op("dve", lambda e, hh=hh, ps_=ps_: e.scalar_tensor_tensor(
                        out=qinz[hh][ps_, :, :].rearrange("p a b -> p (a b)"),
                        in0=qk_c[ps_, 0:2, :].rearrange("p a b -> p (a b)"), scalar=0.125,
                        in1=eq[ps_, :, :].rearrange("p a b -> p (a b)"), op0=ALU.mult, op1=ALU.mult),
                        reads=[Rqk, R["eq"]], writes=[R[f"qinz{hh}"]])
                S.op("dve", lambda e: e.tensor_tensor(out=kin[:, :, :].rearrange("p a b -> p (a b)"),
                                                      in0=qk_c[:, 2:4, :].rearrange("p a b -> p (a b)"),
                                                      in1=ek[:, :, :].rearrange("p a b -> p (a b)"), op=ALU.mult),
                     reads=[Rqk, R["ek"]], writes=[R["kin"]])

            def B1():
                for c in range(2):
                    S.op("pe", lambda e, c=c: e.transpose(out=pb5b[:, c * 128:(c + 1) * 128], in_=kin[:, c, :], identity=ident_b),
                         reads=[R["kin"], R["cst_b"]], writes=[Rpb[5]], signal=(c == 1))
                S.op("act", lambda e: e.activation(out=kin_tok[:, :], in_=pb5b[:, 0:256], func=AF.Copy),
                     reads=[Rpb[5]], writes=[R["kin_tok"]])
                for h in range(4):
                    c, hh = h // 2, h % 2
                    S.op("pe", lambda e, h=h, c=c, hh=hh: e.matmul(pbs[1][:, h * 128:(h + 1) * 128],
                                                                   lhsT=kin[:, c, :], rhs=qinz[hh][:, c, :], start=True, stop=True),
                         reads=[R["kin"], R[f"qinz{hh}"]], writes=[Rpb[1]], signal=(h == 3))
                S.op("dve", lambda e: e.tensor_tensor(out=attm[:, :, :], in0=pbs[1][:, :].rearrange("p (h t) -> p h t", h=4),
                                                      in1=V(cst_f[:, mcol:mcol + 1], [[0, 4], [1, 128]]), op=ALU.mult),
                     reads=[Rpb[1], R["cst_f"]], writes=[R["attm"]])
                if kind == "s":
                    for g in range(4):
                        stg, Rstg = stages[g % 2], Rstages[g % 2]
                        S.dma("sp", lambda e, g=g, stg=stg: e.dma_start(
                            out=stg, in_=sgla[g * 4:(g + 1) * 4].rearrange("s (c hh) d v -> (hh d) s c v", hh=2)),
                            writes=[Rstg])
                        S.op("act", lambda e, g=g, stg=stg: e.activation(out=Ssbf[g // 2][:, (g % 2) * 4:(g % 2) * 4 + 4, :, :], in_=stg,
                                                                         func=AF.Copy),
                             reads=[Rstg], writes=[RSsbf[g // 2]])
                    for i in range(2):
                        S.op("pool", lambda e, i=i: e.memset(qmz[i], 0.0), writes=[Rqmz[i]])

            def B2():
                for c in range(2):
                    for hh in range(2):
                        h = 2 * c + hh
                        ps_ = slice(hh * 64, (hh + 1) * 64)
                        if kind == "s":
                            S.op("dve", lambda e, c=c, hh=hh, ps_=ps_: e.tensor_tensor(
                                out=qmz[hh][ps_, :, :], in0=V(qinz[hh][ps_, c, 0:1], [[0, 16], [1, 128]]),
                                in1=cst_b[ps_, CB_SM:CB_SM + 2048].rearrange("p (s t) -> p s t", s=16), op=ALU.mult),
                                reads=[R[f"qinz{hh}"], R["cst_b"]], writes=[Rqmz[hh]])
                        ocol = pbs[2][:, h * 128:(h + 1) * 128]
                        S.op("pe", lambda e, h=h, ocol=ocol: e.matmul(ocol, lhsT=attm[:, h, :], rhs=v_c[:, h * 128:(h + 1) * 128],
                                                                      start=True, stop=False),
                             reads=[R["attm"], Rv], writes=[Rpb[2]], signal=False)
                        if kind == "p":
                            S.op("pe", lambda e, c=c, hh=hh, ocol=ocol: e.matmul(ocol, lhsT=qinz[hh][:, c, :], rhs=S_bf[:, c, :],
                                                                                start=False, stop=True),
                                 reads=[R[f"qinz{hh}"], R["S_bf"]], writes=[Rpb[2]], signal=True)
                        else:
                            for q in range(16):
                                S.op("pe", lambda e, c=c, hh=hh, q=q, ocol=ocol: e.matmul(
                                    ocol, lhsT=qmz[hh][:, q, :], rhs=Ssbf[q // 8][:, q % 8, c, :], start=False, stop=(q == 15)),
                                    reads=[Rqmz[hh], RSsbf[q // 8]], writes=[Rpb[2]], signal=(q == 15))
                for h in range(4):
                    S.op("act", lambda e, h=h: e.activation(out=og[:, h * 128:(h + 1) * 128], in_=pbs[2][:, h * 128:(h + 1) * 128],
                                                            func=AF.Square, accum_out=ss4[:, h:h + 1]),
                         reads=[Rpb[2]], writes=[R["og"], R["ss4"]])
                S.op("act", lambda e: e.activation(out=ss4[:, 4:8], in_=ss4[:, 0:4], func=AF.Ln, scale=1.0 / 128, bias=eps_ap),
                     reads=[R["ss4"], R["epsb"]], writes=[R["ss4"]])
                S.op("act", lambda e: e.activation(out=ss4[:, 8:12], in_=ss4[:, 4:8], func=AF.Exp, scale=-0.5),
                     reads=[R["ss4"]], writes=[R["ss4"]])
                S.op("dve", lambda e: e.tensor_tensor(out=zc[:, :, :], in0=pbs[2][:, :].rearrange("p (h v) -> p h v", h=4),
                                                      in1=V(ss4[:, 8:9], [[1, 4], [0, 128]]), op=ALU.mult),
                     reads=[Rpb[2], R["ss4"]], writes=[R["zc"]])
                S.op("dve", lambda e: e.tensor_tensor(out=og[:, :], in0=zc[:, :, :].rearrange("p a b -> p (a b)"), in1=sr_c[:, :],
                                                      op=ALU.mult),
                     reads=[R["zc"], Rsr], writes=[R["og"]])
            def B2t():
                for h in range(4):
                    S.op("pe", lambda e, h=h: e.transpose(out=pb6b[:, h * 128:(h + 1) * 128], in_=og[:, h * 128:(h + 1) * 128],
                                                          identity=ident_b),
                         reads=[R["og"], R["cst_b"]], writes=[Rpb[6]], signal=(h == 3))
                S.op("act", lambda e: e.activation(out=mix_c[:, 0:4, :].rearrange("p a b -> p (a b)"), in_=pb6b[:, 0:512], func=AF.Copy),
                     reads=[Rpb[6]], writes=[Rmix])

            def B3():
                if kind == "p":
                    for c in range(2):
                        S.op("pe", lambda e, c=c: e.matmul(pbs[3][:, c * 256:(c + 1) * 256], lhsT=kin_tok[:, c * 128:(c + 1) * 128],
                                                           rhs=v_c[:, c * 256:(c + 1) * 256], start=True, stop=True),
                             reads=[R["kin_tok"], Rv], writes=[Rpb[3]], signal=(c == 1))
                    for hh in range(2):
                        ps = slice(hh * 64, (hh + 1) * 64)
                        S.op("dve", lambda e, hh=hh, ps=ps: e.tensor_tensor(
                            out=S_p[ps, :, :], in0=V(pbs[3][ps, hh * 128:hh * 128 + 1], [[256, 2], [1, 128]]),
                            in1=S_p[ps, :, :], op=ALU.add),
                            reads=[Rpb[3], R["S_p"]], writes=[R["S_p"]])
                    S.op("dve", lambda e: e.tensor_tensor(out=S_p[:, :, :], in0=S_p[:, :, :],
                                                          in1=V(eq[:, 0, 127:128], [[128, 2], [0, 128]]), op=ALU.mult),
                         reads=[R["S_p"], R["eq"]], writes=[R["S_p"]])
                    S.op("pool", lambda e: e.tensor_copy(out=S_bf[:, :, :], in_=S_p[:, :, :]), reads=[R["S_p"]], writes=[R["S_bf"]])
                    if last_prompt:
                        for hh in range(2):
                            S.dma("sp", lambda e, hh=hh: e.dma_start(
                                out=glap.rearrange("(c hh) d v -> hh d c v", hh=2)[hh], in_=S_p[hh * 64:(hh + 1) * 64, :, :]),
                                reads=[R["S_p"]])
                else:
                    def stage_in(g):
                        stg, Rstg = stages[g % 2], Rstages[g % 2]
                        S.dma("sp", lambda e, g=g, stg=stg: e.dma_start(
                            out=stg, in_=sgla[g * 4:(g + 1) * 4].rearrange("s (c hh) d v -> (hh d) s c v", hh=2)),
                            writes=[Rstg])
                    stage_in(0)
                    for g in range(4):
                        stg, Rstg = stages[g % 2], Rstages[g % 2]
                        if g + 1 < 4:
                            stage_in(g + 1)
                        S.op("dve", lambda e, g=g: e.tensor_tensor(
                            out=km, in0=V(kin_tok[:, 0:1], [[0, 4], [1, 256]]),
                            in1=V(cst_f[:, C_SMT + 4 * g:C_SMT + 4 * g + 1], [[1, 4], [0, 256]]), op=ALU.mult),
                            reads=[R["kin_tok"], R["cst_f"]], writes=Rtm)
                        for j in range(4):
                            bank = 3 + (j % 2) * 3
                            for c in range(2):
                                S.op("pe", lambda e, j=j, c=c, bank=bank: e.matmul(
                                    pbs[bank][:, c * 256:(c + 1) * 256], lhsT=km[:, j, c * 128:(c + 1) * 128],
                                    rhs=v_c[:, c * 256:(c + 1) * 256], start=True, stop=True),
                                    reads=Rtm + [Rv], writes=[Rpb[bank]], signal=(c == 1))
                            for hh in range(2):
                                ps = slice(hh * 64, (hh + 1) * 64)
                                S.op("dve", lambda e, j=j, hh=hh, ps=ps, bank=bank, stg=stg: e.tensor_tensor(
                                    out=stg[ps, j, :, :], in0=V(pbs[bank][ps, hh * 128:hh * 128 + 1], [[256, 2], [1, 128]]),
                                    in1=stg[ps, j, :, :], op=ALU.add),
                                    reads=[Rpb[bank], Rstg], writes=[Rstg])
                        S.op("dve", lambda e, g=g, stg=stg: e.tensor_tensor(
                            out=stg, in0=stg, in1=V(eq[:, 0, 32 * g + 7:32 * g + 8], [[8, 4], [128, 2], [0, 128]]), op=ALU.mult),
                            reads=[Rstg, R["eq"]], writes=[Rstg])
                        S.dma("sp", lambda e, g=g, stg=stg: e.dma_start(
                            out=glas[g * 4:(g + 1) * 4].rearrange("s (c hh) d v -> (hh d) s c v", hh=2), in_=stg),
                            reads=[Rstg])

            def B4():
                obank = (1, 7)
                for half in range(2):
                    mm_group(obank[half], pbs[obank[half]][:, :], lambda kc: mix_c[:, kc, :],
                             lambda kc, half=half: w_out_v[:, kc, half * 512:(half + 1) * 512], [Rmix, Rwout])
                for half in range(2):
                    S.op("dve", lambda e, half=half: e.tensor_tensor(out=tmpm[:, half * 512:(half + 1) * 512],
                                                                     in0=pbs[obank[half]][:, :],
                                                                     in1=Grow[:, gi, half * 512:(half + 1) * 512], op=ALU.mult),
                         reads=[Rpb[obank[half]], RG[gi]], writes=[Rtm[half]])
                    S.op("pool", lambda e, half=half: e.tensor_tensor(out=x1[:, half * 512:(half + 1) * 512],
                                                                      in0=x1[:, half * 512:(half + 1) * 512],
                                                                      in1=tmpm[:, half * 512:(half + 1) * 512], op=ALU.add),
                         reads=[Rx, Rtm[half]], writes=[Rx])

            def N2a():
                norm_a(x1, Rx, 1)

            def N2b():
                norm_b(2, kind, h2T_all[:, :, slot * 128:(slot + 1) * 128], Rh2[slot])

            return dict(N1a=N1a, N1b=N1b, A1=A1, A2=A2, A3p=A3p, A3e=A3e, A4=A4, B0=B0, B1=B1, B2=B2, B2t=B2t, B3=B3, B4=B4, N2a=N2a, N2b=N2b)

        def load_ring(e_idx, rs, first):
            if e_idx == NE - 1:
                w_u, w_d, r_up, r_dn = Rarena_up, Rarena_dn, ring_up_last, ring_dn_last
            else:
                w_u = [Rring_up[rs]] + (Rwin_r if first else [])
                w_d, r_up, r_dn = [Rring_dn[rs]], ring_up[rs], ring_dn[rs]
            S.dma("pool", lambda e: e.dma_start(out=r_up[:, :, :],
                                               in_=w_up.rearrange("(k p) n -> p k n", p=128)[:, :, e_idx * ESZ:(e_idx + 1) * ESZ]),
                  writes=w_u)
            S.dma("pool", lambda e: e.dma_start(out=r_dn[:, :, :],
                                               in_=w_down[e_idx * ESZ:(e_idx + 1) * ESZ, :].rearrange("(k p) n -> p k n", p=128)),
                  writes=w_d)

        ring_ctr = [0]
        ab_ctr = [0]
        pre_loaded = [False]

        def pre_phase2():
            base = ring_ctr[0]
            load_ring(0, base % 3, True)
            load_ring(1, (base + 1) % 3, False)
            pre_loaded[0] = True

        def phase2(slots_info, early_reload=False, after_first_up=None, next_x=None):
            nsl = len(slots_info)
            gsz = 3 if nsl % 4 == 1 else 4
            sts = [list(range(i, min(i + gsz, nsl))) for i in range(0, nsl, gsz)]
            base = ring_ctr[0]
            if not pre_loaded[0]:
                load_ring(0, base % 3, True)
                load_ring(1, (base + 1) % 3, False)
            pre_loaded[0] = False
            units = [(e_idx, st) for e_idx in range(NE) for st in sts]

            def up_part(e_idx, st, ab):
                rs = (base + e_idx) % 3
                T = len(st) * 128
                tok0 = st[0] * 128
                for fc in range(4):
                    r_up, Rr = (ring_up_last, Rarena_up) if e_idx == NE - 1 else (ring_up[rs], [Rring_up[rs]])
                    mm_group(fc, pbs[fc][:, 0:T], lambda kc, fc=fc, r_up=r_up: r_up[:, kc, fc * 128:(fc + 1) * 128],
                             lambda kc: h2T_all[:, kc, tok0:tok0 + T], Rr + [Rh2[s_] for s_ in st])
                for fc in range(4):
                    rbuf, Rrb = (er, R["er"]) if fc % 2 == 0 else (sr[0], R["sr0"])
                    S.op("act", lambda e, fc=fc, rbuf=rbuf, T=T: e.activation(out=rbuf[:, 0:T], in_=pbs[fc][:, 0:T], func=AF.Relu),
                         reads=[Rpb[fc]], writes=[Rrb])
                    S.op("dve", lambda e, fc=fc, rbuf=rbuf, ab=ab, T=T: e.scalar_tensor_tensor(
                        out=aT[ab][:, fc, 0:T], in0=pbs[fc][:, 0:T], scalar=0.0, in1=rbuf[:, 0:T], op0=ALU.max, op1=ALU.mult),
                        reads=[Rpb[fc], Rrb], writes=RaT[ab])

            def down_part(e_idx, st, ab):
                rs = (base + e_idx) % 3
                finals = []
                for si, sidx in enumerate(st):
                    slot, kind, out_ap = slots_info[sidx]
                    gi = 2 if kind == "p" else 3
                    x1 = x1_all[:, slot, :]
                    for half in range(2):
                        bank = 4 + half + 2 * (si % 2)
                        mm_group(bank, pbs[bank][:, :], lambda kc, si=si, ab=ab: aT[ab][:, kc, si * 128:(si + 1) * 128],
                                 lambda kc, half=half: (ring_dn_last if e_idx == NE - 1 else ring_dn[rs])[:, kc, half * 512:(half + 1) * 512],
                                 RaT[ab] + (Rarena_dn if e_idx == NE - 1 else [Rring_dn[rs]]), nk=4)
                    for half in range(2):
                        bank = 4 + half + 2 * (si % 2)
                        if si % 2 == 0:
                            tbuf, Rtb = tmpm[:, half * 512:(half + 1) * 512], Rtm[half]
                        else:
                            tz = za if half == 0 else zb
                            tbuf, Rtb = tz[:, :, :].rearrange("p a b -> p (a b)"), R["za" if half == 0 else "zb"]
                        S.op("dve", lambda e, half=half, bank=bank, gi=gi, tbuf=tbuf: e.tensor_tensor(
                            out=tbuf, in0=pbs[bank][:, :],
                            in1=Grow[:, gi, half * 512:(half + 1) * 512], op=ALU.mult),
                            reads=[Rpb[bank], RG[gi]], writes=[Rtb])
                        S.op("pool", lambda e, half=half, x1=x1, tbuf=tbuf: e.tensor_tensor(
                            out=x1[:, half * 512:(half + 1) * 512], in0=x1[:, half * 512:(half + 1) * 512],
                            in1=tbuf, op=ALU.add),
                            reads=[Rx1[slot], Rtb], writes=[Rx1[slot]])
                    if e_idx == NE - 1:
                        def fin(slot=slot, x1=x1, out_ap=out_ap, si=si):
                            rs_ap, Rss = rstd_from(x1, Rx1[slot], 1.0 / D, si % 4)
                            S.op("dve", lambda e: e.scalar_tensor_tensor(out=x1, in0=x1, scalar=rs_ap, in1=fg_row[:, :],
                                                                         op0=ALU.mult, op1=ALU.mult),
                                 reads=[Rx1[slot], Rss, R["fg_row"]], writes=[Rx1[slot]])
                            S.dma("sp", lambda e: e.dma_start(out=out_ap, in_=x1), reads=[Rx1[slot]])
                            if next_x is not None and slot in next_x:
                                nsrc = next_x[slot]
                                S.dma("sp", lambda e: e.dma_start(out=x1_all[:, slot, :], in_=nsrc), writes=[Rx1[slot]])
                        finals.append(fin)
                for f_ in finals:
                    f_()

            abs_ = []
            for u, (e_idx, st) in enumerate(units):
                ab = ab_ctr[0] % 2
                ab_ctr[0] += 1
                abs_.append(ab)
                up_part(e_idx, st, ab)
                if u == 0 and after_first_up is not None:
                    after_first_up()
                if u > 0:
                    pe_, pst = units[u - 1]
                    down_part(pe_, pst, abs_[u - 1])
                if st is sts[0] and e_idx + 2 < NE:
                    load_ring(e_idx + 2, (base + e_idx + 2) % 3, False)
                if early_reload and st is sts[0] and e_idx == NE - 1:
                    for i_, blk in enumerate(WORDER):
                        c0 = WBLK[blk]
                        wd = 16 if blk == "gz" else 512
                        S.dma("pool", lambda e, c0=c0, wd=wd: e.dma_start(out=w_in_v[:, :, c0:c0 + wd], in_=w_in_kv[:, :, c0:c0 + wd],
                                                                       allow_slow_non_contiguous=True),
                              writes=[Rwin_c[blk]] + (Rring if i_ == 0 else []))
            le, lst = units[-1]
            down_part(le, lst, abs_[-1])
            ring_ctr[0] += NE

        res("out")
        S.stage(5)
        for cc in range(4):
            S.op("pe", lambda e, cc=cc: e.transpose(out=pbs[6][:, cc * 32:(cc + 1) * 32], in_=sc_tok[:, cc * 128:(cc + 1) * 128],
                                                    identity=ident_f[0:32, 0:32]),
                 reads=[R["sc_tok"], R["cst_f"]], writes=[Rpb[6]], signal=(cc == 3))
        S.op("act", lambda e: e.activation(out=u_s[:, :, :, 0:2], in_=pbs[6][:, 0:128].rearrange("p (c s j) -> p c s j", c=4, s=16),
                                           func=AF.Copy),
             reads=[Rpb[6]], writes=[R["u_s"]])

        w_in_kv = w_in.rearrange("(k p) n -> p k n", p=128)
        WORDER = ("gz", "qk", "v", "r", "hin", "C", "B")

        w_stage_rr = [0]

        def load_w_block(blk):
            c0 = WBLK[blk]
            if blk == "gz":
                S.dma("pool", lambda e, c0=c0: e.dma_start(out=w_in_v[:, :, c0:c0 + 16], in_=w_in_kv[:, :, c0:c0 + 16],
                                                           allow_slow_non_contiguous=True),
                      writes=[Rwin_c[blk]] + Rring + Rwada_rb)
                return
            stg_ap, stg_R = w_stage[w_stage_rr[0] % 2]
            w_stage_rr[0] += 1
            stg3 = stg_ap.rearrange("p (k n) -> p k n", k=8)
            S.dma("sp", lambda e, c0=c0, stg3=stg3: e.dma_start(out=stg3, in_=w_in_kv[:, :, c0:c0 + 512]), writes=stg_R)
            cast(w_in_v[:, :, c0:c0 + 512], stg3, stg_R, [Rwin_c[blk]] + Rring + Rwada_rb)

        def load_w_out():
            wo_v = w_out.rearrange("(k p) n -> p k n", p=128)
            S.dma("pool", lambda e: e.dma_start(out=w_out_v[:, :, :], in_=wo_v), writes=[Rwout])

        passes = [
            [(0, "s", None)] + [(1 + i, "p", i) for i in range(8)],
            [(i, "p", 8 + i) for i in range(8)],
        ]
        for pi, tiles in enumerate(passes):
            S.stage(6 + 100 * pi)
            def xload(slot, kind, pt):
                src = xs if kind == "s" else xp[pt * 128:(pt + 1) * 128, :]
                S.dma("sp", lambda e: e.dma_start(out=x1_all[:, slot, :], in_=src), writes=[Rx1[slot]])
            if pi == 0:
                xload(*tiles[0])
            NPF = 3
            steps = [phase1_steps(slot, kind, pt, (kind == "p" and pt == 15), k % 2) for k, (slot, kind, pt) in enumerate(tiles)]
            nt = len(tiles)
            ND = 3 if pi == 0 else 0

            def call(k, name):
                if 0 <= k < nt:
                    steps[k][name]()
            if pi == 0:
                call(0, "N1a"); call(0, "N1b")
                load_w_block("gz"); load_w_block("qk")
                call(0, "A1")
                load_w_block("v"); load_w_block("r")
                call(0, "A2")
                load_w_block("hin"); load_w_block("C")
                call(0, "A3p"); call(0, "A3e")
                load_w_block("B")
                call(0, "A4")
                load_w_out()
                for k_ in range(1, min(NPF, len(tiles))):
                    xload(*tiles[k_])
            else:
                for nm in ("N1a", "N1b", "A1", "A2", "A3p", "A3e", "A4"):
                    call(0, nm)
            if pi == 0:
                late_start()
            call(1, "N1a")
            call(1, "N1b")
            for k in range(nt):
                if pi == 0 and k == 0:
                    pass
                elif pi == 0 and k == 1:
                    xload(*tiles[3])
                    xload(*tiles[4])
                elif k + NPF < nt:
                    xload(*tiles[k + NPF])
                S.stage(10 + 100 * pi + k)

                def hook(sub):
                    if pi == 0:
                        for i_ in late_hooks.get((k, sub), []):
                            late_step(i_)
                call(k + 1, "A1"); call(k, "B0"); call(k - 1 - ND, "N2b")
                hook(1)
                call(k + 1, "A2"); call(k, "B1"); call(k + 2, "N1a")
                hook(2)
                call(k + 1, "A3p"); call(k + 2, "N1b"); call(k, "B2"); call(k + 1, "A3e")
                hook(3)
                call(k + 1, "A4")
                if k == nt - 2:
                    pre_phase2()
                call(k, "B2t"); call(k, "B3")
                hook(4)
                call(k, "B4")
                call(k - ND, "N2a")
                hook(5)
            def flush_norm2(nt=nt, ND=ND, call=call):
                call(nt - 1 - ND, "N2b")
                for kk in range(nt - ND, nt):
                    call(kk, "N2a")
                    call(kk, "N2b")
            info = []
            for (slot, kind, pt) in tiles:
                out_ap = ys if kind == "s" else yp[pt * 128:(pt + 1) * 128, :]
                info.append((slot, kind, out_ap))
            S.stage(50 + 100 * pi)
            nx = None
            if pi == 0:
                nx = {sl_: xp[pt_ * 128:(pt_ + 1) * 128, :] for (sl_, kd_, pt_) in passes[1][:NPF]}
            phase2(info, early_reload=(pi == 0), after_first_up=flush_norm2, next_x=nx)

        S.wait_all("sp", [(k, v) for k, v in S.count.items() if k.startswith("dma_sp") and v > 0])
        S.emit(block)
    return nc


_CACHE = {}


def kernel(x_prompt, x_sample, state_gla, state_conv, c_prompt, c_sample, w_ada, b_ada, norm1_g, w_in,
           w_gate_up, b_gate, gla_norm_g, w_conv, w_out, norm2_g, w_up, w_down, final_g):
    f = lambda a: np.ascontiguousarray(np.asarray(a, dtype=np.float32))
    x_prompt, x_sample, state_gla, state_conv = f(x_prompt), f(x_sample), f(state_gla), f(state_conv)
    c_prompt, c_sample = f(c_prompt), f(c_sample)
    if "nc" not in _CACHE:
        _CACHE["nc"] = build_program()
        _CACHE["consts"] = make_consts()
    nc = _CACHE["nc"]
    cf, cb = _CACHE["consts"]
    shared = {
        "w_ada": f(w_ada)[0], "b_ada": f(b_ada).reshape(1, -1), "norm1_g": f(norm1_g).reshape(1, -1),
        "w_in": f(w_in)[0], "w_gate_up": f(w_gate_up)[0], "b_gate": f(b_gate).reshape(1, -1),
        "gla_norm_g": f(gla_norm_g).reshape(1, -1), "w_conv": f(w_conv)[0], "w_out": f(w_out)[0],
        "norm2_g": f(norm2_g).reshape(1, -1), "w_up": f(w_up)[0], "w_down": f(w_down)[0],
        "final_g": f(final_g).reshape(1, -1), "consts_f": cf, "consts_b": cb,
    }
    in_maps = []
    for c in range(NCORES):
        m = dict(shared)
        m["xp"] = x_prompt[c]
        m["xs"] = np.ascontiguousarray(x_sample[16 * c:16 * c + 16].reshape(128, D))
        m["sgla"] = np.ascontiguousarray(state_gla[0, 16 * c:16 * c + 16])
        m["sconv"] = np.ascontiguousarray(state_conv[0, 16 * c:16 * c + 16].reshape(32, 512))
        m["cvec"] = np.ascontiguousarray(np.concatenate([c_prompt[c:c + 1], c_sample[16 * c:16 * c + 16]], axis=0))
        in_maps.append(m)
    res = run_bass_kernel_spmd(nc, in_maps, core_ids=list(range(NCORES)))
    rs = res.results
    y_prompt = np.stack([np.asarray(r["yp"]) for r in rs], axis=0).astype(np.float32)
    y_sample = np.concatenate([np.asarray(r["ys"]).reshape(16, 8, D) for r in rs], axis=0).astype(np.float32)
    gla_p = np.stack([np.asarray(r["glap"]) for r in rs], axis=0)[None].astype(np.float32)
    conv_p = np.stack([np.asarray(r["convp"]) for r in rs], axis=0)[None].astype(np.float32)
    gla_s = np.concatenate([np.asarray(r["glas"]) for r in rs], axis=0)[None].astype(np.float32)
    conv_s = np.concatenate([np.asarray(r["convs"]).reshape(16, 2, 512) for r in rs], axis=0)[None].astype(np.float32)
    return (y_prompt, y_sample, gla_p, conv_p, gla_s, conv_s)
```

```python
from contextlib import ExitStack
import numpy as np
import ml_dtypes
import concourse.bass as bass
import concourse.mybir as mybir
from concourse.bass_utils import run_bass_kernel_spmd

F32 = mybir.dt.float32
BF16 = mybir.dt.bfloat16
AF = mybir.ActivationFunctionType
ALU = mybir.AluOpType
EPS = 1e-6
NCORES = 8
D = 1024
DIN = 3088
DFF = 4096
NE = 8
ESZ = DFF // NE


class Res:
    __slots__ = ("name", "last_w", "readers")

    def __init__(self, name):
        self.name = name
        self.last_w = None
        self.readers = {}


class Sched:
    ENGS = ("pe", "act", "dve", "pool", "sp")

    def __init__(self, n_dma_sems=8):
        self.streams = {e: [] for e in self.ENGS}
        self.sems = {}
        self.count = {}
        self.known = {e: {} for e in self.ENGS}
        self.n_dma_sems = n_dma_sems
        self.dma_rr = {e: 0 for e in self.ENGS}
        self.sem_keys = ["pe", "act", "dve", "pool"]
        for q in ("sp", "pool"):
            for i in range(n_dma_sems):
                self.sem_keys.append(f"dma_{q}_{i}")
        for k in self.sem_keys:
            self.count[k] = 0
        self.dead = False
        self.limit = 10 ** 9

    def stage(self, n):
        if n >= self.limit:
            self.dead = True

    def _need(self, eng, waits, ev, war=False):
        if ev is None:
            return
        key, val = ev
        if key == eng and (eng == "pe" or (war and eng != "pool")):
            return
        if self.known[eng].get(key, 0) >= val:
            return
        if waits.get(key, 0) < val:
            waits[key] = val

    def _collect(self, eng, reads, writes):
        waits = {}
        for r in reads:
            self._need(eng, waits, r.last_w)
        for w in writes:
            self._need(eng, waits, w.last_w)
            for k, v in w.readers.items():
                self._need(eng, waits, (k, v), war=True)
        for k, v in waits.items():
            self.known[eng][k] = v
        return waits

    def op(self, eng, fn, reads=(), writes=(), signal=True):
        assert signal or eng == "pe"
        if self.dead:
            return None
        waits = self._collect(eng, reads, writes)
        if signal:
            self.count[eng] += 1
            ev = (eng, self.count[eng])
        else:
            ev = (eng, self.count[eng] + 1)
        for r in reads:
            if r.readers.get(ev[0], 0) < ev[1]:
                r.readers[ev[0]] = ev[1]
        for w in writes:
            w.last_w = ev
            w.readers = {}
        self.streams[eng].append((waits, fn, (eng, 1) if signal else None))
        return ev

    def dma(self, q, fn, reads=(), writes=()):
        if self.dead:
            return None
        i = self.dma_rr[q]
        self.dma_rr[q] = (i + 1) % self.n_dma_sems
        key = f"dma_{q}_{i}"
        waits = self._collect(q, reads, writes)
        prev = self.count[key]
        if prev > 0 and self.known[q].get(key, 0) < prev:
            waits[key] = prev
            self.known[q][key] = prev
        self.count[key] = prev + 16
        ev = (key, prev + 16)
        for r in reads:
            if r.readers.get(key, 0) < ev[1]:
                r.readers[key] = ev[1]
        for w in writes:
            w.last_w = ev
            w.readers = {}
        self.streams[q].append((waits, fn, (key, 16)))
        return ev

    def wait_all(self, eng, events):
        waits = {}
        for ev in events:
            self._need(eng, waits, ev)
        for k, v in waits.items():
            self.known[eng][k] = v
        self.streams[eng].append((waits, None, None))

    def emit(self, block):
        sems = self.sems

        def run(engname):
            def body(e):
                for waits, fn, inc in self.streams[engname]:
                    for k, v in waits.items():
                        e.wait_ge(sems[k], v)
                    if fn is None:
                        continue
                    ins = fn(e)
                    if inc is not None:
                        ins.then_inc(sems[inc[0]], inc[1])
            return body

        block.tensor(run("pe"))
        block.scalar(run("act"))
        block.vector(run("dve"))
        block.gpsimd(run("pool"))
        block.sync(run("sp"))


def V(apobj, dims):
    return bass.AP(apobj.tensor, apobj.offset, [list(apobj.ap[0])] + [list(d) for d in dims])


C_ID, C_MP, C_MS, C_SCAN, C_ONES, C_EP, C_ES, C_SMT = 0, 128, 256, 384, 512, 640, 768, 896
NCF = 912
CB_ID, CB_SM = 0, 128
NCB = 128 + 2048


def make_consts():
    cf = np.zeros((128, NCF), np.float32)
    s = np.arange(128)[:, None]
    t = np.arange(128)[None, :]
    cf[:, C_ID:C_ID + 128] = (s == t)
    cf[:, C_MP:C_MP + 128] = (s <= t)
    cf[:, C_MS:C_MS + 128] = (s <= t) & (s // 8 == t // 8)
    cf[:, C_SCAN:C_SCAN + 128] = (t % 8 != 0)
    cf[:, C_ONES:C_ONES + 128] = 1.0
    cf[0, C_EP:C_EP + 128] = 1.0
    for q in range(16):
        cf[1 + q, C_ES + 8 * q:C_ES + 8 * q + 8] = 1.0
    cf[:, C_SMT:C_SMT + 16] = (np.arange(128)[:, None] // 8 == np.arange(16)[None, :])
    cb = np.zeros((128, NCB), np.float32)
    cb[:, CB_ID:CB_ID + 128] = (s == t)
    sm = (np.arange(128)[None, :] // 8 == np.arange(16)[:, None]).astype(np.float32)
    cb[:, CB_SM:CB_SM + 2048] = sm.reshape(1, 2048)
    return cf, cb.astype(ml_dtypes.bfloat16)


def build_program(limit=10 ** 9):
    nc = bass.Bass("TRN2", target_bir_lowering=False)

    def din(name, shape, dt=F32):
        return nc.dram_tensor(name, list(shape), dt, kind="ExternalInput").ap()

    def dout(name, shape):
        return nc.dram_tensor(name, list(shape), F32, kind="ExternalOutput").ap()

    xp = din("xp", [2048, D])
    xs = din("xs", [128, D])
    sgla = din("sgla", [16, 4, 64, 128])
    sconv = din("sconv", [32, 512])
    cvec = din("cvec", [17, D])
    w_ada = din("w_ada", [D, 6 * D])
    b_ada = din("b_ada", [1, 6 * D])
    n1g = din("norm1_g", [1, D])
    w_in = din("w_in", [D, DIN])
    w_gu = din("w_gate_up", [16, 256])
    b_gate = din("b_gate", [1, 256])
    glag = din("gla_norm_g", [1, 512])
    w_conv = din("w_conv", [3, 512])
    w_out = din("w_out", [D, D])
    n2g = din("norm2_g", [1, D])
    w_up = din("w_up", [D, DFF])
    w_down = din("w_down", [DFF, D])
    fing = din("final_g", [1, D])
    cfd = din("consts_f", [128, NCF])
    cbd = din("consts_b", [128, NCB], BF16)

    yp = dout("yp", [2048, D])
    ys = dout("ys", [128, D])
    glap = dout("glap", [4, 64, 128])
    convp = dout("convp", [2, 512])
    glas = dout("glas", [16, 4, 64, 128])
    convs = dout("convs", [32, 512])

    S = Sched()
    S.limit = limit
    R = {}

    def res(name):
        R[name] = Res(name)
        return R[name]

    with ExitStack() as es:
        E = es.enter_context

        def sb(name, shape, dt=F32):
            res(name)
            return E(nc.sbuf_tensor(name, list(shape), dt))

        NSLOT = 9
        x1_all = E(nc.sbuf_tensor("x1_all", [128, NSLOT, D], F32))
        Rx1 = [res(f"x1_{i}") for i in range(NSLOT)]
        h2T_all = E(nc.sbuf_tensor("h2T_all", [128, 8, NSLOT * 128], BF16))
        Rh2 = [res(f"h2_{i}") for i in range(NSLOT)]
        NWB = 8 * DIN + 8 * D
        wbig = E(nc.sbuf_tensor("wbig", [128, NWB], BF16))
        Rwout = res("wout")
        WBLK = {"qk": 0, "v": 512, "gz": 1024, "r": 1040, "B": 1552, "C": 2064, "hin": 2576}
        Rwin_c = {nm: res(f"win_{nm}") for nm in WBLK}
        Rwin_r = list(Rwin_c.values())
        Rring_up = [res(f"ringu{i}") for i in range(3)]
        Rring_dn = [res(f"ringd{i}") for i in range(3)]
        Rring = Rring_up + Rring_dn
        w_in_v = wbig[:, 0:8 * DIN].rearrange("p (k n) -> p k n", k=8)
        w_out_v = wbig[:, 8 * DIN:NWB].rearrange("p (k n) -> p k n", k=8)
        ring_up = [wbig[:, s * 8192:s * 8192 + 4096].rearrange("p (k n) -> p k n", k=8) for s in range(3)]
        ring_dn = [wbig[:, s * 8192 + 4096:(s + 1) * 8192].rearrange("p (k n) -> p k n", k=4) for s in range(3)]

        cst_f = sb("cst_f", [128, NCF])
        cst_b = sb("cst_b", [128, NCB], BF16)
        wg_sb = sb("wg_sb", [16, 256])
        nbg = sb("nbg", [128, 2])
        wc = sb("wc", [128, 3, 4])
        ngT = sb("ngT", [128, 2, 8])
        modT = sb("modT", [128, 4, 8, 17])
        fg_row = sb("fg_row", [128, D])
        gg_row = sb("gg_row", [128, 512])
        Grow = E(nc.sbuf_tensor("Grow", [128, 4, D], F32))
        RG = [res(f"G{i}") for i in range(4)]
        S_p = sb("S_p", [128, 2, 128])
        S_bf = sb("S_bf", [128, 2, 128], BF16)
        u_p = sb("u_p", [128, 4, 130])
        u_s = sb("u_s", [128, 4, 16, 10])
        sc_tok = sb("sc_tok", [32, 512])
        cn_sb = sc_tok
        cn_T = sb("cn_T", [128, 4, 32])
        scT = sb("scT", [128, 8, 17], BF16)
        bch0 = sb("bch0", [17, 512])
        bch = [bch0, bch0]
        ss_bufs = [sb(f"ss{i}", [128, 4]) for i in range(4)]
        ss4 = sb("ss4", [128, 12])
        xs_bf = sb("xs_bf", [128, D], BF16)
        tmpm = E(nc.sbuf_tensor("tmpm", [128, D], F32))
        Rtm = [res("tmpm0"), res("tmpm1")]
        hTbuf = E(nc.sbuf_tensor("hTbuf", [128, 2, 8, 128], BF16))
        RhT = [res("hT0"), res("hT1")]
        hTs = [hTbuf[:, 0, :, :], hTbuf[:, 1, :, :]]
        arena = E(nc.sbuf_tensor("arena", [128, 4096], F32))
        Rarena = []
        arena_off = [0]

        def carve(name, shape, dt=F32):
            n = 1
            for d_ in shape[1:]:
                n *= d_
            ncol = n if dt == F32 else n // 2
            a0 = arena_off[0]
            arena_off[0] += ncol
            assert arena_off[0] <= 4096
            ap_ = arena[:, a0:a0 + ncol]
            if dt != F32:
                ap_ = ap_.bitcast(dt)
            if len(shape) == 3:
                ap_ = ap_.rearrange("p (a b) -> p a b", a=shape[1])
            Rarena.append(res(name))
            return ap_
        qk_sb = [carve(f"qk_sb{i}", [128, 4, 128]) for i in range(2)]
        gz_sb = [sb(f"gz_sb{i}", [16, 128]) for i in range(2)]
        v_bf = [carve(f"v_bf{i}", [128, 512], BF16) for i in range(2)]
        er = sb("er", [128, 512])
        sr = [sb(f"sr{i}", [128, 512]) for i in range(2)]
        hin_sb = carve("hin_sb", [128, 4, 128])
        zc = carve("zc", [128, 4, 128])
        mixbuf = E(nc.sbuf_tensor("mixbuf", [128, 2, 8, 128], BF16))
        res("mixT0"); res("mixT1")
        mixT = [mixbuf[:, 0, :, :], mixbuf[:, 1, :, :]]
        el = carve("el", [128, 2, 128])
        cum = carve("cum", [128, 2, 128])
        eq = carve("eq", [128, 2, 128])
        ek = carve("ek", [128, 2, 128])
        qinz = [sb(f"qinz{i}", [128, 2, 128], BF16) for i in range(2)]
        kin = sb("kin", [128, 2, 128], BF16)
        kin_tok = sb("kin_tok", [128, 256], BF16)
        attm = carve("attm", [128, 4, 128], BF16)
        og = carve("og", [128, 512], BF16)
        assert arena_off[0] == 4096
        assert len(Rarena) == 12
        Rarena_up = Rarena[0:5] + [res("ring_last_up")]
        Rarena_dn = Rarena[5:12] + [res("ring_last_dn")]
        ring_up_last = arena[:, 0:2048].bitcast(BF16).rearrange("p (k n) -> p k n", k=8)
        ring_dn_last = arena[:, 2048:4096].bitcast(BF16).rearrange("p (k n) -> p k n", k=4)
        aT = [hTbuf[:, :, :, :].rearrange("p a k t -> p (a k t)").rearrange("p (f t) -> p f t", f=4),
              mixbuf[:, :, :, :].rearrange("p a k t -> p (a k t)").rearrange("p (f t) -> p f t", f=4)]
        RaT = [RhT, [R["mixT0"], R["mixT1"]]]
        za = sb("za", [128, 4, 128])
        zb = sb("zb", [128, 4, 128])

        pbs = [E(nc.psum_tensor(f"pb{i}", [128, 512], F32)) for i in range(8)]
        Rpb = [res(f"pb{i}") for i in range(8)]
        pb0b = pbs[0][:, :].bitcast(BF16)
        pb5b = pbs[5][:, :].bitcast(BF16)
        pb6b = pbs[6][:, :].bitcast(BF16)

        for k in S.sem_keys:
            S.sems[k] = E(nc.semaphore(k))
        block = E(nc.Block())

        ident_f = cst_f[:, C_ID:C_ID + 128]
        ident_b = cst_b[:, CB_ID:CB_ID + 128]
        ones_f = cst_f[:, C_ONES:C_ONES + 128]

        def slot_bf(j):
            return x1_all[:, j, :].bitcast(BF16)
        wada_ring = [slot_bf(5).rearrange("p (k n) -> p k n", k=8), slot_bf(6).rearrange("p (k n) -> p k n", k=8)]
        Rwada = [Rx1[5], Rx1[6]]
        Ssbf = [slot_bf(5).rearrange("p (s c v) -> p s c v", s=8, c=2), slot_bf(6).rearrange("p (s c v) -> p s c v", s=8, c=2)]
        RSsbf = [Rx1[5], Rx1[6]]
        stages = [x1_all[:, 7, :].rearrange("p (s c v) -> p s c v", s=4, c=2),
                  x1_all[:, 3, :].rearrange("p (s c v) -> p s c v", s=4, c=2)]
        Rstages = [Rx1[7], Rx1[3]]
        qmz = [slot_bf(8)[:, 0:2048].rearrange("p (s t) -> p s t", s=16),
               slot_bf(4)[:, 0:2048].rearrange("p (s t) -> p s t", s=16)]
        Rqmz = [Rx1[8], Rx1[4]]
        km = tmpm[:, :].bitcast(BF16)[:, 0:1024].rearrange("p (s f) -> p s f", s=4)


        cast_rr = [0]

        def cast(out, in_, reads, writes):
            eng = ("dve", "act", "dve")[cast_rr[0] % 3]
            cast_rr[0] += 1
            if eng == "act":
                S.op("act", lambda e: e.activation(out=out, in_=in_, func=AF.Copy), reads=reads, writes=writes)
            else:
                S.op(eng, lambda e: e.tensor_copy(out=out, in_=in_), reads=reads, writes=writes)

        def stage2(a):
            return x1_all[:, a:a + 2, :].rearrange("p a b -> p (a b)")

        def stage4(a):
            return x1_all[:, a:a + 4, :].rearrange("p a b -> p (a b)")
        wada_stage = [(stage2(1), [Rx1[1], Rx1[2]]), (stage2(3), [Rx1[3], Rx1[4]]), (stage2(7), [Rx1[7], Rx1[8]])]
        w_stage = [(stage4(1), [Rx1[1], Rx1[2], Rx1[3], Rx1[4]]), (stage4(5), [Rx1[5], Rx1[6], Rx1[7], Rx1[8]])]

        def ld(out, in_, writes, q="sp", nonc=False):
            if nonc:
                S.dma(q, lambda e: e.dma_start(out=out, in_=in_, allow_slow_non_contiguous=True), writes=writes)
            else:
                S.dma(q, lambda e: e.dma_start(out=out, in_=in_), writes=writes)

        ld(cst_f[:, :], cfd, [R["cst_f"]])
        ld(cst_b[:, :], cbd, [R["cst_b"]])
        ld(tmpm[0:17, :], cvec, Rtm)
        ld(wg_sb[:, :], w_gu, [R["wg_sb"]])
        ld(nbg[:, :], b_gate.rearrange("o (c p) -> p (o c)", p=128), [R["nbg"]], nonc=True)
        ld(wc[:, :, :], w_conv.rearrange("j (c p) -> p j c", p=128), [R["wc"]], nonc=True)
        ld(ngT[:, 0, :], n1g.rearrange("o (c p) -> p (o c)", p=128), [R["ngT"]], nonc=True)
        ld(ngT[:, 1, :], n2g.rearrange("o (c p) -> p (o c)", p=128), [R["ngT"]], nonc=True)
        ld(fg_row[:, :], bass.AP(fing.tensor, fing.offset, [[0, 128], [1, D]]), [R["fg_row"]])
        ld(gg_row[:, :], bass.AP(glag.tensor, glag.offset, [[0, 128], [1, 512]]), [R["gg_row"]])
        ld(sc_tok[:, :], sconv, [R["sc_tok"]])

        S.stage(1)
        S.op("dve", lambda e: e.tensor_scalar(out=nbg[:, :], in0=nbg[:, :], scalar1=-1.0, scalar2=None, op0=ALU.mult),
             reads=[R["nbg"]], writes=[R["nbg"]])
        S.op("pool", lambda e: e.memset(S_p[:, :, :], 0.0), writes=[R["S_p"]])
        S.op("pool", lambda e: e.memset(S_bf[:, :, :], 0.0), writes=[R["S_bf"]])
        S.op("pool", lambda e: e.memset(u_p[:, :, :], 0.0), writes=[R["u_p"]])
        for i in range(2):
            S.op("pool", lambda e, i=i: e.memset(qinz[i][:, :, :], 0.0), writes=[R[f"qinz{i}"]])

        S.stage(2)
        cv = tmpm[0:17, :]
        ex = Grow[0:17, 3, :]
        Rtmpe = [RG[3]]
        cvb = xs_bf[0:17, :]
        S.op("act", lambda e: e.activation(out=ex, in_=cv, func=AF.Exp, scale=-1.0),
             reads=Rtm, writes=Rtmpe)
        S.op("act", lambda e: e.activation(out=ex, in_=ex, func=AF.Ln, bias=1.0), reads=Rtmpe, writes=Rtmpe)
        S.op("act", lambda e: e.activation(out=ex, in_=ex, func=AF.Exp, scale=-1.0), reads=Rtmpe, writes=Rtmpe)
        S.op("dve", lambda e: e.tensor_tensor(out=cvb, in0=cv, in1=ex, op=ALU.mult),
             reads=Rtm + Rtmpe, writes=[R["xs_bf"]])
        for kc in range(8):
            S.op("pe", lambda e, kc=kc: e.transpose(out=pb0b[:, kc * 32:kc * 32 + 17], in_=cvb[:, kc * 128:(kc + 1) * 128],
                                                    identity=ident_b[0:17, 0:17]),
                 reads=[R["xs_bf"], R["cst_b"]], writes=[Rpb[0]], signal=(kc == 7))
        S.op("act", lambda e: e.activation(out=scT[:, :, :], in_=pb0b[:, 0:256].rearrange("p (k s) -> p k s", k=8)[:, :, 0:17],
                                           func=AF.Copy),
             reads=[Rpb[0]], writes=[R["scT"]])

        vec_slot = {0: 1, 1: 0, 3: 3, 4: 2}
        wada_v = w_ada.rearrange("(k p) n -> p k n", p=128)
        modc = er[0:17, 0:512]
        S.stage(3)
        wada_rb = [wbig[:, r * 4096:(r + 1) * 4096].rearrange("p (k n) -> p k n", k=8) for r in range(4)]
        Rwada_rb = [res(f"wadar{r}") for r in range(4)]
        RmodT = [res(f"modT{i}") for i in range(4)]

        def mod_dma(j, ring3, ringR, direct):
            if direct:
                S.dma("pool", lambda e, j=j: e.dma_start(out=ring3, in_=wada_v[:, :, j * 512:(j + 1) * 512]), writes=ringR)
            else:
                stg_ap, stg_R = w_stage[j % 2]
                stg3 = stg_ap.rearrange("p (k n) -> p k n", k=8)
                S.dma("sp", lambda e, j=j, stg3=stg3: e.dma_start(out=stg3, in_=wada_v[:, :, j * 512:(j + 1) * 512]), writes=stg_R)
                cast(ring3, stg3, stg_R, ringR)

        modc2_t = sb("modc2", [17, 512])

        def mod_compute(j, ring3, ringR, part=None):
            vec, sub = j // 2, j % 2
            if part is None:
                mc, Rmc = modc, R["er"]
            else:
                mc, Rmc = modc2_t[:, :], R["modc2"]
            if part in (None, "a"):
                mod_compute_a(j, ring3, ringR, mc, Rmc)
            if part in (None, "b"):
                mod_compute_b(vec, sub, mc, Rmc)

        def mod_compute_a(j, ring3, ringR, mc, Rmc):
            S.dma("sp", lambda e, j=j: e.dma_start(
                out=bch0[:, :], in_=bass.AP(b_ada.tensor, b_ada.offset + j * 512, [[0, 17], [1, 512]])),
                writes=[R["bch0"]])
            for kc in range(8):
                S.op("pe", lambda e, kc=kc: e.matmul(pbs[7][0:17, 0:512], lhsT=scT[:, kc, :], rhs=ring3[:, kc, :],
                                                    start=(kc == 0), stop=(kc == 7)),
                     reads=[R["scT"]] + ringR, writes=[Rpb[7]], signal=(kc == 7))
            S.op("dve", lambda e: e.tensor_tensor(out=mc, in0=pbs[7][0:17, 0:512], in1=bch0[:, :], op=ALU.add),
                 reads=[Rpb[7], R["bch0"]], writes=[Rmc])

        def mod_compute_b(vec, sub, mc, Rmc):
            if vec in vec_slot:
                slot = vec_slot[vec]
                for h in range(4):
                    S.op("pe", lambda e, h=h: e.transpose(out=pbs[6][:, h * 32:h * 32 + 17], in_=mc[:, h * 128:(h + 1) * 128],
                                                          identity=ident_f[0:17, 0:17]),
                         reads=[Rmc, R["cst_f"]], writes=[Rpb[6]], signal=(h == 3))
                S.op("act", lambda e, slot=slot, sub=sub: e.activation(
                    out=modT[:, slot, 4 * sub:4 * sub + 4, :],
                    in_=pbs[6][:, 0:128].rearrange("p (k s) -> p k s", k=4)[:, :, 0:17], func=AF.Copy),
                    reads=[Rpb[6]], writes=[RmodT[slot]])
            else:
                gi0 = 0 if vec == 2 else 2
                for g in range(2):
                    esel = cst_f[0:17, C_EP:C_EP + 128] if g == 0 else cst_f[0:17, C_ES:C_ES + 128]
                    S.op("pe", lambda e, g=g, esel=esel: e.matmul(pbs[5 - g][:, :], lhsT=esel, rhs=mc, start=True, stop=True),
                         reads=[Rmc, R["cst_f"]], writes=[Rpb[5 - g]], signal=True)
                for g in range(2):
                    S.op("act", lambda e, g=g, gi0=gi0, sub=sub: e.activation(
                        out=Grow[:, gi0 + g, sub * 512:(sub + 1) * 512], in_=pbs[5 - g][:, :], func=AF.Copy),
                        reads=[Rpb[5 - g]], writes=[RG[gi0 + g]])

        def mod_post(which, slot):
            S.op("dve", lambda e: e.tensor_scalar(out=modT[:, slot, :, :], in0=modT[:, slot, :, :], scalar1=1.0,
                                                  scalar2=None, op0=ALU.add),
                 reads=[RmodT[slot]], writes=[RmodT[slot]])
            S.op("dve", lambda e: e.tensor_tensor(
                out=modT[:, slot, :, :], in0=modT[:, slot, :, :], in1=V(ngT[:, which, 0:1], [[1, 8], [0, 17]]), op=ALU.mult),
                reads=[RmodT[slot], R["ngT"]], writes=[RmodT[slot]])

        for j in range(4):
            mod_dma(j, wada_rb[j % 4][:, :, :], [Rwada_rb[j % 4]], False)
            mod_compute(j, wada_rb[j % 4][:, :, :], [Rwada_rb[j % 4]])
        mod_post(0, 0)
        late = [4, 5, 6, 7, 8, 9, 10, 11]
        late_rb = [(h2T_all[:, :, 128:640], [Rh2[i] for i in range(1, 5)]), (h2T_all[:, :, 640:1152], [Rh2[i] for i in range(5, 9)])]

        def late_start():
            for i in range(2):
                mod_dma(late[i], late_rb[i][0], late_rb[i][1], True)

        def late_step(i2):
            i, half = i2 // 2, i2 % 2
            ring3, ringR = late_rb[i % 2]
            if half == 0:
                mod_compute(late[i], ring3, ringR, part="a")
                if i + 2 < len(late):
                    mod_dma(late[i + 2], ring3, ringR, True)
            else:
                mod_compute(late[i], ring3, ringR, part="b")
                if late[i] == 9:
                    mod_post(1, 2)
        late_hooks = {}
        for i2 in range(16):
            late_hooks[(i2 // 4, 1 + i2 % 4)] = [i2]

        def mod_bc(slot, kind):
            if kind == "p":
                return V(modT[:, slot, 0, 0:1], [[17, 8], [0, 128]])
            return V(modT[:, slot, 0, 1:2], [[17, 8], [1, 16], [0, 8]])

        def tokview(ap2d, kind):
            if len(ap2d.shape) == 3:
                return ap2d if kind == "p" else ap2d.rearrange("p k (s t) -> p k s t", s=16)
            if kind == "p":
                return ap2d.rearrange("p (k t) -> p k t", k=8)
            return ap2d.rearrange("p (k s t) -> p k s t", k=8, s=16)

        def rstd_from(src_ap, Rsrc, n_inv, si=0):
            ss, Rss = ss_bufs[si], R[f"ss{si}"]
            S.op("act", lambda e: e.activation(out=xs_bf[:, :], in_=src_ap, func=AF.Square, accum_out=ss[:, 0:1]),
                 reads=[Rsrc], writes=[R["xs_bf"], Rss])
            S.op("act", lambda e: e.activation(out=ss[:, 1:2], in_=ss[:, 0:1], func=AF.Ln, scale=n_inv, bias=eps_ap),
                 reads=[Rss, R["epsb"]], writes=[Rss])
            S.op("act", lambda e: e.activation(out=ss[:, 2:3], in_=ss[:, 1:2], func=AF.Exp, scale=-0.5),
                 reads=[Rss], writes=[Rss])
            return ss[:, 2:3], Rss

        def norm_a(src_ap, Rsrc, si=0):
            rs_ap, Rss = rstd_from(src_ap, Rsrc, 1.0 / D, si)
            S.op("dve", lambda e: e.tensor_scalar(out=xs_bf[:, :], in0=src_ap, scalar1=rs_ap, scalar2=None, op0=ALU.mult),
                 reads=[Rsrc, Rss], writes=[R["xs_bf"]])

        def norm_b(gslot, kind, dst_ap, Rdst):
            for kc in range(8):
                S.op("pe", lambda e, kc=kc: e.transpose(out=pb0b[:, kc * 128:(kc + 1) * 128], in_=xs_bf[:, kc * 128:(kc + 1) * 128],
                                                        identity=ident_b),
                     reads=[R["xs_bf"], R["cst_b"]], writes=[Rpb[0]], signal=(kc == 7))
            S.op("dve", lambda e: e.tensor_tensor(out=tokview(tmpm[:, :], kind), in0=tokview(pb0b[:, :], kind),
                                                  in1=mod_bc(gslot, kind), op=ALU.mult),
                 reads=[Rpb[0], RmodT[gslot]], writes=Rtm)
            S.op("pool", lambda e: e.tensor_tensor(out=tokview(dst_ap, kind), in0=tokview(tmpm[:, :], kind),
                                                   in1=mod_bc(gslot + 1, kind), op=ALU.add),
                 reads=Rtm + [RmodT[gslot + 1]], writes=[Rdst])

        eps_t = sb("epsb", [128, 1])
        eps_ap = eps_t[:, 0:1]
        S.op("pool", lambda e: e.memset(eps_t[:, :], EPS), writes=[R["epsb"]])

        CQ, CK, CV_, CGZ, CR, CB, CC, CH = 0, 256, 512, 1024, 1040, 1552, 2064, 2576

        def mm_group(bank, col, lhsT_fn, rhs_fn, reads, nk=8, last_signal=True):
            for kc in range(nk):
                l_ap, r_ap = lhsT_fn(kc), rhs_fn(kc)
                S.op("pe", lambda e, kc=kc, l_ap=l_ap, r_ap=r_ap, col=col, nk=nk: e.matmul(
                    col, lhsT=l_ap, rhs=r_ap, start=(kc == 0), stop=(kc == nk - 1)),
                     reads=(reads(kc) if callable(reads) else reads), writes=[Rpb[bank]],
                     signal=(last_signal and kc == nk - 1))

        def phase1_steps(slot, kind, ptile, last_prompt, par):
            gi = 0 if kind == "p" else 1
            x1 = x1_all[:, slot, :]
            Rx = Rx1[slot]
            qk_c, gz_c, v_c, sr_c, mix_c = qk_sb[par], gz_sb[par], v_bf[par], sr[par], mixT[par]
            Rqk, Rgz, Rv, Rsr, Rmix = R[f"qk_sb{par}"], R[f"gz_sb{par}"], R[f"v_bf{par}"], R[f"sr{par}"], R[f"mixT{par}"]
            hT = hTs[par]
            def rdb(blk):
                return [RhT[par], Rwin_c[blk]]
            mcol = C_MP if kind == "p" else C_MS
            scan0 = ones_f if kind == "p" else cst_f[:, C_SCAN:C_SCAN + 128]

            def N1a():
                norm_a(x1, Rx)

            def N1b():
                norm_b(0, kind, hT, RhT[par])

            def A1():
                mm_group(7, pbs[7][0:16, 0:128], lambda kc: w_in_v[:, kc, CGZ:CGZ + 16], lambda kc: hT[:, kc, :], rdb("gz"))
                for j in range(4):
                    c0 = (CQ if j < 2 else CK) + (j % 2) * 128
                    mm_group(1, pbs[1][:, j * 128:(j + 1) * 128], lambda kc, c0=c0: w_in_v[:, kc, c0:c0 + 128],
                             lambda kc: hT[:, kc, :], rdb("qk"), last_signal=(j == 3))
                S.op("act", lambda e: e.activation(out=gz_c[:, :], in_=pbs[7][0:16, 0:128], func=AF.Copy),
                     reads=[Rpb[7]], writes=[Rgz])
                S.op("act", lambda e: e.activation(out=qk_c[:, :, :].rearrange("p a b -> p (a b)"), in_=pbs[1][:, :], func=AF.Copy),
                     reads=[Rpb[1]], writes=[Rqk])

            def A2():
                mm_group(2, pbs[2][:, :], lambda kc: hT[:, kc, :], lambda kc: w_in_v[:, kc, CV_:CV_ + 512], rdb("v"))
                mm_group(3, pbs[3][:, :], lambda kc: hT[:, kc, :], lambda kc: w_in_v[:, kc, CR:CR + 512], rdb("r"))
                S.op("act", lambda e: e.activation(out=v_c[:, :], in_=pbs[2][:, :], func=AF.Copy), reads=[Rpb[2]], writes=[Rv])
                S.op("act", lambda e: e.activation(out=er[:, :], in_=pbs[3][:, :], func=AF.Exp, scale=-1.0),
                     reads=[Rpb[3]], writes=[R["er"]])
                S.op("act", lambda e: e.activation(out=er[:, :], in_=er[:, :], func=AF.Ln, bias=1.0), reads=[R["er"]], writes=[R["er"]])
                S.op("act", lambda e: e.activation(out=er[:, :], in_=er[:, :], func=AF.Exp, scale=-1.0), reads=[R["er"]], writes=[R["er"]])
                S.op("dve", lambda e: e.tensor_tensor(out=sr_c[:, :], in0=pbs[3][:, :], in1=er[:, :], op=ALU.mult),
                     reads=[Rpb[3], R["er"]], writes=[Rsr])
                S.op("pool", lambda e: e.tensor_tensor(out=sr_c[:, :], in0=sr_c[:, :], in1=gg_row[:, :], op=ALU.mult),
                     reads=[Rsr, R["gg_row"]], writes=[Rsr])

            if kind == "p":
                Ru = R["u_p"]

                def uview(cc, j):
                    return u_p[:, cc, j:j + 128]

                def zview(cc):
                    return zc[:, cc, :]
            else:
                Ru = R["u_s"]

                def uview(cc, j):
                    return u_s[:, cc, :, j:j + 8]

                def zview(cc):
                    return zc[:, cc, :].rearrange("p (s t) -> p s t", s=16)

            def A3p():
                for bank, cbase, blk in ((6, CH, "hin"), (5, CC, "C")):
                    for j in range(4):
                        mm_group(bank, pbs[bank][:, j * 128:(j + 1) * 128],
                                 lambda kc, c0=cbase + j * 128: w_in_v[:, kc, c0:c0 + 128], lambda kc: hT[:, kc, :], rdb(blk),
                                 last_signal=(j == 3))

            def A3e():
                S.op("act", lambda e: e.activation(out=hin_sb[:, :, :].rearrange("p a b -> p (a b)"), in_=pbs[6][:, :], func=AF.Copy),
                     reads=[Rpb[6]], writes=[R["hin_sb"]])
                if kind == "p":
                    S.op("dve", lambda e: e.tensor_tensor(out=u_p[:, :, 2:130], in0=pbs[5][:, :].rearrange("p (c t) -> p c t", c=4),
                                                          in1=hin_sb[:, :, :], op=ALU.mult),
                         reads=[Rpb[5], R["hin_sb"]], writes=[Ru])
                else:
                    for cc in range(4):
                        S.op("dve", lambda e, cc=cc: e.tensor_tensor(
                            out=u_s[:, cc, :, 2:10], in0=pbs[5][:, cc * 128:(cc + 1) * 128].rearrange("p (s t) -> p s t", s=16),
                            in1=hin_sb[:, cc, :].rearrange("p (s t) -> p s t", s=16), op=ALU.mult),
                            reads=[Rpb[5], R["hin_sb"]], writes=[Ru])
                if kind == "p":
                    def uall(j):
                        return u_p[:, :, j:j + 128]

                    def zall(t):
                        return t[:, :, :]

                    def wbc(j):
                        return V(wc[:, j, 0:1], [[1, 4], [0, 128]])
                    S.op("pool", lambda e: e.tensor_tensor(out=zall(za), in0=uall(0), in1=wbc(0), op=ALU.mult),
                         reads=[Ru, R["wc"]], writes=[R["za"]])
                    S.op("pool", lambda e: e.tensor_tensor(out=zall(zb), in0=uall(1), in1=wbc(1), op=ALU.mult),
                         reads=[Ru, R["wc"]], writes=[R["zb"]])
                    S.op("dve", lambda e: e.tensor_tensor(out=zall(zc), in0=uall(2), in1=wbc(2), op=ALU.mult),
                         reads=[Ru, R["wc"]], writes=[R["zc"]])
                else:
                    for cc in range(4):
                        for j, (zt, eng) in enumerate(((za, "pool"), (zb, "pool"), (zc, "dve"))):
                            S.op(eng, lambda e, cc=cc, j=j, zt=zt: e.tensor_tensor(
                                out=zt[:, cc, :].rearrange("p (s t) -> p s t", s=16), in0=uview(cc, j),
                                in1=V(wc[:, j, cc:cc + 1], [[0, 16], [0, 8]]), op=ALU.mult),
                                reads=[Ru, R["wc"]], writes=[R[("za", "zb", "zc")[j]]])
                S.op("dve", lambda e: e.tensor_tensor(out=zc[:, :, :], in0=zc[:, :, :], in1=za[:, :, :], op=ALU.add),
                     reads=[R["zc"], R["za"]], writes=[R["zc"]])
                S.op("dve", lambda e: e.tensor_tensor(out=zc[:, :, :], in0=zc[:, :, :], in1=zb[:, :, :], op=ALU.add),
                     reads=[R["zc"], R["zb"]], writes=[R["zc"]])

            def A4():
                for j in range(4):
                    mm_group(4, pbs[4][:, j * 128:(j + 1) * 128],
                             lambda kc, c0=CB + j * 128: w_in_v[:, kc, c0:c0 + 128], lambda kc: hT[:, kc, :], rdb("B"),
                             last_signal=(j == 3))
                S.op("dve", lambda e: e.tensor_tensor(out=mix_c[:, 4:8, :], in0=pbs[4][:, :].rearrange("p (c t) -> p c t", c=4),
                                                      in1=zc[:, :, :], op=ALU.mult),
                     reads=[Rpb[4], R["zc"]], writes=[Rmix])
                if kind == "p":
                    if last_prompt:
                        for cc in range(4):
                            S.op("pe", lambda e, cc=cc: e.transpose(out=pbs[6][0:2, cc * 128:(cc + 1) * 128], in_=u_p[:, cc, 128:130],
                                                                    identity=ident_f),
                                 reads=[Ru, R["cst_f"]], writes=[Rpb[6]], signal=(cc == 3))
                        S.op("act", lambda e: e.activation(out=cn_sb[0:2, :], in_=pbs[6][0:2, :], func=AF.Copy),
                             reads=[Rpb[6]], writes=[R["sc_tok"]])
                        S.dma("sp", lambda e: e.dma_start(out=convp, in_=cn_sb[0:2, :]), reads=[R["sc_tok"]])
                    else:
                        S.op("pool", lambda e: e.tensor_copy(out=u_p[:, :, 0:2], in_=u_p[:, :, 128:130]), reads=[Ru], writes=[Ru])
                else:
                    S.op("pool", lambda e: e.tensor_copy(out=cn_T[:, :, :].rearrange("p c (s j) -> p c s j", s=16),
                                                         in_=u_s[:, :, :, 8:10]), reads=[Ru], writes=[R["cn_T"]])
                    for cc in range(4):
                        S.op("pe", lambda e, cc=cc: e.transpose(out=pbs[6][0:32, cc * 128:(cc + 1) * 128], in_=cn_T[:, cc, :],
                                                                identity=ident_f),
                             reads=[R["cn_T"], R["cst_f"]], writes=[Rpb[6]], signal=(cc == 3))
                    S.op("act", lambda e: e.activation(out=cn_sb[:, :], in_=pbs[6][0:32, :], func=AF.Copy),
                         reads=[Rpb[6]], writes=[R["sc_tok"]])
                    S.dma("sp", lambda e: e.dma_start(out=convs, in_=cn_sb[:, :]), reads=[R["sc_tok"]])

            def B0():
                for c in range(2):
                    S.op("pe", lambda e, c=c: e.matmul(pbs[7][:, 128 + c * 128:256 + c * 128], lhsT=wg_sb[0:16, c * 128:(c + 1) * 128],
                                                       rhs=gz_c[0:16, :], start=True, stop=True),
                         reads=[R["wg_sb"], Rgz], writes=[Rpb[7]], signal=(c == 1))
                for c in range(2):
                    S.op("act", lambda e, c=c: e.activation(out=el[:, c, :], in_=pbs[7][:, 128 + c * 128:256 + c * 128], func=AF.Exp,
                                                            scale=-1.0, bias=nbg[:, c:c + 1]),
                         reads=[Rpb[7], R["nbg"]], writes=[R["el"]])
                el2 = el[:, :, :].rearrange("p a b -> p (a b)")
                S.op("act", lambda e: e.activation(out=el2, in_=el2, func=AF.Ln, bias=1.0), reads=[R["el"]], writes=[R["el"]])
                for c in range(2):
                    S.op("dve", lambda e, c=c: e.tensor_tensor_scan(out=cum[:, c, :], data0=scan0, data1=el[:, c, :], initial=0.0,
                                                                    op0=ALU.mult, op1=ALU.add),
                         reads=[R["el"], R["cst_f"]], writes=[R["cum"]])
                cum2 = cum[:, :, :].rearrange("p a b -> p (a b)")
                S.op("act", lambda e: e.activation(out=eq[:, :, :].rearrange("p a b -> p (a b)"), in_=cum2, func=AF.Exp, scale=-1.0 / 16),
                     reads=[R["cum"]], writes=[R["eq"]])
                S.op("act", lambda e: e.activation(out=ek[:, :, :].rearrange("p a b -> p (a b)"), in_=cum2, func=AF.Exp, scale=1.0 / 16),
                     reads=[R["cum"]], writes=[R["ek"]])
                for hh in range(2):
                    ps_ = slice(hh * 64, (hh + 1) * 64)
                    S.op("dve", lambda e, hh=hh, ps_=ps_: e.scalar_tensor_tensor(
                        out=qinz[hh][ps_, :, :].rearrange("p a b -> p (a b)"),
                        in0=qk_c[ps_, 0:2, :].rearrange("p a b -> p (a b)"), scalar=0.125,
                        in1=eq[ps_, :, :].rearrange("p a b -> p (a b)"), op0=ALU.mult, op1=ALU.mult),
                        reads=[Rqk, R["eq"]], writes=[R[f"qinz{hh}"]])
                S.op("dve", lambda e: e.tensor_tensor(out=kin[:, :, :].rearrange("p a b -> p (a b)"),
                                                      in0=qk_c[:, 2:4, :].rearrange("p a b -> p (a b)"),
                                                      in1=ek[:, :, :].rearrange("p a b -> p (a b)"), op=ALU.mult),
                     reads=[Rqk, R["ek"]], writes=[R["kin"]])

            def B1():
                for c in range(2):
                    S.op("pe", lambda e, c=c: e.transpose(out=pb5b[:, c * 128:(c + 1) * 128], in_=kin[:, c, :], identity=ident_b),
                         reads=[R["kin"], R["cst_b"]], writes=[Rpb[5]], signal=(c == 1))
                S.op("act", lambda e: e.activation(out=kin_tok[:, :], in_=pb5b[:, 0:256], func=AF.Copy),
                     reads=[Rpb[5]], writes=[R["kin_tok"]])
                for h in range(4):
                    c, hh = h // 2, h % 2
                    S.op("pe", lambda e, h=h, c=c, hh=hh: e.matmul(pbs[1][:, h * 128:(h + 1) * 128],
                                                                   lhsT=kin[:, c, :], rhs=qinz[hh][:, c, :], start=True, stop=True),
                         reads=[R["kin"], R[f"qinz{hh}"]], writes=[Rpb[1]], signal=(h == 3))
                S.op("dve", lambda e: e.tensor_tensor(out=attm[:, :, :], in0=pbs[1][:, :].rearrange("p (h t) -> p h t", h=4),
                                                      in1=V(cst_f[:, mcol:mcol + 1], [[0, 4], [1, 128]]), op=ALU.mult),
                     reads=[Rpb[1], R["cst_f"]], writes=[R["attm"]])
                if kind == "s":
                    for g in range(4):
                        stg, Rstg = stages[g % 2], Rstages[g % 2]
                        S.dma("sp", lambda e, g=g, stg=stg: e.dma_start(
                            out=stg, in_=sgla[g * 4:(g + 1) * 4].rearrange("s (c hh) d v -> (hh d) s c v", hh=2)),
                            writes=[Rstg])
                        S.op("act", lambda e, g=g, stg=stg: e.activation(out=Ssbf[g // 2][:, (g % 2) * 4:(g % 2) * 4 + 4, :, :], in_=stg,
                                                                         func=AF.Copy),
                             reads=[Rstg], writes=[RSsbf[g // 2]])
                    for i in range(2):
                        S.op("pool", lambda e, i=i: e.memset(qmz[i], 0.0), writes=[Rqmz[i]])

            def B2():
                for c in range(2):
                    for hh in range(2):
                        h = 2 * c + hh
                        ps_ = slice(hh * 64, (hh + 1) * 64)
                        if kind == "s":
                            S.op("dve", lambda e, c=c, hh=hh, ps_=ps_: e.tensor_tensor(
                                out=qmz[hh][ps_, :, :], in0=V(qinz[hh][ps_, c, 0:1], [[0, 16], [1, 128]]),
                                in1=cst_b[ps_, CB_SM:CB_SM + 2048].rearrange("p (s t) -> p s t", s=16), op=ALU.mult),
                                reads=[R[f"qinz{hh}"], R["cst_b"]], writes=[Rqmz[hh]])
                        ocol = pbs[2][:, h * 128:(h + 1) * 128]
                        S.op("pe", lambda e, h=h, ocol=ocol: e.matmul(ocol, lhsT=attm[:, h, :], rhs=v_c[:, h * 128:(h + 1) * 128],
                                                                      start=True, stop=False),
                             reads=[R["attm"], Rv], writes=[Rpb[2]], signal=False)
                        if kind == "p":
                            S.op("pe", lambda e, c=c, hh=hh, ocol=ocol: e.matmul(ocol, lhsT=qinz[hh][:, c, :], rhs=S_bf[:, c, :],
                                                                                start=False, stop=True),
                                 reads=[R[f"qinz{hh}"], R["S_bf"]], writes=[Rpb[2]], signal=True)
                        else:
                            for q in range(16):
                                S.op("pe", lambda e, c=c, hh=hh, q=q, ocol=ocol: e.matmul(
                                    ocol, lhsT=qmz[hh][:, q, :], rhs=Ssbf[q // 8][:, q % 8, c, :], start=False, stop=(q == 15)),
                                    reads=[Rqmz[hh], RSsbf[q // 8]], writes=[Rpb[2]], signal=(q == 15))
                for h in range(4):
                    S.op("act", lambda e, h=h: e.activation(out=og[:, h * 128:(h + 1) * 128], in_=pbs[2][:, h * 128:(h + 1) * 128],
                                                            func=AF.Square, accum_out=ss4[:, h:h + 1]),
                         reads=[Rpb[2]], writes=[R["og"], R["ss4"]])
                S.op("act", lambda e: e.activation(out=ss4[:, 4:8], in_=ss4[:, 0:4], func=AF.Ln, scale=1.0 / 128, bias=eps_ap),
                     reads=[R["ss4"], R["epsb"]], writes=[R["ss4"]])
                S.op("act", lambda e: e.activation(out=ss4[:, 8:12], in_=ss4[:, 4:8], func=AF.Exp, scale=-0.5),
                     reads=[R["ss4"]], writes=[R["ss4"]])
                S.op("dve", lambda e: e.tensor_tensor(out=zc[:, :, :], in0=pbs[2][:, :].rearrange("p (h v) -> p h v", h=4),
                                                      in1=V(ss4[:, 8:9], [[1, 4], [0, 128]]), op=ALU.mult),
                     reads=[Rpb[2], R["ss4"]], writes=[R["zc"]])
                S.op("dve", lambda e: e.tensor_tensor(out=og[:, :], in0=zc[:, :, :].rearrange("p a b -> p (a b)"), in1=sr_c[:, :],
                                                      op=ALU.mult),
                     reads=[R["zc"], Rsr], writes=[R["og"]])
            def B2t():
                for h in range(4):
                    S.op("pe", lambda e, h=h: e.transpose(out=pb6b[:, h * 128:(h + 1) * 128], in_=og[:, h * 128:(h + 1) * 128],
                                                          identity=ident_b),
                         reads=[R["og"], R["cst_b"]], writes=[Rpb[6]], signal=(h == 3))
                S.op("act", lambda e: e.activation(out=mix_c[:, 0:4, :].rearrange("p a b -> p (a b)"), in_=pb6b[:, 0:512], func=AF.Copy),
                     reads=[Rpb[6]], writes=[Rmix])

            def B3():
                if kind == "p":
                    for c in range(2):
                        S.op("pe", lambda e, c=c: e.matmul(pbs[3][:, c * 256:(c + 1) * 256], lhsT=kin_tok[:, c * 128:(c + 1) * 128],
                                                           rhs=v_c[:, c * 256:(c + 1) * 256], start=True, stop=True),
                             reads=[R["kin_tok"], Rv], writes=[Rpb[3]], signal=(c == 1))
                    for hh in range(2):
                        ps = slice(hh * 64, (hh + 1) * 64)
                        S.op("dve", lambda e, hh=hh, ps=ps: e.tensor_tensor(
                            out=S_p[ps, :, :], in0=V(pbs[3][ps, hh * 128:hh * 128 + 1], [[256, 2], [1, 128]]),
                            in1=S_p[ps, :, :], op=ALU.add),
                            reads=[Rpb[3], R["S_p"]], writes=[R["S_p"]])
                    S.op("dve", lambda e: e.tensor_tensor(out=S_p[:, :, :], in0=S_p[:, :, :],
                                                          in1=V(eq[:, 0, 127:128], [[128, 2], [0, 128]]), op=ALU.mult),
                         reads=[R["S_p"], R["eq"]], writes=[R["S_p"]])
                    S.op("pool", lambda e: e.tensor_copy(out=S_bf[:, :, :], in_=S_p[:, :, :]), reads=[R["S_p"]], writes=[R["S_bf"]])
                    if last_prompt:
                        for hh in range(2):
                            S.dma("sp", lambda e, hh=hh: e.dma_start(
                                out=glap.rearrange("(c hh) d v -> hh d c v", hh=2)[hh], in_=S_p[hh * 64:(hh + 1) * 64, :, :]),
                                reads=[R["S_p"]])
                else:
                    def stage_in(g):
                        stg, Rstg = stages[g % 2], Rstages[g % 2]
                        S.dma("sp", lambda e, g=g, stg=stg: e.dma_start(
                            out=stg, in_=sgla[g * 4:(g + 1) * 4].rearrange("s (c hh) d v -> (hh d) s c v", hh=2)),
                            writes=[Rstg])
                    stage_in(0)
                    for g in range(4):
                        stg, Rstg = stages[g % 2], Rstages[g % 2]
                        if g + 1 < 4:
                            stage_in(g + 1)
                        S.op("dve", lambda e, g=g: e.tensor_tensor(
                            out=km, in0=V(kin_tok[:, 0:1], [[0, 4], [1, 256]]),
                            in1=V(cst_f[:, C_SMT + 4 * g:C_SMT + 4 * g + 1], [[1, 4], [0, 256]]), op=ALU.mult),
                            reads=[R["kin_tok"], R["cst_f"]], writes=Rtm)
                        for j in range(4):
                            bank = 3 + (j % 2) * 3
                            for c in range(2):
                                S.op("pe", lambda e, j=j, c=c, bank=bank: e.matmul(
                                    pbs[bank][:, c * 256:(c + 1) * 256], lhsT=km[:, j, c * 128:(c + 1) * 128],
                                    rhs=v_c[:, c * 256:(c + 1) * 256], start=True, stop=True),
                                    reads=Rtm + [Rv], writes=[Rpb[bank]], signal=(c == 1))
                            for hh in range(2):
                                ps = slice(hh * 64, (hh + 1) * 64)
                                S.op("dve", lambda e, j=j, hh=hh, ps=ps, bank=bank, stg=stg: e.tensor_tensor(
                                    out=stg[ps, j, :, :], in0=V(pbs[bank][ps, hh * 128:hh * 128 + 1], [[256, 2], [1, 128]]),
                                    in1=stg[ps, j, :, :], op=ALU.add),
                                    reads=[Rpb[bank], Rstg], writes=[Rstg])
                        S.op("dve", lambda e, g=g, stg=stg: e.tensor_tensor(
                            out=stg, in0=stg, in1=V(eq[:, 0, 32 * g + 7:32 * g + 8], [[8, 4], [128, 2], [0, 128]]), op=ALU.mult),
                            reads=[Rstg, R["eq"]], writes=[Rstg])
                        S.dma("sp", lambda e, g=g, stg=stg: e.dma_start(
                            out=glas[g * 4:(g + 1) * 4].rearrange("s (c hh) d v -> (hh d) s c v", hh=2), in_=stg),
                            reads=[Rstg])

            def B4():
                obank = (1, 7)
                for half in range(2):
                    mm_group(obank[half], pbs[obank[half]][:, :], lambda kc: mix_c[:, kc, :],
                             lambda kc, half=half: w_out_v[:, kc, half * 512:(half + 1) * 512], [Rmix, Rwout])
                for half in range(2):
                    S.op("dve", lambda e, half=half: e.tensor_tensor(out=tmpm[:, half * 512:(half + 1) * 512],
                                                                     in0=pbs[obank[half]][:, :],
                                                                     in1=Grow[:, gi, half * 512:(half + 1) * 512], op=ALU.mult),
                         reads=[Rpb[obank[half]], RG[gi]], writes=[Rtm[half]])
                    S.op("pool", lambda e, half=half: e.tensor_tensor(out=x1[:, half * 512:(half + 1) * 512],
                                                                      in0=x1[:, half * 512:(half + 1) * 512],
                                                                      in1=tmpm[:, half * 512:(half + 1) * 512], op=ALU.add),
                         reads=[Rx, Rtm[half]], writes=[Rx])

            def N2a():
                norm_a(x1, Rx, 1)

            def N2b():
                norm_b(2, kind, h2T_all[:, :, slot * 128:(slot + 1) * 128], Rh2[slot])

            return dict(N1a=N1a, N1b=N1b, A1=A1, A2=A2, A3p=A3p, A3e=A3e, A4=A4, B0=B0, B1=B1, B2=B2, B2t=B2t, B3=B3, B4=B4, N2a=N2a, N2b=N2b)

        def load_ring(e_idx, rs, first):
            if e_idx == NE - 1:
                w_u, w_d, r_up, r_dn = Rarena_up, Rarena_dn, ring_up_last, ring_dn_last
            else:
                w_u = [Rring_up[rs]] + (Rwin_r if first else [])
                w_d, r_up, r_dn = [Rring_dn[rs]], ring_up[rs], ring_dn[rs]
            S.dma("pool", lambda e: e.dma_start(out=r_up[:, :, :],
                                               in_=w_up.rearrange("(k p) n -> p k n", p=128)[:, :, e_idx * ESZ:(e_idx + 1) * ESZ]),
                  writes=w_u)
            S.dma("pool", lambda e: e.dma_start(out=r_dn[:, :, :],
                                               in_=w_down[e_idx * ESZ:(e_idx + 1) * ESZ, :].rearrange("(k p) n -> p k n", p=128)),
                  writes=w_d)

        ring_ctr = [0]
        ab_ctr = [0]
        pre_loaded = [False]

        def pre_phase2():
            base = ring_ctr[0]
            load_ring(0, base % 3, True)
            load_ring(1, (base + 1) % 3, False)
            pre_loaded[0] = True

        def phase2(slots_info, early_reload=False, after_first_up=None, next_x=None):
            nsl = len(slots_info)
            gsz = 3 if nsl % 4 == 1 else 4
            sts = [list(range(i, min(i + gsz, nsl))) for i in range(0, nsl, gsz)]
            base = ring_ctr[0]
            if not pre_loaded[0]:
                load_ring(0, base % 3, True)
                load_ring(1, (base + 1) % 3, False)
            pre_loaded[0] = False
            units = [(e_idx, st) for e_idx in range(NE) for st in sts]

            def up_part(e_idx, st, ab):
                rs = (base + e_idx) % 3
                T = len(st) * 128
                tok0 = st[0] * 128
                for fc in range(4):
                    r_up, Rr = (ring_up_last, Rarena_up) if e_idx == NE - 1 else (ring_up[rs], [Rring_up[rs]])
                    mm_group(fc, pbs[fc][:, 0:T], lambda kc, fc=fc, r_up=r_up: r_up[:, kc, fc * 128:(fc + 1) * 128],
                             lambda kc: h2T_all[:, kc, tok0:tok0 + T], Rr + [Rh2[s_] for s_ in st])
                for fc in range(4):
                    rbuf, Rrb = (er, R["er"]) if fc % 2 == 0 else (sr[0], R["sr0"])
                    S.op("act", lambda e, fc=fc, rbuf=rbuf, T=T: e.activation(out=rbuf[:, 0:T], in_=pbs[fc][:, 0:T], func=AF.Relu),
                         reads=[Rpb[fc]], writes=[Rrb])
                    S.op("act", lambda e, fc=fc, rbuf=rbuf, ab=ab, T=T: e.activation(out=aT[ab][:, fc, 0:T], in_=rbuf[:, 0:T],
                                                                                    func=AF.Square),
                         reads=[Rrb], writes=RaT[ab])

            def down_part(e_idx, st, ab):
                rs = (base + e_idx) % 3
                finals = []
                for si, sidx in enumerate(st):
                    slot, kind, out_ap = slots_info[sidx]
                    gi = 2 if kind == "p" else 3
                    x1 = x1_all[:, slot, :]
                    for half in range(2):
                        bank = 4 + half + 2 * (si % 2)
                        mm_group(bank, pbs[bank][:, :], lambda kc, si=si, ab=ab: aT[ab][:, kc, si * 128:(si + 1) * 128],
                                 lambda kc, half=half: (ring_dn_last if e_idx == NE - 1 else ring_dn[rs])[:, kc, half * 512:(half + 1) * 512],
                                 RaT[ab] + (Rarena_dn if e_idx == NE - 1 else [Rring_dn[rs]]), nk=4)
                    for half in range(2):
                        bank = 4 + half + 2 * (si % 2)
                        if si % 2 == 0:
                            tbuf, Rtb = tmpm[:, half * 512:(half + 1) * 512], Rtm[half]
                        else:
                            tz = za if half == 0 else zb
                            tbuf, Rtb = tz[:, :, :].rearrange("p a b -> p (a b)"), R["za" if half == 0 else "zb"]
                        S.op("dve", lambda e, half=half, bank=bank, gi=gi, tbuf=tbuf: e.tensor_tensor(
                            out=tbuf, in0=pbs[bank][:, :],
                            in1=Grow[:, gi, half * 512:(half + 1) * 512], op=ALU.mult),
                            reads=[Rpb[bank], RG[gi]], writes=[Rtb])
                        S.op("pool", lambda e, half=half, x1=x1, tbuf=tbuf: e.tensor_tensor(
                            out=x1[:, half * 512:(half + 1) * 512], in0=x1[:, half * 512:(half + 1) * 512],
                            in1=tbuf, op=ALU.add),
                            reads=[Rx1[slot], Rtb], writes=[Rx1[slot]])
                    if e_idx == NE - 1:
                        def fin(slot=slot, x1=x1, out_ap=out_ap, si=si):
                            rs_ap, Rss = rstd_from(x1, Rx1[slot], 1.0 / D, si % 4)
                            S.op("dve", lambda e: e.scalar_tensor_tensor(out=x1, in0=x1, scalar=rs_ap, in1=fg_row[:, :],
                                                                         op0=ALU.mult, op1=ALU.mult),
                                 reads=[Rx1[slot], Rss, R["fg_row"]], writes=[Rx1[slot]])
                            S.dma("sp", lambda e: e.dma_start(out=out_ap, in_=x1), reads=[Rx1[slot]])
                            if next_x is not None and slot in next_x:
                                nsrc = next_x[slot]
                                S.dma("sp", lambda e: e.dma_start(out=x1_all[:, slot, :], in_=nsrc), writes=[Rx1[slot]])
                        finals.append(fin)
                for f_ in finals:
                    f_()

            abs_ = []
            for u, (e_idx, st) in enumerate(units):
                ab = ab_ctr[0] % 2
                ab_ctr[0] += 1
                abs_.append(ab)
                up_part(e_idx, st, ab)
                if u == 0 and after_first_up is not None:
                    after_first_up()
                if u > 0:
                    pe_, pst = units[u - 1]
                    down_part(pe_, pst, abs_[u - 1])
                if st is sts[0] and e_idx + 2 < NE:
                    load_ring(e_idx + 2, (base + e_idx + 2) % 3, False)
                if early_reload and st is sts[0] and e_idx == NE - 1:
                    for i_, blk in enumerate(WORDER):
                        c0 = WBLK[blk]
                        wd = 16 if blk == "gz" else 512
                        S.dma("pool", lambda e, c0=c0, wd=wd: e.dma_start(out=w_in_v[:, :, c0:c0 + wd], in_=w_in_kv[:, :, c0:c0 + wd],
                                                                       allow_slow_non_contiguous=True),
                              writes=[Rwin_c[blk]] + (Rring if i_ == 0 else []))
            le, lst = units[-1]
            down_part(le, lst, abs_[-1])
            ring_ctr[0] += NE

        res("out")
        S.stage(5)
        for cc in range(4):
            S.op("pe", lambda e, cc=cc: e.transpose(out=pbs[6][:, cc * 32:(cc + 1) * 32], in_=sc_tok[:, cc * 128:(cc + 1) * 128],
                                                    identity=ident_f[0:32, 0:32]),
                 reads=[R["sc_tok"], R["cst_f"]], writes=[Rpb[6]], signal=(cc == 3))
        S.op("act", lambda e: e.activation(out=u_s[:, :, :, 0:2], in_=pbs[6][:, 0:128].rearrange("p (c s j) -> p c s j", c=4, s=16),
                                           func=AF.Copy),
             reads=[Rpb[6]], writes=[R["u_s"]])

        w_in_kv = w_in.rearrange("(k p) n -> p k n", p=128)
        WORDER = ("gz", "qk", "v", "r", "hin", "C", "B")

        w_stage_rr = [0]

        def load_w_block(blk):
            c0 = WBLK[blk]
            if blk == "gz":
                S.dma("pool", lambda e, c0=c0: e.dma_start(out=w_in_v[:, :, c0:c0 + 16], in_=w_in_kv[:, :, c0:c0 + 16],
                                                           allow_slow_non_contiguous=True),
                      writes=[Rwin_c[blk]] + Rring + Rwada_rb)
                return
            stg_ap, stg_R = w_stage[w_stage_rr[0] % 2]
            w_stage_rr[0] += 1
            stg3 = stg_ap.rearrange("p (k n) -> p k n", k=8)
            S.dma("sp", lambda e, c0=c0, stg3=stg3: e.dma_start(out=stg3, in_=w_in_kv[:, :, c0:c0 + 512]), writes=stg_R)
            cast(w_in_v[:, :, c0:c0 + 512], stg3, stg_R, [Rwin_c[blk]] + Rring + Rwada_rb)

        def load_w_out():
            wo_v = w_out.rearrange("(k p) n -> p k n", p=128)
            S.dma("pool", lambda e: e.dma_start(out=w_out_v[:, :, :], in_=wo_v), writes=[Rwout])

        passes = [
            [(0, "s", None)] + [(1 + i, "p", i) for i in range(8)],
            [(i, "p", 8 + i) for i in range(8)],
        ]
        for pi, tiles in enumerate(passes):
            S.stage(6 + 100 * pi)
            def xload(slot, kind, pt):
                src = xs if kind == "s" else xp[pt * 128:(pt + 1) * 128, :]
                S.dma("sp", lambda e: e.dma_start(out=x1_all[:, slot, :], in_=src), writes=[Rx1[slot]])
            if pi == 0:
                xload(*tiles[0])
            NPF = 3
            steps = [phase1_steps(slot, kind, pt, (kind == "p" and pt == 15), k % 2) for k, (slot, kind, pt) in enumerate(tiles)]
            nt = len(tiles)
            ND = 3 if pi == 0 else 0

            def call(k, name):
                if 0 <= k < nt:
                    steps[k][name]()
            if pi == 0:
                call(0, "N1a"); call(0, "N1b")
                load_w_block("gz"); load_w_block("qk")
                call(0, "A1")
                load_w_block("v"); load_w_block("r")
                call(0, "A2")
                load_w_block("hin"); load_w_block("C")
                call(0, "A3p"); call(0, "A3e")
                load_w_block("B")
                call(0, "A4")
                load_w_out()
                for k_ in range(1, min(NPF, len(tiles))):
                    xload(*tiles[k_])
            else:
                for nm in ("N1a", "N1b", "A1", "A2", "A3p", "A3e", "A4"):
                    call(0, nm)
            if pi == 0:
                late_start()
            call(1, "N1a")
            call(1, "N1b")
            for k in range(nt):
                if pi == 0 and k == 0:
                    pass
                elif pi == 0 and k == 1:
                    xload(*tiles[3])
                    xload(*tiles[4])
                elif k + NPF < nt:
                    xload(*tiles[k + NPF])
                S.stage(10 + 100 * pi + k)

                def hook(sub):
                    if pi == 0:
                        for i_ in late_hooks.get((k, sub), []):
                            late_step(i_)
                call(k + 1, "A1"); call(k, "B0"); call(k - 1 - ND, "N2b")
                hook(1)
                call(k + 1, "A2"); call(k, "B1"); call(k + 2, "N1a")
                hook(2)
                call(k + 1, "A3p"); call(k + 2, "N1b"); call(k, "B2"); call(k + 1, "A3e")
                hook(3)
                call(k + 1, "A4")
                if k == nt - 2:
                    pre_phase2()
                call(k, "B2t"); call(k, "B3")
                hook(4)
                call(k, "B4")
                call(k - ND, "N2a")
                hook(5)
            def flush_norm2(nt=nt, ND=ND, call=call):
                call(nt - 1 - ND, "N2b")
                for kk in range(nt - ND, nt):
                    call(kk, "N2a")
                    call(kk, "N2b")
            info = []
            for (slot, kind, pt) in tiles:
                out_ap = ys if kind == "s" else yp[pt * 128:(pt + 1) * 128, :]
                info.append((slot, kind, out_ap))
            S.stage(50 + 100 * pi)
            nx = None
            if pi == 0:
                nx = {sl_: xp[pt_ * 128:(pt_ + 1) * 128, :] for (sl_, kd_, pt_) in passes[1][:NPF]}
            phase2(info, early_reload=(pi == 0), after_first_up=flush_norm2, next_x=nx)

        S.wait_all("sp", [(k, v) for k, v in S.count.items() if k.startswith("dma_sp") and v > 0])
        S.emit(block)
    return nc


_CACHE = {}


def kernel(x_prompt, x_sample, state_gla, state_conv, c_prompt, c_sample, w_ada, b_ada, norm1_g, w_in,
           w_gate_up, b_gate, gla_norm_g, w_conv, w_out, norm2_g, w_up, w_down, final_g):
    f = lambda a: np.ascontiguousarray(np.asarray(a, dtype=np.float32))
    x_prompt, x_sample, state_gla, state_conv = f(x_prompt), f(x_sample), f(state_gla), f(state_conv)
    c_prompt, c_sample = f(c_prompt), f(c_sample)
    if "nc" not in _CACHE:
        _CACHE["nc"] = build_program()
        _CACHE["consts"] = make_consts()
    nc = _CACHE["nc"]
    cf, cb = _CACHE["consts"]
    shared = {
        "w_ada": f(w_ada)[0], "b_ada": f(b_ada).reshape(1, -1), "norm1_g": f(norm1_g).reshape(1, -1),
        "w_in": f(w_in)[0], "w_gate_up": f(w_gate_up)[0], "b_gate": f(b_gate).reshape(1, -1),
        "gla_norm_g": f(gla_norm_g).reshape(1, -1), "w_conv": f(w_conv)[0], "w_out": f(w_out)[0],
        "norm2_g": f(norm2_g).reshape(1, -1), "w_up": f(w_up)[0], "w_down": f(w_down)[0],
        "final_g": f(final_g).reshape(1, -1), "consts_f": cf, "consts_b": cb,
    }
    in_maps = []
    for c in range(NCORES):
        m = dict(shared)
        m["xp"] = x_prompt[c]
        m["xs"] = np.ascontiguousarray(x_sample[16 * c:16 * c + 16].reshape(128, D))
        m["sgla"] = np.ascontiguousarray(state_gla[0, 16 * c:16 * c + 16])
        m["sconv"] = np.ascontiguousarray(state_conv[0, 16 * c:16 * c + 16].reshape(32, 512))
        m["cvec"] = np.ascontiguousarray(np.concatenate([c_prompt[c:c + 1], c_sample[16 * c:16 * c + 16]], axis=0))
        in_maps.append(m)
    res = run_bass_kernel_spmd(nc, in_maps, core_ids=list(range(NCORES)))
    rs = res.results
    y_prompt = np.stack([np.asarray(r["yp"]) for r in rs], axis=0).astype(np.float32)
    y_sample = np.concatenate([np.asarray(r["ys"]).reshape(16, 8, D) for r in rs], axis=0).astype(np.float32)
    gla_p = np.stack([np.asarray(r["glap"]) for r in rs], axis=0)[None].astype(np.float32)
    conv_p = np.stack([np.asarray(r["convp"]) for r in rs], axis=0)[None].astype(np.float32)
    gla_s = np.concatenate([np.asarray(r["glas"]) for r in rs], axis=0)[None].astype(np.float32)
    conv_s = np.concatenate([np.asarray(r["convs"]).reshape(16, 2, 512) for r in rs], axis=0)[None].astype(np.float32)
    return (y_prompt, y_sample, gla_p, conv_p, gla_s, conv_s)
```
